# Optimizing a Trainium2 kernel written in Bass

```python
import math
import jax, jax.numpy as jnp
from jax import lax
import numpy as np

D_MODEL = 1024
BATCH = 32
SEQ = 256
DEPTH = 2
DEC_BATCH = 4
DEC_SEQ = 1024
PAST_LEN = 256

GRID_W = 64
N_EVEN = (DEPTH + 1) // 2
N_ODD = DEPTH // 2
W_A = D_MODEL
SSM_GROUP = 16
G_A = W_A // SSM_GROUP
P_A = 64
DT_MIN = 1e-3
DT_MAX = 1e-1
DH = 64
H_B = D_MODEL // (2 * DH)
W_B = H_B * 2 * DH
QB = 128
ROPE_BASE = 10000.0
ROT_HALF = DH // 2
ROT_FREQS = DH // 4
E_IN = 2 * W_A + 4 * W_B
W_C = 2 * D_MODEL
NG_C = 8
GC_C = W_C // NG_C
ALPHA = (2 * DEPTH) ** 0.25
BETA = (8 * DEPTH) ** -0.25
LN_EPS = 1e-5

kernel_name = 'hybrid_s5_diffattn_fnet_prefix_diffusion_step'

F32 = jnp.float32


def _ln(x):
    xf = x.astype(F32)
    mu = jnp.mean(xf, axis=-1, keepdims=True)
    var = jnp.mean(jnp.square(xf - mu), axis=-1, keepdims=True)
    return (xf - mu) * lax.rsqrt(var + LN_EPS)


def _modulate_input(x, cond, w_mod_l, b_mod_l):
    m = jax.nn.silu(cond.astype(F32)) @ w_mod_l.astype(F32) + b_mod_l.astype(F32)
    shift, scale, gate = jnp.split(m[:, None, :], 3, axis=-1)
    h = (_ln(x) * (1.0 + scale) + shift).astype(x.dtype)
    return h, gate


def _post_norm(x, gate, out, ln_g_l, ln_b_l):
    z = ALPHA * x.astype(F32) + gate * out.astype(F32)
    return (_ln(z) * ln_g_l.astype(F32) + ln_b_l.astype(F32)).astype(x.dtype)


def _axial_rope(L):
    rows = L // GRID_W
    row = jnp.repeat(jnp.arange(rows), GRID_W).astype(F32)
    col = jnp.tile(jnp.arange(GRID_W), rows).astype(F32)
    freqs = ROPE_BASE ** (-jnp.arange(ROT_FREQS, dtype=F32) / ROT_FREQS)
    ang = jnp.concatenate([row[:, None] * freqs, col[:, None] * freqs], axis=-1)
    return jnp.cos(ang), jnp.sin(ang)


def _apply_rope(x, cos, sin):
    xf = x.astype(F32)
    x1, x2 = xf[..., :ROT_HALF], xf[..., ROT_HALF:]
    c = cos[None, :, None, None, :]
    s = sin[None, :, None, None, :]
    return jnp.concatenate([x1 * c - x2 * s, x1 * s + x2 * c], axis=-1).astype(x.dtype)


def _diff_attn(q, k, v, lam):
    bsz, lq = q.shape[0], q.shape[1]
    qb = min(QB, lq)
    nb = lq // qb
    qs = jnp.moveaxis(q.reshape(bsz, nb, qb, H_B, 2, DH), 1, 0)
    kf = k.astype(F32)
    vf = v.astype(F32)

    def block(qi):
        s = jnp.einsum('bqhmd,bkhmd->bhmqk', qi.astype(F32), kf) * (DH ** -0.5)
        p = jax.nn.softmax(s, axis=-1)
        a = p[:, :, 0] - lam * p[:, :, 1]
        return jnp.einsum('bhqk,bkhe->bqhe', a, vf)

    o = lax.map(block, qs)
    return jnp.moveaxis(o, 0, 1).reshape(bsz, lq, H_B, 2 * DH)


def _ssm_combine(e1, e2):
    a1r, a1i, b1r, b1i = e1
    a2r, a2i, b2r, b2i = e2
    return (a2r * a1r - a2i * a1i,
            a2r * a1i + a2i * a1r,
            a2r * b1r - a2i * b1i + b2r,
            a2r * b1i + a2i * b1r + b2i)


def _s5_bidir(u, lam_re, lam_im, log_dt, b_re, b_im, c_re, c_im, d_skip, h0_re, h0_im):
    bsz, L = u.shape[0], u.shape[1]
    uf = u.astype(F32).reshape(bsz, L, G_A, SSM_GROUP)
    y = uf * d_skip.astype(F32).reshape(G_A, SSM_GROUP)
    fin_re, fin_im = [], []
    for d in range(2):
        lr = lam_re[d].astype(F32)
        li = lam_im[d].astype(F32)
        dt = jnp.exp(log_dt[d].astype(F32))[:, None]
        mag = jnp.exp(lr * dt)
        ar = mag * jnp.cos(li * dt)
        ai = mag * jnp.sin(li * dt)
        den = lr * lr + li * li
        fr = ((ar - 1.0) * lr + ai * li) / den
        fi = (ai * lr - (ar - 1.0) * li) / den
        br = b_re[d].astype(F32)
        bi = b_im[d].astype(F32)
        bbr = fr[..., None] * br - fi[..., None] * bi
        bbi = fr[..., None] * bi + fi[..., None] * br
        xr = jnp.einsum('blgn,gpn->blgp', uf, bbr)
        xi = jnp.einsum('blgn,gpn->blgp', uf, bbi)
        t0 = 0 if d == 0 else L - 1
        h0r = h0_re[:, d].astype(F32)
        h0i = h0_im[:, d].astype(F32)
        xr = xr.at[:, t0].add(ar * h0r - ai * h0i)
        xi = xi.at[:, t0].add(ar * h0i + ai * h0r)
        arb = jnp.broadcast_to(ar, xr.shape)
        aib = jnp.broadcast_to(ai, xr.shape)
        _, _, hr, hi = lax.associative_scan(_ssm_combine, (arb, aib, xr, xi), reverse=(d == 1), axis=1)
        y = y + jnp.einsum('blgp,gnp->blgn', hr, c_re[d].astype(F32)) \
              - jnp.einsum('blgp,gnp->blgn', hi, c_im[d].astype(F32))
        tf = L - 1 if d == 0 else 0
        fin_re.append(hr[:, tf])
        fin_im.append(hi[:, tf])
    return y.reshape(bsz, L, W_A), jnp.stack(fin_re, axis=1), jnp.stack(fin_im, axis=1)


def _even_layer(x, cond, w_mod_l, b_mod_l, ln_g_l, ln_b_l, pe, lam_init, rope, ctx_k, ctx_v, h0_re, h0_im):
    bsz, L = x.shape[0], x.shape[1]
    h, gate = _modulate_input(x, cond, w_mod_l, b_mod_l)
    proj = h @ pe['w_in']
    u_a, z_a, q, k, v, z_b = jnp.split(
        proj, [W_A, 2 * W_A, 2 * W_A + W_B, 2 * W_A + 2 * W_B, 2 * W_A + 3 * W_B], axis=-1)
    y_a, hf_re, hf_im = _s5_bidir(u_a, pe['lam_re'], pe['lam_im'], pe['log_dt'], pe['b_re'], pe['b_im'],
                                  pe['c_re'], pe['c_im'], pe['d'], h0_re, h0_im)
    g_a = jax.nn.gelu(y_a.astype(x.dtype))
    y_a = g_a * jax.nn.sigmoid(g_a @ pe['w_glu'] + pe['b_glu'])
    y_a = y_a * jax.nn.silu(z_a)
    q = q.reshape(bsz, L, H_B, 2, DH)
    k = k.reshape(bsz, L, H_B, 2, DH)
    v = v.reshape(bsz, L, H_B, 2 * DH)
    if rope is None:
        q_r, k_all, v_all = q, k, v
    else:
        cos, sin = rope
        q_r = _apply_rope(q, cos, sin)
        k_all = jnp.concatenate([ctx_k.astype(k.dtype), _apply_rope(k, cos, sin)], axis=1)
        v_all = jnp.concatenate([ctx_v.astype(v.dtype), v], axis=1)
    lam = (jnp.exp(jnp.sum(pe['lq1'].astype(F32) * pe['lk1'].astype(F32)))
           - jnp.exp(jnp.sum(pe['lq2'].astype(F32) * pe['lk2'].astype(F32))) + lam_init)
    o = _diff_attn(q_r, k_all, v_all, lam)
    o = o * lax.rsqrt(jnp.mean(jnp.square(o), axis=-1, keepdims=True) + LN_EPS)
    o = o * pe['subln_g'].astype(F32) * (1.0 - lam_init)
    y_b = o.reshape(bsz, L, W_B).astype(x.dtype) * jax.nn.silu(z_b)
    out = jnp.concatenate([y_a.astype(x.dtype), y_b], axis=-1) @ pe['w_out']
    return _post_norm(x, gate, out, ln_g_l, ln_b_l), k, v, hf_re, hf_im


def _odd_layer(x, cond, w_mod_l, b_mod_l, ln_g_l, ln_b_l, w_in, w_fno, b_fno, w_out):
    bsz, L = x.shape[0], x.shape[1]
    h, gate = _modulate_input(x, cond, w_mod_l, b_mod_l)
    u, z = jnp.split(h @ w_in, 2, axis=-1)
    uf = u.astype(F32).reshape(bsz, L, NG_C, GC_C)
    mixed = jnp.real(jnp.fft.fft2(uf, axes=(1, 3), norm='ortho')).reshape(bsz, L, W_C).astype(x.dtype)
    y = (mixed @ w_fno + b_fno) * jax.nn.silu(z)
    return _post_norm(x, gate, y @ w_out, ln_g_l, ln_b_l)


def setup_inputs(seed: int = 0) -> dict:
    key = jax.random.key(seed)
    ks = jax.random.split(key, 40)

    def nrm(k, shape, s):
        return jax.random.normal(k, shape, F32) * s

    n_idx = jnp.arange(P_A, dtype=F32)
    return {
        'x_prompt': nrm(ks[0], (BATCH, SEQ, D_MODEL), 1.0),
        'x_sample': nrm(ks[1], (DEC_BATCH, DEC_SEQ, D_MODEL), 1.0),
        'cache_k': nrm(ks[2], (DEC_BATCH, N_EVEN, PAST_LEN, H_B, 2, DH), 1.0),
        'cache_v': nrm(ks[3], (DEC_BATCH, N_EVEN, PAST_LEN, H_B, 2 * DH), 1.0),
        'state_ssm_re': nrm(ks[4], (DEC_BATCH, N_EVEN, 2, G_A, P_A), 0.5),
        'state_ssm_im': nrm(ks[5], (DEC_BATCH, N_EVEN, 2, G_A, P_A), 0.5),
        'c': nrm(ks[6], (DEC_BATCH, D_MODEL), 1.0),
        'c_ctx': nrm(ks[7], (D_MODEL,), 1.0),
        'w_mod': nrm(ks[8], (DEPTH, D_MODEL, 3 * D_MODEL), D_MODEL ** -0.5),
        'b_mod': nrm(ks[9], (DEPTH, 3 * D_MODEL), 0.02),
        'ln_g': 1.0 + nrm(ks[10], (DEPTH, D_MODEL), 0.02),
        'ln_b': nrm(ks[11], (DEPTH, D_MODEL), 0.02),
        'w_in_e': nrm(ks[12], (N_EVEN, D_MODEL, E_IN), D_MODEL ** -0.5),
        'ssm_lam_re': -0.5 + nrm(ks[13], (N_EVEN, 2, G_A, P_A), 0.01),
        'ssm_lam_im': math.pi * n_idx + nrm(ks[14], (N_EVEN, 2, G_A, P_A), 0.01),
        'ssm_log_dt': jax.random.uniform(ks[15], (N_EVEN, 2, G_A), F32, math.log(DT_MIN), math.log(DT_MAX)),
        'ssm_b_re': nrm(ks[16], (N_EVEN, 2, G_A, P_A, SSM_GROUP), (2 * SSM_GROUP) ** -0.5),
        'ssm_b_im': nrm(ks[17], (N_EVEN, 2, G_A, P_A, SSM_GROUP), (2 * SSM_GROUP) ** -0.5),
        'ssm_c_re': nrm(ks[18], (N_EVEN, 2, G_A, SSM_GROUP, P_A), P_A ** -0.5),
        'ssm_c_im': nrm(ks[19], (N_EVEN, 2, G_A, SSM_GROUP, P_A), P_A ** -0.5),
        'ssm_d': nrm(ks[20], (N_EVEN, W_A), 1.0),
        'w_glu': nrm(ks[21], (N_EVEN, W_A, W_A), W_A ** -0.5),
        'b_glu': nrm(ks[22], (N_EVEN, W_A), 0.02),
        'lam_q1': nrm(ks[23], (N_EVEN, DH), 0.1),
        'lam_k1': nrm(ks[24], (N_EVEN, DH), 0.1),
        'lam_q2': nrm(ks[25], (N_EVEN, DH), 0.1),
        'lam_k2': nrm(ks[26], (N_EVEN, DH), 0.1),
        'subln_g': 1.0 + nrm(ks[27], (N_EVEN, 2 * DH), 0.02),
        'w_out_e': nrm(ks[28], (N_EVEN, W_A + W_B, D_MODEL), (W_A + W_B) ** -0.5 * BETA),
        'w_in_o': nrm(ks[29], (N_ODD, D_MODEL, 2 * W_C), D_MODEL ** -0.5),
        'w_fno': nrm(ks[30], (N_ODD, W_C, W_C), W_C ** -0.5),
        'b_fno': nrm(ks[31], (N_ODD, W_C), 0.02),
        'w_out_o': nrm(ks[32], (N_ODD, W_C, D_MODEL), W_C ** -0.5 * BETA),
    }


def reference(x_prompt, x_sample, cache_k, cache_v, state_ssm_re, state_ssm_im, c, c_ctx,
              w_mod, b_mod, ln_g, ln_b, w_in_e, ssm_lam_re, ssm_lam_im, ssm_log_dt,
              ssm_b_re, ssm_b_im, ssm_c_re, ssm_c_im, ssm_d, w_glu, b_glu,
              lam_q1, lam_k1, lam_q2, lam_k2, subln_g, w_out_e, w_in_o, w_fno, b_fno, w_out_o):
    rope = _axial_rope(x_sample.shape[1])
    cond_ctx = c_ctx[None, :]
    bp = x_prompt.shape[0]
    xp, xs = x_prompt, x_sample
    new_k, new_v, new_sr, new_si = [], [], [], []
    for layer in range(DEPTH):
        wm, bm, lg, lb = w_mod[layer], b_mod[layer], ln_g[layer], ln_b[layer]
        if layer % 2 == 0:
            e = layer // 2
            lam_init = 0.8 - 0.6 * math.exp(-0.3 * layer)
            pe = dict(w_in=w_in_e[e], lam_re=ssm_lam_re[e], lam_im=ssm_lam_im[e], log_dt=ssm_log_dt[e],
                      b_re=ssm_b_re[e], b_im=ssm_b_im[e], c_re=ssm_c_re[e], c_im=ssm_c_im[e], d=ssm_d[e],
                      w_glu=w_glu[e], b_glu=b_glu[e], lq1=lam_q1[e], lk1=lam_k1[e], lq2=lam_q2[e],
                      lk2=lam_k2[e], subln_g=subln_g[e], w_out=w_out_e[e])
            zeros = jnp.zeros((bp, 2, G_A, P_A), F32)
            xp, k_ctx, v_ctx, s_re, s_im = _even_layer(xp, cond_ctx, wm, bm, lg, lb, pe, lam_init,
                                                       None, None, None, zeros, zeros)
            new_k.append(k_ctx)
            new_v.append(v_ctx)
            new_sr.append(s_re)
            new_si.append(s_im)
            xs = _even_layer(xs, c, wm, bm, lg, lb, pe, lam_init, rope, cache_k[:, e], cache_v[:, e],
                             state_ssm_re[:, e], state_ssm_im[:, e])[0]
        else:
            o = layer // 2
            xp = _odd_layer(xp, cond_ctx, wm, bm, lg, lb, w_in_o[o], w_fno[o], b_fno[o], w_out_o[o])
            xs = _odd_layer(xs, c, wm, bm, lg, lb, w_in_o[o], w_fno[o], b_fno[o], w_out_o[o])
    return (xp, xs, jnp.stack(new_k, axis=1), jnp.stack(new_v, axis=1),
            jnp.stack(new_sr, axis=1), jnp.stack(new_si, axis=1))
```

```python
import contextlib
import math
import numpy as np
import concourse.bass as bass
import concourse.mybir as mybir
from concourse.bass_utils import run_bass_kernel_spmd

F32 = mybir.dt.float32
BF16 = mybir.dt.bfloat16
ALU = mybir.AluOpType
AF = mybir.ActivationFunctionType
AX = mybir.AxisListType

D = 1024
NCORES = 8
TOK = 1024
LN_EPS = 1e-5
ALPHA = (2 * 2) ** 0.25
ENGS = ["sp", "act", "dve", "pool", "pe"]
EPOCH = 8000
T8 = 8
GELU_C = 2.0 * math.sqrt(2.0 / math.pi)


class Tracker:
    def __init__(self, sem_pool):
        self.sem_pool = list(sem_pool)
        self.ops = []
        self.buf = {}
        self.eng_cnt = {e: 0 for e in ENGS}
        self.eng_sems = {e: [] for e in ENGS}
        self.dma_sem = {}
        self.dma_cnt = {}

    def _st(self, k):
        if k not in self.buf:
            self.buf[k] = [None, []]
        return self.buf[k]

    def add(self, eng, emit, reads=(), writes=(), dma_key=None):
        reads = list(reads)
        writes = list(writes)
        if dma_key is not None:
            writes.append("#dma:" + dma_key)
        deps = set()
        for k in reads:
            st = self._st(k)
            if st[0] is not None:
                deps.add(st[0])
            if k.startswith("pf") or k.startswith("pt"):
                for r_ in st[1]:
                    if r_[4] != eng:
                        deps.add(r_)
        for k in writes:
            st = self._st(k)
            if st[0] is not None:
                deps.add(st[0])
            for r in st[1]:
                deps.add(r)
        if dma_key is not None:
            if dma_key not in self.dma_sem:
                self.dma_sem[dma_key] = self.sem_pool.pop()
                self.dma_cnt[dma_key] = 0
            self.dma_cnt[dma_key] += 16
            sem = self.dma_sem[dma_key]
            token = (id(sem), self.dma_cnt[dma_key], sem, 16, None)
        else:
            c = self.eng_cnt[eng]
            ep, v = divmod(c, EPOCH)
            if ep >= len(self.eng_sems[eng]):
                self.eng_sems[eng].append(self.sem_pool.pop())
            sem = self.eng_sems[eng][ep]
            self.eng_cnt[eng] = c + 1
            token = (id(sem), v + 1, sem, 1, eng)
        for k in reads:
            self._st(k)[1].append(token)
        for k in writes:
            st = self._st(k)
            st[0] = token
            st[1] = []
        self.ops.append((eng, emit, deps, token))
        return token

    def emit_engine(self, eng, e):
        waited = {}
        for (oe, emit, deps, token) in self.ops:
            if oe != eng:
                continue
            need = {}
            for (sid, val, sem, inc, deng) in deps:
                if deng == "pe" and eng == "pe":
                    continue
                if sid not in need or need[sid][0] < val:
                    need[sid] = (val, sem)
            for sid, (val, sem) in need.items():
                if waited.get(sid, 0) >= val:
                    continue
                e.wait_ge(sem, val)
                waited[sid] = val
            emit(e).then_inc(token[2], token[3])

    def all_tokens(self):
        toks = []
        for k, st in self.buf.items():
            if st[0] is not None:
                toks.append(st[0])
            toks += st[1]
        return toks

    def final_waits(self, e, tokens):
        need = {}
        for (sid, val, sem, inc, deng) in tokens:
            if sid not in need or need[sid][0] < val:
                need[sid] = (val, sem)
        for sid, (val, sem) in need.items():
            e.wait_ge(sem, val)


def run_block(nc, tr, final_tokens):
    with nc.Block() as block:
        @block.sync
        def _(e):
            tr.emit_engine("sp", e)
            tr.final_waits(e, final_tokens)

        @block.scalar
        def _(e):
            tr.emit_engine("act", e)

        @block.vector
        def _(e):
            tr.emit_engine("dve", e)

        @block.gpsimd
        def _(e):
            tr.emit_engine("pool", e)

        @block.tensor
        def _(e):
            tr.emit_engine("pe", e)


def phase0(nc, dr, sems):
    tr = Tracker(sems)
    es = contextlib.ExitStack()
    with es:
        def sb(name, shape, dt):
            return es.enter_context(nc.sbuf_tensor("p0_" + name, shape, dt))

        def ps(name, shape, dt):
            return es.enter_context(nc.psum_tensor("p0_" + name, shape, dt))
        uid = [0]

        def A(eng, fn, r=(), w=(), dk=None):
            return tr.add(eng, fn, reads=r, writes=w, dma_key=dk)

        identf = sb("identf", [128, 128], F32)
        identb = sb("identb", [128, 128], BF16)
        maskF = sb("maskF", [128, 128], F32)
        maskB = sb("maskB", [128, 128], F32)
        lin = [sb(f"lin{i}", [64, 2, 64], F32) for i in range(2)]
        LR = sb("LR", [128, 64], F32)
        LI = sb("LI", [128, 64], F32)
        LDT = sb("LDT", [128, 64], F32)
        BR = sb("BR", [128, 64, 16], F32)
        BI = sb("BI", [128, 64, 16], F32)
        CR = sb("CR", [128, 64, 16], F32)
        CI = sb("CI", [128, 64, 16], F32)
        nCR = sb("nCR", [128, 64, 16], F32)
        nCI = sb("nCI", [128, 64, 16], F32)
        cin = [sb(f"cin{i}", [128, 2, 64], F32) for i in range(2)]
        Dcol = sb("Dcol", [128, 64], F32)
        sm = {n: sb(n, [128, 64], F32) for n in
              ["dt", "ldr", "ldi", "c", "s", "mag", "ar", "ai", "e2", "ivr", "ivi", "t1", "t2", "t3", "t4",
               "cr", "ci", "nr", "ni", "am1", "den", "fr", "fi", "A8r", "A8i"]}
        EPr = sb("EPr", [128, 64, 8], F32)
        EPi = sb("EPi", [128, 64, 8], F32)
        ENr = sb("ENr", [128, 64, 8], F32)
        ENi = sb("ENi", [128, 64, 8], F32)
        bbr = sb("bbr", [128, 64, 16], F32)
        bbi = sb("bbi", [128, 64, 16], F32)
        bt1 = sb("bt1", [128, 64, 16], F32)
        bt2 = sb("bt2", [128, 64, 16], F32)
        T1 = sb("T1", [128, 16, 8, 16], F32)
        T2 = sb("T2", [128, 16, 8, 16], F32)
        T3 = sb("T3", [128, 16, 8, 16], F32)
        T4 = sb("T4", [128, 16, 8, 16], F32)
        PRb = sb("PRb", [128, 64, 128], BF16)
        PIb = sb("PIb", [128, 64, 128], BF16)
        Qrb = sb("Qrb", [128, 64, 128], BF16)
        nQib = sb("nQib", [128, 64, 128], BF16)
        Wop = sb("Wop", [128, 64, 2, 128], BF16)
        Mop = sb("Mop", [128, 64, 128], BF16)
        lq = sb("lq", [128, 4, 64], F32)
        lsum = sb("lsum", [128, 2], F32)
        lprod = sb("lprod", [128, 2, 64], F32)
        nlam = sb("nlam", [128, 1], F32)
        pT = ps("pT", [128, 128], F32)
        pW = [ps(f"pW{i}", [128, 8, 128], BF16) for i in range(2)]
        pMf = ps("pMf", [128, 4, 128], F32)
        pMb = ps("pMb", [128, 4, 128], F32)
        pmod = [ps(f"pmod{i}", [128, 256], F32) for i in range(2)]
        ccm = sb("ccm", [128, 2, 8], F32)
        scm = sb("scm", [128, 2, 8], F32)

        A("sp", lambda e: e.dma_start(out=identf[:], in_=dr["ident"]), w=["identf"], dk="identf")
        A("sp", lambda e: e.dma_start(out=maskF[:], in_=dr["maskF"]), w=["maskF"], dk="maskF")
        A("sp", lambda e: e.dma_start(out=maskB[:], in_=dr["maskB"]), w=["maskB"], dk="maskB")
        A("dve", lambda e: e.tensor_copy(out=identb[:], in_=identf[:]), r=["identf"], w=["identb"])
        if "wmod" in dr and "s_mod" in dr:
            for ci, cn in enumerate(["cctx", "csmp"]):
                A("sp", lambda e, cn=cn, ci=ci: e.dma_start(out=ccm[:, ci, :], in_=dr[cn].rearrange("(k p) -> p k", p=128), allow_slow_non_contiguous=True),
                  w=["ccm"], dk="ccm")
            A("act", lambda e: e.activation(out=scm[:], in_=ccm[:], func=AF.Silu), r=["ccm"], w=["scm"])
        for i, (src, dst, dn) in enumerate([("lam_re", LR, "LR"), ("lam_im", LI, "LI")]):
            A("sp", lambda e, i=i, src=src: e.dma_start(out=lin[i][:], in_=dr[src].rearrange("d g p -> g d p")),
              w=[f"lin{i}"], dk=f"lin{i}")
            A("pe", lambda e, i=i: e.transpose(pT[:, 0:64], lin[i][:].rearrange("g d p -> g (d p)"), identf[0:64, 0:64]),
              r=[f"lin{i}", "identf"], w=["pT"])
            A("dve", lambda e, dst=dst: e.tensor_copy(out=dst[:], in_=pT[:, 0:64]), r=["pT"], w=[dn])
        for d in range(2):
            A("sp", lambda e, d=d: e.dma_start(out=LDT[64 * d:64 * d + 64, :], in_=dr["log_dt"][d].partition_broadcast(64)),
              w=["LDT"], dk=f"LDT{d}")
            A("sp", lambda e, d=d: e.dma_start(out=BR[64 * d:64 * d + 64], in_=dr["b_re"][d].rearrange("g p m -> p g m")),
              w=["BR"], dk=f"BR{d}")
            A("act", lambda e, d=d: e.dma_start(out=BI[64 * d:64 * d + 64], in_=dr["b_im"][d].rearrange("g p m -> p g m")),
              w=["BI"], dk=f"BI{d}")
        k = 0
        for (src, dst, dn) in [("c_re", CR, "CR"), ("c_im", CI, "CI")]:
            for blk in range(8):
                b = k % 2
                k += 1
                A("sp", lambda e, b=b, src=src, blk=blk: e.dma_start(
                    out=cin[b][:], in_=dr[src][:, 8 * blk:8 * blk + 8].rearrange("d g n p -> (g n) d p")),
                  w=[f"cin{b}"], dk=f"cin{b}")
                A("pe", lambda e, b=b: e.transpose(pT[:], cin[b][:].rearrange("q d p -> q (d p)"), identf[:]),
                  r=[f"cin{b}", "identf"], w=["pT"])
                A("dve", lambda e, dst=dst, blk=blk: e.tensor_copy(
                    out=dst[:, 8 * blk:8 * blk + 8, :], in_=pT[:].rearrange("q (g n) -> q g n", n=16)), r=["pT"], w=[dn])
        S = sm

        def tt(o, a, b, op, eng="dve"):
            A(eng, lambda e: e.tensor_tensor(out=S[o][:], in0=S[a][:], in1=S[b][:], op=op), r=[a, b], w=[o])

        def cmul(zr, zi, xr, xi, yr, yi):
            tt("t1", xr, yr, ALU.mult); tt("t2", xi, yi, ALU.mult)
            tt("t3", xr, yi, ALU.mult); tt("t4", xi, yr, ALU.mult)
            tt(zr, "t1", "t2", ALU.subtract); tt(zi, "t3", "t4", ALU.add)

        A("act", lambda e: e.activation(out=S["dt"][:], in_=LDT[:], func=AF.Exp), r=["LDT"], w=["dt"])
        A("dve", lambda e: e.tensor_tensor(out=S["ldr"][:], in0=LR[:], in1=S["dt"][:], op=ALU.mult), r=["LR", "dt"], w=["ldr"])
        A("dve", lambda e: e.tensor_tensor(out=S["ldi"][:], in0=LI[:], in1=S["dt"][:], op=ALU.mult), r=["LI", "dt"], w=["ldi"])
        A("dve", lambda e: e.tensor_scalar(out=S["t1"][:], in0=S["ldi"][:], scalar1=1.0 / 32, scalar2=math.pi / 2,
                                           op0=ALU.mult, op1=ALU.add), r=["ldi"], w=["t1"])
        A("act", lambda e: e.activation(out=S["c"][:], in_=S["t1"][:], func=AF.Sin), r=["t1"], w=["c"])
        A("act", lambda e: e.activation(out=S["s"][:], in_=S["ldi"][:], func=AF.Sin, scale=1.0 / 32), r=["ldi"], w=["s"])
        for _ in range(5):
            tt("t1", "c", "c", ALU.mult); tt("t2", "s", "s", ALU.mult); tt("t3", "c", "s", ALU.mult)
            tt("c", "t1", "t2", ALU.subtract); tt("s", "t3", "t3", ALU.add)
        A("act", lambda e: e.activation(out=S["mag"][:], in_=S["ldr"][:], func=AF.Exp), r=["ldr"], w=["mag"])
        A("act", lambda e: e.activation(out=S["e2"][:], in_=S["ldr"][:], func=AF.Exp, scale=-2.0), r=["ldr"], w=["e2"])
        tt("ar", "mag", "c", ALU.mult); tt("ai", "mag", "s", ALU.mult)
        tt("ivr", "ar", "e2", ALU.mult)
        A("dve", lambda e: e.scalar_tensor_tensor(out=S["ivi"][:], in0=S["ai"][:], scalar=-1.0, in1=S["e2"][:],
                                                  op0=ALU.mult, op1=ALU.mult), r=["ai", "e2"], w=["ivi"])
        for (tr_, ti_, br_, bi_, cr_, ci_) in [(EPr, EPi, "ar", "ai", "cr", "ci"), (ENr, ENi, "ivr", "ivi", "nr", "ni")]:
            A("dve", lambda e, cr_=cr_, br_=br_: e.tensor_copy(out=S[cr_][:], in_=S[br_][:]), r=[br_], w=[cr_])
            A("dve", lambda e, ci_=ci_, bi_=bi_: e.tensor_copy(out=S[ci_][:], in_=S[bi_][:]), r=[bi_], w=[ci_])
            tn = "EP" if tr_ is EPr else "EN"
            for kk in range(1, 9):
                for (tab, cur) in [(tr_, cr_), (ti_, ci_)]:
                    A("act", lambda e, tab=tab, cur=cur, kk=kk: e.activation(out=tab[0:64, :, kk - 1], in_=S[cur][0:64, :], func=AF.Copy),
                      r=[cur], w=[tn])
                    A("act", lambda e, tab=tab, cur=cur, kk=kk: e.activation(out=tab[64:128, :, 8 - kk], in_=S[cur][64:128, :], func=AF.Copy),
                      r=[cur], w=[tn])
                if kk == 8 and tr_ is EPr:
                    A("dve", lambda e: e.tensor_copy(out=S["A8r"][:], in_=S["cr"][:]), r=["cr"], w=["A8r"])
                    A("dve", lambda e: e.tensor_copy(out=S["A8i"][:], in_=S["ci"][:]), r=["ci"], w=["A8i"])
                if kk < 8:
                    cmul(cr_, ci_, cr_, ci_, br_, bi_)
        A("sp", lambda e: e.dma_start(out=dr["s_A8"][:, 0, :], in_=S["A8r"][:]), r=["A8r"], w=["s_A8r"], dk="s_A8r")
        A("sp", lambda e: e.dma_start(out=dr["s_A8"][:, 1, :], in_=S["A8i"][:]), r=["A8i"], w=["s_A8i"], dk="s_A8i")
        for i in range(8):
            A("sp", lambda e, i=i: e.dma_start(out=Dcol[16 * i:16 * i + 16, :], in_=dr["ssm_d"].rearrange("(g m) -> m g", m=16),
                                               allow_slow_non_contiguous=True), w=["Dcol"], dk="Dcol")
        for i, nm in enumerate(["lq1", "lk1", "lq2", "lk2"]):
            A("sp", lambda e, i=i, nm=nm: e.dma_start(out=lq[:, i, :], in_=dr[nm].partition_broadcast(128)), w=["lq"], dk="lq")
        A("dve", lambda e: e.tensor_tensor(out=lprod[:, 0, :], in0=lq[:, 0, :], in1=lq[:, 1, :], op=ALU.mult), r=["lq"], w=["lprod"])
        A("dve", lambda e: e.tensor_tensor(out=lprod[:, 1, :], in0=lq[:, 2, :], in1=lq[:, 3, :], op=ALU.mult), r=["lq"], w=["lprod"])
        A("dve", lambda e: e.reduce_sum(out=lsum[:], in_=lprod[:], axis=AX.X), r=["lprod"], w=["lsum"])
        A("act", lambda e: e.activation(out=lsum[:], in_=lsum[:], func=AF.Exp), r=["lsum"], w=["lsum"])
        A("dve", lambda e: e.scalar_tensor_tensor(out=nlam[:], in0=lsum[:, 1:2], scalar=-0.2, in1=lsum[:, 0:1],
                                                  op0=ALU.add, op1=ALU.subtract), r=["lsum"], w=["nlam"])
        A("sp", lambda e: e.dma_start(out=dr["s_nlam"], in_=nlam[:]), r=["nlam"], w=["s_nlam"], dk="s_nlam")

        A("dve", lambda e: e.tensor_scalar_add(out=S["am1"][:], in0=S["ar"][:], scalar1=-1.0), r=["ar"], w=["am1"])
        A("dve", lambda e: e.tensor_tensor(out=S["t1"][:], in0=LR[:], in1=LR[:], op=ALU.mult), r=["LR"], w=["t1"])
        A("dve", lambda e: e.tensor_tensor(out=S["t2"][:], in0=LI[:], in1=LI[:], op=ALU.mult), r=["LI"], w=["t2"])
        tt("den", "t1", "t2", ALU.add)
        A("dve", lambda e: e.reciprocal(out=S["den"][:], in_=S["den"][:]), r=["den"], w=["den"])
        A("dve", lambda e: e.tensor_tensor(out=S["t1"][:], in0=S["am1"][:], in1=LR[:], op=ALU.mult), r=["am1", "LR"], w=["t1"])
        A("dve", lambda e: e.tensor_tensor(out=S["t2"][:], in0=S["ai"][:], in1=LI[:], op=ALU.mult), r=["ai", "LI"], w=["t2"])
        tt("t3", "t1", "t2", ALU.add); tt("fr", "t3", "den", ALU.mult)
        A("dve", lambda e: e.tensor_tensor(out=S["t1"][:], in0=S["ai"][:], in1=LR[:], op=ALU.mult), r=["ai", "LR"], w=["t1"])
        A("dve", lambda e: e.tensor_tensor(out=S["t2"][:], in0=S["am1"][:], in1=LI[:], op=ALU.mult), r=["am1", "LI"], w=["t2"])
        tt("t3", "t1", "t2", ALU.subtract); tt("fi", "t3", "den", ALU.mult)
        frb = S["fr"][:].unsqueeze(2).to_broadcast([128, 64, 16])
        fib = S["fi"][:].unsqueeze(2).to_broadcast([128, 64, 16])
        A("dve", lambda e: e.tensor_tensor(out=bt1[:], in0=BR[:], in1=frb, op=ALU.mult), r=["BR", "fr"], w=["bt1"])
        A("dve", lambda e: e.tensor_tensor(out=bt2[:], in0=BI[:], in1=fib, op=ALU.mult), r=["BI", "fi"], w=["bt2"])
        A("dve", lambda e: e.tensor_tensor(out=bbr[:], in0=bt1[:], in1=bt2[:], op=ALU.subtract), r=["bt1", "bt2"], w=["bbr"])
        A("dve", lambda e: e.tensor_tensor(out=bt1[:], in0=BI[:], in1=frb, op=ALU.mult), r=["BI", "fr"], w=["bt1"])
        A("dve", lambda e: e.tensor_tensor(out=bt2[:], in0=BR[:], in1=fib, op=ALU.mult), r=["BR", "fi"], w=["bt2"])
        A("dve", lambda e: e.tensor_tensor(out=bbi[:], in0=bt1[:], in1=bt2[:], op=ALU.add), r=["bt1", "bt2"], w=["bbi"])
        A("dve", lambda e: e.tensor_scalar_mul(out=nCR[:], in0=CR[:], scalar1=-1.0), r=["CR"], w=["nCR"])
        A("dve", lambda e: e.tensor_scalar_mul(out=nCI[:], in0=CI[:], scalar1=-1.0), r=["CI"], w=["nCI"])
        for gb in range(4):
            gs = slice(16 * gb, 16 * gb + 16)

            def prod(o, tab, vec, tn, vn, gs=gs, eng="dve"):
                a0 = tab[:, gs, :].unsqueeze(3).to_broadcast([128, 16, 8, 16])
                a1 = vec[:, gs, :].unsqueeze(2).to_broadcast([128, 16, 8, 16])
                A(eng, lambda e, o=o, a0=a0, a1=a1: e.tensor_tensor(out=o[:], in0=a0, in1=a1, op=ALU.mult), r=[tn, vn], w=[o.name])

            def comb(dst, dn, op, gs=gs, ta=T1, tb=T2, eng="dve"):
                ov = dst[:, gs, :].rearrange("q g (i m) -> q g i m", m=16)
                A(eng, lambda e, ov=ov, op=op, ta=ta, tb=tb: e.tensor_tensor(out=ov, in0=ta[:], in1=tb[:], op=op), r=[ta.name, tb.name], w=[dn])
            prod(T1, ENr, bbr, "EN", "bbr"); prod(T2, ENi, bbi, "EN", "bbi"); comb(PRb, "PRb", ALU.subtract)
            prod(T1, ENr, bbi, "EN", "bbi"); prod(T2, ENi, bbr, "EN", "bbr"); comb(PIb, "PIb", ALU.add)
            if gb < 2:
                qe, qa, qb_ = "pool", T3, T4
            else:
                qe, qa, qb_ = "dve", T1, T2
            prod(qa, EPr, CR, "EP", "CR", eng=qe); prod(qb_, EPi, CI, "EP", "CI", eng=qe); comb(Qrb, "Qrb", ALU.subtract, ta=qa, tb=qb_, eng=qe)
            prod(qa, EPr, nCI, "EP", "nCI", eng=qe); prod(qb_, EPi, nCR, "EP", "nCR", eng=qe); comb(nQib, "nQib", ALU.add, ta=qa, tb=qb_, eng=qe)
        A("sp", lambda e: e.dma_start(out=dr["s_opQ"][:, :, 0, :], in_=Qrb[:]), r=["Qrb"], w=["s_opQ0"], dk="s_opQ0")
        A("sp", lambda e: e.dma_start(out=dr["s_opQ"][:, :, 1, :], in_=nQib[:]), r=["nQib"], w=["s_opQ1"], dk="s_opQ1")
        k = 0
        for ri, (P_, pn) in enumerate([(PRb, "PRb"), (PIb, "PIb")]):
            for blk in range(8):
                b = k % 2
                k += 1
                for gl in range(8):
                    A("pe", lambda e, b=b, gl=gl, P_=P_, blk=blk: e.transpose(pW[b][:, gl, :], P_[:, 8 * blk + gl, :], identb[:]),
                      r=[pn, "identb"], w=[f"pW{b}"])
                A("act", lambda e, b=b, ri=ri, blk=blk: e.activation(out=Wop[:, 8 * blk:8 * blk + 8, ri, :], in_=pW[b][:], func=AF.Copy),
                  r=[f"pW{b}"], w=["Wop"])
        A("sp", lambda e: e.dma_start(out=dr["s_opW"], in_=Wop[:]), r=["Wop"], w=["s_opW"], dk="s_opW")
        do_mod = "wmod" in dr and "s_mod" in dr
        if do_mod:
            screpv = [t[:].bitcast(BF16).rearrange("p a b -> p (a b)").rearrange("p (k n) -> p k n", n=128) for t in (ENr, ENi)]
            wslots = [t[:].bitcast(BF16).rearrange("p a b -> p (a b)").rearrange("p (k n) -> p k n", n=256) for t in (BR, BI, CR, CI, nCR, nCI)]
            wkeys = ["BR", "BI", "CR", "CI", "nCR", "nCI"]
            NSL = 6
            biasf = EPr[:].rearrange("p a b -> p (a b)")
            rows = [[(T3[:].rearrange("p a b c -> p (a b c)"), T3.name), (bt1[:].rearrange("p a b -> p (a b)"), "bt1")],
                    [(T4[:].rearrange("p a b c -> p (a b c)"), T4.name), (bt2[:].rearrange("p a b -> p (a b)"), "bt2")]]
            for ci in range(2):
                A("dve", lambda e, ci=ci: e.tensor_copy(out=screpv[ci], in_=scm[:, ci, :].unsqueeze(2).to_broadcast([128, 8, 128])), r=["scm"], w=["EN"])

            def emit_mod_dma(idx):
                if idx >= 24:
                    return
                layer, n = divmod(idx, 12)
                slot = idx % NSL
                wv, wk = wslots[slot], wkeys[slot]
                A("pool", lambda e, wv=wv, layer=layer, n=n: e.dma_start(
                    out=wv, in_=dr["wmod"][layer][:, n * 256:(n + 1) * 256].rearrange("(k p) n -> p k n", p=128)), w=[wk], dk=f"mw{slot}")

            def emit_mod_chunk(idx):
                layer, n = divmod(idx, 12)
                slot = idx % NSL
                wv, wk = wslots[slot], wkeys[slot]
                if n % 2 == 0:
                    A("pool", lambda e, layer=layer, n=n: e.dma_start(out=biasf, in_=dr["bmod"][layer][(n // 2) * 512:(n // 2 + 1) * 512].partition_broadcast(128)),
                      w=["EP"], dk="mbias")
                for ci in range(2):
                    p = pmod[ci]
                    for kt in range(8):
                        A("pe", lambda e, kt=kt, p=p, wv=wv, ci=ci: e.matmul(p[:], lhsT=screpv[ci][:, kt, :], rhs=wv[:, kt, :], start=(kt == 0), stop=(kt == 7)),
                          r=["EN", wk], w=[f"pmod{ci}"])
                    c0 = n * 256
                    rb, rk = rows[ci][0] if c0 < 2048 else rows[ci][1]
                    cc0 = c0 if c0 < 2048 else c0 - 2048
                    A("dve", lambda e, p=p, rb=rb, cc0=cc0, n=n: e.tensor_tensor(out=rb[:, cc0:cc0 + 256], in0=p[:], in1=biasf[:, (n % 2) * 256:(n % 2) * 256 + 256], op=ALU.add),
                      r=[f"pmod{ci}", "EP"], w=[rk])
                emit_mod_dma(idx + NSL)
                if n == 11:
                    for ci in range(2):
                        (ra, rak), (rb2, rbk) = rows[ci]
                        A("sp", lambda e, ra=ra, layer=layer, ci=ci: e.dma_start(out=dr["s_mod"][layer, ci:ci + 1, 0:2048], in_=ra[0:1, :]),
                          r=[rak], w=["s_mod"], dk="smw")
                        A("sp", lambda e, rb2=rb2, layer=layer, ci=ci: e.dma_start(out=dr["s_mod"][layer, ci:ci + 1, 2048:3072], in_=rb2[0:1, :]),
                          r=[rbk], w=["s_mod"], dk="smw")
            for i0_ in range(NSL):
                emit_mod_dma(i0_)
        mod_next = [0]
        mF4 = maskF[:].unsqueeze(1).to_broadcast([128, 4, 128])
        mB4 = maskB[:].unsqueeze(1).to_broadcast([128, 4, 128])
        T1v = T1[:].rearrange("q a b c -> q (a b c)")[:, 0:512].rearrange("q (g n) -> q g n", n=128)
        T2v = T2[:].rearrange("q a b c -> q (a b c)")[:, 0:512].rearrange("q (g n) -> q g n", n=128)
        for blk in range(16):
            for gl in range(4):
                g = 4 * blk + gl
                for (pp, lo) in [(pMf, 0), (pMb, 64)]:
                    A("pe", lambda e, pp=pp, lo=lo, g=g, gl=gl: e.matmul(pp[:, gl, :], lhsT=PRb[lo:lo + 64, g, :], rhs=Qrb[lo:lo + 64, g, :],
                                                                       start=True, stop=False),
                      r=["PRb", "Qrb"], w=[pp.name])
                    A("pe", lambda e, pp=pp, lo=lo, g=g, gl=gl: e.matmul(pp[:, gl, :], lhsT=PIb[lo:lo + 64, g, :], rhs=nQib[lo:lo + 64, g, :],
                                                                       start=False, stop=True),
                      r=["PIb", "nQib"], w=[pp.name])
            A("dve", lambda e: e.tensor_tensor(out=T1v, in0=pMf[:], in1=mF4, op=ALU.mult), r=[pMf.name, "maskF"], w=[T1.name])
            A("dve", lambda e: e.tensor_tensor(out=T2v, in0=pMb[:], in1=mB4, op=ALU.mult), r=[pMb.name, "maskB"], w=[T2.name])
            A("dve", lambda e: e.tensor_tensor(out=T1v, in0=T1v, in1=T2v, op=ALU.add), r=[T1.name, T2.name], w=[T1.name])
            for gl in range(4):
                g = 4 * blk + gl
                A("dve", lambda e, g=g, gl=gl: e.scalar_tensor_tensor(out=Mop[:, g, :], in0=identf[:], scalar=Dcol[:, g:g + 1], in1=T1v[:, gl, :],
                                                                      op0=ALU.mult, op1=ALU.add), r=["identf", "Dcol", T1.name], w=["Mop"])
            if do_mod:
                for _ in range(2):
                    if mod_next[0] < 24:
                        emit_mod_chunk(mod_next[0])
                        mod_next[0] += 1
        A("sp", lambda e: e.dma_start(out=dr["s_opM"], in_=Mop[:]), r=["Mop"], w=["s_opM"], dk="s_opM")
        run_block(nc, tr, tr.all_tokens())
    return tr


class Main:
    def __init__(self, nc, dr, sems, pre_tokens, debug=None):
        self.nc = nc
        self.dr = dr
        self.tr = Tracker(sems)
        self.pre = pre_tokens
        self.rot = {}
        self.out_tokens = []
        self.debug = debug or {}

    def A(self, eng, fn, r=(), w=(), dk=None):
        return self.tr.add(eng, fn, reads=r, writes=w, dma_key=dk)

    def nxt(self, name, n):
        i = self.rot.get(name, 0)
        self.rot[name] = (i + 1) % n
        return i

    def alloc(self, es):
        nc = self.nc

        def sb(name, shape, dt):
            return es.enter_context(nc.sbuf_tensor("m_" + name, shape, dt))

        def ps(name, shape, dt):
            return es.enter_context(nc.psum_tensor("m_" + name, shape, dt))
        self.hT = sb("hT", [128, 8, 1024], BF16)
        self.AB = sb("AB", [128, 16384], BF16)
        self.C = sb("C", [128, 16384], BF16)
        self.T = sb("T", [128, 16384], BF16)
        self.wr = [sb(f"wr{i}", [128, 2048], BF16) for i in range(6)]
        self.mrep = sb("mrep", [128, 3072], F32)
        self.lngr = sb("lngr", [128, 1024], F32)
        self.lnbr = sb("lnbr", [128, 1024], F32)
        self.xt = [sb(f"xt{i}", [128, 1024], F32) for i in range(2)]
        self.xn = [sb(f"xn{i}", [128, 1024], F32) for i in range(2)]
        self.hb = [sb(f"hb{i}", [128, 1024], BF16) for i in range(2)]
        self.tmp = [sb(f"tmp{i}", [128, 512], F32) for i in range(4)]
        self.ost = [sb(f"ost{i}", [128, 1024], F32) for i in range(2)]
        self.statsT = sb("statsT", [128, 2, 8, 2, 6], F32)
        self.mvT = sb("mvT", [128, 2, 8, 2], F32)
        self.rstdT = sb("rstdT", [128, 2, 8], F32)
        self.nmrT = sb("nmrT", [128, 2, 8], F32)
        self.identf = sb("identf", [128, 128], F32)
        self.identb = sb("identb", [128, 128], BF16)
        self.onesb = sb("onesb", [128, 128], BF16)
        self.onesf = sb("onesf", [128, 128], F32)
        self.pswf = sb("pswf", [128, 128], F32)
        self.pswb = sb("pswb", [128, 128], BF16)
        self.A8 = sb("A8", [128, 2, 64], F32)
        self.nlam = sb("nlam", [128, 1], F32)
        self.subg = sb("subg", [128, 1], F32)
        self.bglu = sb("bglu", [128, 8], F32)
        self.bfno = sb("bfno", [128, 16], F32)
        self.cc = sb("cc", [128, 8], F32)
        self.sc = sb("sc", [128, 8], F32)
        self.screp = sb("screp", [128, 8, 128], BF16)
        self.st = sb("st", [128, 2, 32, 4], F32)
        self.st2 = sb("st2", [128, 2, 32, 4], F32)
        self.zz = sb("zz", [128, 2, 32, 4], F32)
        self.q12 = sb("q12", [128, 2, 2, 32, 4], F32)
        self.a8 = sb("a8", [128, 2, 2, 32, 4], F32)
        self.epsc = sb("epsc", [128, 1], F32)
        self.fin = sb("fin", [128, 2, 4, 64], F32)
        self.pf = [ps(f"pf{i}", [128, 512], F32) for i in range(6)]
        self.pt = [ps(f"pt{i}", [128, 8, 128], BF16) for i in range(2)]

    def load_consts(self):
        A, dr = self.A, self.dr
        A("sp", lambda e: e.dma_start(out=self.identf[:], in_=dr["ident"]), w=["identf"], dk="identf")
        A("dve", lambda e: e.tensor_copy(out=self.identb[:], in_=self.identf[:]), r=["identf"], w=["identb"])
        A("dve", lambda e: e.memset(self.onesb[:], 1.0), w=["onesb"])
        A("dve", lambda e: e.memset(self.epsc[:], LN_EPS), w=["epsc"])
        A("dve", lambda e: e.memset(self.onesf[:], 1.0 / 128), w=["onesf"])
        A("sp", lambda e: e.dma_start(out=self.pswf[:], in_=dr["pswap"]), w=["pswf"], dk="pswf")
        A("dve", lambda e: e.tensor_copy(out=self.pswb[:], in_=self.pswf[:]), r=["pswf"], w=["pswb"])
        A("sp", lambda e: e.dma_start(out=self.A8[:], in_=dr["s_A8"]), w=["A8"], dk="A8")
        A("sp", lambda e: e.dma_start(out=self.nlam[:], in_=dr["s_nlam"]), w=["nlam"], dk="nlam")
        A("sp", lambda e: e.dma_start(out=self.subg[:], in_=dr["subg"].rearrange("(p o) -> p o", o=1)), w=["subg"], dk="subg")
        A("dve", lambda e: e.tensor_scalar_mul(out=self.subg[:], in0=self.subg[:], scalar1=0.8), r=["subg"], w=["subg"])
        A("sp", lambda e: e.dma_start(out=self.bglu[:], in_=dr["bglu"].rearrange("(k p) -> p k", p=128), allow_slow_non_contiguous=True),
          w=["bglu"], dk="bglu")
        A("sp", lambda e: e.dma_start(out=self.bfno[:], in_=dr["bfno"].rearrange("(k p) -> p k", p=128), allow_slow_non_contiguous=True),
          w=["bfno"], dk="bfno")

    def wload(self, src2d, ktiles, ncols):
        i = self.nxt("wr", 6)
        assert ktiles * ncols <= 2048
        view = self.wr[i][:, 0:ktiles * ncols].rearrange("p (k n) -> p k n", n=ncols)
        srcv = src2d.rearrange("(k p) n -> p k n", p=128)
        self.A("pool", lambda e: e.dma_start(out=view, in_=srcv), w=[f"wr{i}"], dk=f"wr{i}")
        return view, f"wr{i}"

    def mod_vectors(self, layer, cond_ap):
        A, dr = self.A, self.dr
        A("sp", lambda e: e.dma_start(out=self.cc[:], in_=cond_ap.rearrange("(k p) -> p k", p=128), allow_slow_non_contiguous=True),
          w=["cc"], dk="cc")
        A("act", lambda e: e.activation(out=self.sc[:], in_=self.cc[:], func=AF.Silu), r=["cc"], w=["sc"])
        A("dve", lambda e: e.tensor_copy(out=self.screp[:], in_=self.sc[:].unsqueeze(2).to_broadcast([128, 8, 128])), r=["sc"], w=["screp"])
        for n in range(12):
            wv, wk = self.wload(dr["wmod"][layer][:, n * 256:(n + 1) * 256], 8, 256)
            pi = self.nxt("pf01", 2)
            p = self.pf[pi]
            if n % 2 == 0:
                nn = n // 2
                A("sp", lambda e, nn=nn: e.dma_start(out=self.tmp[3][:], in_=dr["bmod"][layer][nn * 512:(nn + 1) * 512].partition_broadcast(128)),
                  w=["tmp3"], dk="brep")
            for kt in range(8):
                A("pe", lambda e, kt=kt, p=p, wv=wv: e.matmul(p[:, 0:256], lhsT=self.screp[:, kt, :], rhs=wv[:, kt, :], start=(kt == 0), stop=(kt == 7)),
                  r=["screp", wk], w=[f"pf{pi}"])
            A("dve", lambda e, n=n, p=p: e.tensor_tensor(out=self.mrep[:, n * 256:(n + 1) * 256], in0=p[:, 0:256],
                                                         in1=self.tmp[3][:, (n % 2) * 256:(n % 2) * 256 + 256], op=ALU.add),
              r=[f"pf{pi}", "tmp3"], w=["mrep"])
        A("dve", lambda e: e.tensor_scalar_add(out=self.mrep[:, 1024:2048], in0=self.mrep[:, 1024:2048], scalar1=1.0), r=["mrep"], w=["mrep"])

    def precompute_mod(self):
        A, dr = self.A, self.dr
        screps = [self.screp[:], self.hb[1][:].rearrange("p (k n) -> p k n", n=128)]
        skeys = ["screp", "hb1"]
        for ci, cond_ap in enumerate([dr["cctx"], dr["csmp"]]):
            A("sp", lambda e, cond_ap=cond_ap: e.dma_start(out=self.cc[:], in_=cond_ap.rearrange("(k p) -> p k", p=128), allow_slow_non_contiguous=True),
              w=["cc"], dk="cc")
            A("act", lambda e: e.activation(out=self.sc[:], in_=self.cc[:], func=AF.Silu), r=["cc"], w=["sc"])
            A("dve", lambda e, ci=ci: e.tensor_copy(out=screps[ci], in_=self.sc[:].unsqueeze(2).to_broadcast([128, 8, 128])), r=["sc"], w=[skeys[ci]])
        def rowdst(ci, n):
            c0 = n * 256
            if ci == 0:
                return self.mrep[:, c0:c0 + 256], "mrep"
            t, k = [(self.xn[0], "xn0"), (self.xn[1], "xn1"), (self.xt[0], "xt0")][c0 // 1024]
            return t[:, c0 % 1024:c0 % 1024 + 256], k
        for layer in range(2):
            for n in range(12):
                wv, wk = self.wload(dr["wmod"][layer][:, n * 256:(n + 1) * 256], 8, 256)
                if n % 2 == 0:
                    nn = n // 2
                    A("sp", lambda e, nn=nn, layer=layer: e.dma_start(out=self.tmp[3][:], in_=dr["bmod"][layer][nn * 512:(nn + 1) * 512].partition_broadcast(128)),
                      w=["tmp3"], dk="brep")
                for ci in range(2):
                    pi = self.nxt("pf01", 2)
                    p = self.pf[pi]
                    for kt in range(8):
                        A("pe", lambda e, kt=kt, p=p, wv=wv, ci=ci: e.matmul(p[:, 0:256], lhsT=screps[ci][:, kt, :], rhs=wv[:, kt, :], start=(kt == 0), stop=(kt == 7)),
                          r=[skeys[ci], wk], w=[f"pf{pi}"])
                    dst, dk_ = rowdst(ci, n)
                    A("dve", lambda e, n=n, p=p, dst=dst: e.tensor_tensor(out=dst, in0=p[:, 0:256], in1=self.tmp[3][:, (n % 2) * 256:(n % 2) * 256 + 256], op=ALU.add),
                      r=[f"pf{pi}", "tmp3"], w=[dk_])
            A("sp", lambda e, layer=layer: e.dma_start(out=dr["s_mod"][layer, 0:1, :], in_=self.mrep[0:1, :]), r=["mrep"], w=["s_mod"], dk="smw0")
            for j, (t, k) in enumerate([(self.xn[0], "xn0"), (self.xn[1], "xn1"), (self.xt[0], "xt0")]):
                A("sp", lambda e, layer=layer, j=j, t=t: e.dma_start(out=dr["s_mod"][layer, 1:2, j * 1024:(j + 1) * 1024], in_=t[0:1, :]),
                  r=[k], w=["s_mod"], dk=f"smw{j + 1}")

    def load_mod(self, layer, ci):
        A, dr = self.A, self.dr
        A("sp", lambda e: e.dma_start(out=self.mrep[:], in_=dr["s_mod"][layer, ci].partition_broadcast(128)), r=["s_mod"], w=["mrep"], dk="mrepld")
        A("dve", lambda e: e.tensor_scalar_add(out=self.mrep[:, 1024:2048], in0=self.mrep[:, 1024:2048], scalar1=1.0), r=["mrep"], w=["mrep"])


    def ln_stats_t(self, src, skey, tt, S):
        A = self.A
        k = f"ln{S}_{tt}"
        stats, mv = self.statsT[:, S, tt], self.mvT[:, S, tt]
        rstd, nmr = self.rstdT[:, S, tt:tt + 1], self.nmrT[:, S, tt:tt + 1]
        for c in range(2):
            A("dve", lambda e, c=c: e.bn_stats(out=stats[:, c, :], in_=src[:, c * 512:(c + 1) * 512]), r=[skey], w=[k + "s"])
        A("dve", lambda e: e.bn_aggr(out=mv, in_=stats), r=[k + "s"], w=[k + "m"])
        A("act", lambda e: e.activation(out=rstd, in_=mv[:, 1:2], func=AF.Sqrt, bias=LN_EPS, scale=1.0), r=[k + "m"], w=[k + "r"])
        A("dve", lambda e: e.reciprocal(out=rstd, in_=rstd), r=[k + "r"], w=[k + "r"])
        A("dve", lambda e: e.tensor_scalar(out=nmr, in0=mv[:, 0:1], scalar1=-1.0, scalar2=rstd, op0=ALU.mult, op1=ALU.mult),
          r=[k + "m", k + "r"], w=[k + "n"])
        return rstd, nmr, [k + "r", k + "n"]

    def ln_mod_tiles(self, tiles, st=None):
        A = self.A
        if st is None:
            st = [self.ln_stats_t(src, key, tt, 1) for (src, key, tt) in tiles]
        n = len(tiles)
        info = {}

        def norm(i):
            (src, key, tt), (rstd, nmr, lk) = tiles[i], st[i]
            xb = self.nxt("xn", 2)
            xn = self.xn[xb]
            A("act", lambda e, xn=xn, src=src, nmr=nmr, rstd=rstd: e.activation(out=xn[:], in_=src, func=AF.Identity, bias=nmr, scale=rstd),
              r=[key] + lk, w=[f"xn{xb}"])
            info[i] = (xn, xb)
        norm(0)
        for i in range(n):
            (src, key, tt) = tiles[i]
            xn, xb = info[i]
            b = self.nxt("hb", 2)
            hb = self.hb[b]
            A("dve", lambda e, xn=xn: e.tensor_tensor(out=xn[:], in0=xn[:], in1=self.mrep[:, 1024:2048], op=ALU.mult), r=[f"xn{xb}", "mrep"], w=[f"xn{xb}"])
            A("dve", lambda e, xn=xn, hb=hb: e.tensor_tensor(out=hb[:], in0=xn[:], in1=self.mrep[:, 0:1024], op=ALU.add), r=[f"xn{xb}", "mrep"], w=[f"hb{b}"])
            if i + 1 < n:
                norm(i + 1)
            pi = self.nxt("pt", 2)
            pt = self.pt[pi]
            for kt in range(8):
                A("pe", lambda e, kt=kt, pt=pt, hb=hb: e.transpose(pt[:, kt, :], hb[:, kt * 128:(kt + 1) * 128], self.identb[:]),
                  r=[f"hb{b}", "identb"], w=[f"pt{pi}"])
            A("act", lambda e, pt=pt, tt=tt: e.activation(out=self.hT[:, :, tt * 128:(tt + 1) * 128], in_=pt[:], func=AF.Copy), r=[f"pt{pi}"], w=["hT"])

    def fence(self, reads, writes):
        if not hasattr(self, "_dummy"):
            raise RuntimeError("alloc dummy first")
        self.A("dve", lambda e: e.memset(self._dummy[:], 0.0), r=list(reads), w=list(writes) + ["_dummy"])

    REG = {
        "AB": ["A", "gaA", "attA", "mixA"] + [f"Z{t}" for t in range(8)] + [ "ropeC", "ropeS", "qraw", "gz0", "gz1", "vh", "qT", "kT"] + [f"PT{i}" for i in range(4)] + [f"B{k}" for k in range(8)],
        "C": ["CX", "Ctab"] + [f"C{k}" for k in range(16)],
        "T": ["Wb0", "Wb1", "MQb0", "MQb1", "MQb0q", "MQb1q", "Sb0", "Sb1", "qT", "kT", "cs256", "ucs", "uT0", "uT1"],
    }

    def rfence(self, reg):
        ks = self.REG[reg]
        self.fence(ks, ks)

    def alloc2(self, es):
        self._dummy = es.enter_context(self.nc.sbuf_tensor("m_dummy", [128, 2], F32))

    def s5_path(self, grp):
        A, dr = self.A, self.dr
        isS = (grp == "S")
        ns = 1 if isS else 4
        NS = 128 // ns
        ua = self.AB[:, 0:8192].rearrange("p (g s m) -> p g s m", s=8, m=16)
        ga_t = self.AB[:, 0:8192].rearrange("p (j n) -> p j n", n=1024)
        ug = self.AB[:, 8192:16384].rearrange("p (g c) -> p g c", c=128)
        gaT = self.AB[:, 8192:16384].rearrange("p (k t) -> p k t", t=1024)
        Xp = self.C[:].bitcast(F32).rearrange("p (r g c) -> p r g c", r=2, g=32)
        Wb = [self.T[:, i * 1024:(i + 1) * 1024].rearrange("p (g r n) -> p g r n", g=4, r=2) for i in range(2)]
        MQb = [self.T[:, 2048 + i * 1536:2048 + (i + 1) * 1536] for i in range(2)]
        Sb = self.T[:, 5120:5120 + 8192].rearrange("p (r g c) -> p r g c", r=2, g=32)
        ALLB = [f"B{k}" for k in range(8)]
        pre = [self.wload(dr["win_e"][:, cb * 256:(cb + 1) * 256], 8, 256) for cb in range(4)]
        self.rfence("AB"); self.rfence("C"); self.rfence("T")
        for cb in range(4):
            wv, wk = pre[cb]
            for s in range(8):
                pi = self.nxt("pf01", 2)
                p = self.pf[pi]
                for kt in range(8):
                    A("pe", lambda e, kt=kt, s=s, p=p, wv=wv: e.matmul(p[:, 0:256], lhsT=self.hT[:, kt, s:1024:8], rhs=wv[:, kt, :],
                                                                      start=(kt == 0), stop=(kt == 7)), r=["hT", wk], w=[f"pf{pi}"])
                eng = "act" if s % 2 == 0 else "dve"
                if eng == "act":
                    A("act", lambda e, s=s, cb=cb, p=p: e.activation(out=ua[:, cb * 16:(cb + 1) * 16, s, :],
                                                                     in_=p[:, 0:256].rearrange("q (g m) -> q g m", m=16), func=AF.Copy),
                      r=[f"pf{pi}"], w=["A"])
                else:
                    A("dve", lambda e, s=s, cb=cb, p=p: e.tensor_copy(out=ua[:, cb * 16:(cb + 1) * 16, s, :],
                                                                      in_=p[:, 0:256].rearrange("q (g m) -> q g m", m=16)),
                      r=[f"pf{pi}"], w=["A"])
        for blk in range(8):
            pi = self.nxt("pt", 2)
            pt = self.pt[pi]
            for gl in range(8):
                g = 8 * blk + gl
                A("pe", lambda e, gl=gl, g=g, pt=pt: e.transpose(pt[:, gl, :], ua[:, g, :, :].rearrange("q s m -> q (s m)"), self.identb[:]),
                  r=["A", "identb"], w=[f"pt{pi}"])
            A("act", lambda e, blk=blk, pt=pt: e.activation(out=ug[:, 8 * blk:8 * blk + 8, :], in_=pt[:], func=AF.Copy),
              r=[f"pt{pi}"], w=[f"B{blk}"])
        self.fence(["A"], ["gaA"])
        for half in range(2):
            g0 = 32 * half
            self.fence([f"C{k}" for k in range(16)], ["CX"])
            for blk in range(8):
                bi = self.nxt("Wb", 2)
                A("sp", lambda e, bi=bi, blk=blk, g0=g0: e.dma_start(out=Wb[bi], in_=dr["s_opW"][:, g0 + 4 * blk:g0 + 4 * blk + 4]),
                  w=[f"Wb{bi}"], dk=f"Wb{bi}")
                for ri in range(2):
                    pi = 2 + self.nxt("pf23", 2)
                    p = self.pf[pi]
                    for gl in range(4):
                        g = g0 + 4 * blk + gl
                        A("pe", lambda e, p=p, gl=gl, g=g, ri=ri, bi=bi: e.matmul(p[:, gl * 128:(gl + 1) * 128], lhsT=Wb[bi][:, gl, ri, :],
                                                                                  rhs=ug[:, g, :], start=True, stop=True),
                          r=[f"Wb{bi}", f"B{g // 8}"], w=[f"pf{pi}"])
                    A("act", lambda e, p=p, ri=ri, blk=blk: e.activation(out=Xp[0:64, ri, 4 * blk:4 * blk + 4, :],
                                                                         in_=p[0:64, :].rearrange("q (g c) -> q g c", c=128), func=AF.Copy),
                      r=[f"pf{pi}"], w=["CX"])
                    A("act", lambda e, p=p, ri=ri, blk=blk: e.activation(
                        out=Xp[64:128, ri, 4 * blk:4 * blk + 4, :].rearrange("q g (s c) -> q g s c", s=ns),
                        in_=p[64:128, :].rearrange("q (g s c) -> q g s c", g=4, s=ns)[:, :, :, ::-1], func=AF.Copy),
                      r=[f"pf{pi}"], w=["CX"])
            a8r_b = self.A8[:, 0, g0:g0 + 32].unsqueeze(1).unsqueeze(3).to_broadcast([128, 2, 32, ns])
            A("dve", lambda e, a8r_b=a8r_b: e.tensor_copy(out=self.a8[:, 0, :, :, 0:ns], in_=a8r_b), r=["A8"], w=["a8"])
            a8i_b = self.A8[:, 1, g0:g0 + 32].unsqueeze(2).to_broadcast([128, 32, ns])
            A("dve", lambda e, a8i_b=a8i_b: e.tensor_copy(out=self.a8[:, 1, 0, :, 0:ns], in_=a8i_b), r=["A8"], w=["a8"])
            A("dve", lambda e: e.tensor_scalar_mul(out=self.a8[:, 1, 1, :, 0:ns], in0=self.a8[:, 1, 0, :, 0:ns], scalar1=-1.0), r=["a8"], w=["a8"])
            if isS:
                for (ri, nm) in [(0, "st_re"), (1, "st_im")]:
                    for d in range(2):
                        A("sp", lambda e, ri=ri, nm=nm, d=d, g0=g0: e.dma_start(
                            out=self.st[64 * d:64 * d + 64, ri, :, 0:1],
                            in_=dr[nm][d, g0:g0 + 32, :].rearrange("g (p o) -> p g o", o=1), allow_slow_non_contiguous=True),
                          w=["st"], dk=f"st{ri}{d}")
            else:
                A("dve", lambda e: e.memset(self.st[:], 0.0), w=["st"])
            def cs(k):
                return slice(k, 128, NS) if ns > 1 else slice(k, k + 1)

            def csb(k):
                return cs(NS - 1 - k)
            stb = [self.st, self.st2]
            zz = self.zz[:, :, :, 0:ns]
            zsw = self.zz[:, ::-1, :, 0:ns]
            q12 = self.q12[:, :, :, :, 0:ns]
            q1 = self.q12[:, 0, :, :, 0:ns]
            q2s = self.q12[:, 1, ::-1, :, 0:ns]
            a8v = self.a8[:, :, :, :, 0:ns]
            zz2 = self.zz[:, :, :, 0:ns].unsqueeze(1).to_broadcast([128, 2, 2, 32, ns])

            def store(k, sbuf, skey):
                cf, cb_ = cs(k), csb(k)
                A("act", lambda e, cf=cf, sbuf=sbuf: e.activation(out=Sb[0:64, :, :, cf], in_=sbuf[0:64, :, :, 0:ns], func=AF.Copy), r=[skey], w=["Sb0"])
                A("pool", lambda e, cb_=cb_, sbuf=sbuf: e.tensor_copy(out=Sb[64:128, :, :, cb_], in_=sbuf[64:128, :, :, 0:ns]), r=[skey], w=["Sb1"])
            store(0, self.st, "st")
            A("dve", lambda e, c0=cs(0): e.tensor_tensor(out=zz, in0=self.st[:, :, :, 0:ns], in1=Xp[:, :, :, c0], op=ALU.add), r=["st", "CX"], w=["zz"])
            cur, curk = self.st, "st"
            for k in range(NS):
                nxt_, nxtk = (self.st2, "st2") if cur is self.st else (self.st, "st")
                if k == NS - 1:
                    nxt_, nxtk = self.st, "st"
                    if cur is self.st:
                        pass
                A("dve", lambda e: e.tensor_tensor(out=q12, in0=a8v, in1=zz2, op=ALU.mult), r=["a8", "zz"], w=["q12"])
                A("dve", lambda e, nxt_=nxt_: e.tensor_tensor(out=nxt_[:, :, :, 0:ns], in0=q1, in1=q2s, op=ALU.add), r=["q12"], w=[nxtk])
                cur, curk = nxt_, nxtk
                if k < NS - 1:
                    store(k + 1, cur, curk)
                    A("dve", lambda e, cn=cs(k + 1), cur=cur: e.tensor_tensor(out=zz, in0=cur[:, :, :, 0:ns], in1=Xp[:, :, :, cn], op=ALU.add), r=[curk, "CX"], w=["zz"])
            if not isS:
                A("act", lambda e, g0=g0: e.activation(out=self.fin[:, :, :, g0:g0 + 32].rearrange("q r s g -> q r g s"),
                                                     in_=self.st[:, :, :, 0:4], func=AF.Copy), r=["st"], w=["fin"])
            ga = ga_t
            for blk in range(8):
                bi = self.nxt("MQb", 2)
                Mv = MQb[bi][:, 0:512].rearrange("p (g n) -> p g n", n=128)
                Qv = MQb[bi][:, 512:1536].rearrange("p (g r n) -> p g r n", g=4, r=2)
                gq = g0 + 4 * blk
                A("sp", lambda e, Mv=Mv, gq=gq: e.dma_start(out=Mv, in_=dr["s_opM"][:, gq:gq + 4]), w=[f"MQb{bi}"], dk=f"Mb{bi}")
                A("sp", lambda e, Qv=Qv, gq=gq: e.dma_start(out=Qv, in_=dr["s_opQ"][:, gq:gq + 4]), w=[f"MQb{bi}q"], dk=f"Qb{bi}")
                pi = self.nxt("pf01", 2)
                p = self.pf[pi]
                pv = p[:].rearrange("q (g j n) -> q g j n", j=8, n=16)
                for gl in range(4):
                    g = gq + gl
                    gh = g - g0
                    A("pe", lambda e, p=p, gl=gl, g=g, Mv=Mv: e.matmul(p[:, gl * 128:(gl + 1) * 128], lhsT=ug[:, g, :],
                                                                        rhs=Mv[:, gl, :], start=True, stop=False),
                      r=[f"B{g // 8}", f"MQb{bi}"], w=[f"pf{pi}"])
                    for ri in range(2):
                        A("pe", lambda e, p=p, gl=gl, gh=gh, ri=ri, Qv=Qv: e.matmul(p[:, gl * 128:(gl + 1) * 128], lhsT=Sb[:, ri, gh, :],
                                                                                     rhs=Qv[:, gl, ri, :],
                                                                                     start=False, stop=(ri == 1)),
                          r=["Sb0", "Sb1", f"MQb{bi}q"], w=[f"pf{pi}"])
                t0, t1 = self.tmp[0], self.tmp[1]
                A("act", lambda e, p=p: e.activation(out=t0[:], in_=p[:], func=AF.Square), r=[f"pf{pi}"], w=["tmp0"])
                A("dve", lambda e: e.tensor_scalar(out=t0[:], in0=t0[:], scalar1=0.044715, scalar2=1.0, op0=ALU.mult, op1=ALU.add), r=["tmp0"], w=["tmp0"])
                A("dve", lambda e, p=p: e.tensor_tensor(out=t0[:], in0=t0[:], in1=p[:], op=ALU.mult), r=["tmp0", f"pf{pi}"], w=["tmp0"])
                A("act", lambda e: e.activation(out=t1[:], in_=t0[:], func=AF.Sigmoid, scale=GELU_C), r=["tmp0"], w=["tmp1"])
                gav = ga[:, :, gq * 16:gq * 16 + 64].rearrange("q j (g n) -> q g j n", n=16)
                A("dve", lambda e, pv=pv, gav=gav: e.tensor_tensor(out=gav, in0=t1[:].rearrange("q (g j n) -> q g j n", j=8, n=16), in1=pv, op=ALU.mult),
                  r=["tmp1", f"pf{pi}"], w=["gaA"])
                if blk % 2 == 1:
                    kt = gq // 8
                    pti = self.nxt("pt", 2)
                    pt = self.pt[pti]
                    for j in range(8):
                        A("pe", lambda e, pt=pt, j=j, kt=kt: e.transpose(pt[:, j, :], ga[:, j, kt * 128:(kt + 1) * 128], self.identb[:]),
                          r=["gaA", "identb"], w=[f"pt{pti}"])
                    A("act", lambda e, pt=pt, kt=kt: e.activation(out=gaT[:, kt, :].rearrange("q (c j) -> q j c", j=8), in_=pt[:], func=AF.Copy),
                      r=[f"pt{pti}"], w=[f"B{kt}"])
            self.fence(["CX"], [f"C{k}" for k in range(16)])
        if not isS:
            for (ri, nm) in [(0, "fre"), (1, "fim")]:
                for sp2 in range(2):
                    pj = 2 + self.nxt("pf23", 2)
                    p2 = self.pf[pj]
                    A("pe", lambda e, p2=p2, ri=ri, sp2=sp2: e.transpose(p2[:, 0:128], self.fin[:, ri, 2 * sp2:2 * sp2 + 2, :].rearrange("q s g -> q (s g)"),
                                                                       self.identf[:]), r=["fin", "identf"], w=[f"pf{pj}"])
                    oi = self.nxt("ost", 2)
                    ost = self.ost[oi]
                    A("dve", lambda e, p2=p2, ost=ost: e.tensor_copy(out=ost[:, 0:128], in_=p2[:, 0:128]), r=[f"pf{pj}"], w=[f"ost{oi}"])
                    for sl in range(2):
                        sq = 2 * sp2 + sl
                        tk = A("sp", lambda e, ost=ost, sl=sl, sq=sq, nm=nm: e.dma_start(
                            out=dr[nm][sq].rearrange("d g p -> g d p"), in_=ost[64 * sl:64 * sl + 64, 0:128].rearrange("q (d p) -> q d p", p=64)),
                            r=[f"ost{oi}"], w=[nm], dk=f"ostd{oi}")
                        self.out_tokens.append(tk)

    def glu_path(self):
        A, dr = self.A, self.dr
        gaT = self.AB[:, 8192:16384].rearrange("p (k t) -> p k t", t=1024)
        yT = self.C[:].rearrange("p (k t) -> p k t", t=1024)
        ALLB = [f"B{k}" for k in range(8)]
        for ot in range(8):
            wg, wgk = self.wload(dr["wglu"][:, ot * 128:(ot + 1) * 128], 8, 128)
            wz, wzk = self.wload(dr["win_e"][:, 1024 + ot * 128:1024 + (ot + 1) * 128], 8, 128)
            for tb in range(2):
                ts = slice(tb * 512, (tb + 1) * 512)
                pi = self.nxt("pf01", 2)
                p = self.pf[pi]
                for kt in range(8):
                    A("pe", lambda e, p=p, kt=kt, wg=wg, ts=ts: e.matmul(p[:], lhsT=wg[:, kt, :], rhs=gaT[:, kt, ts], start=(kt == 0), stop=(kt == 7)),
                      r=[wgk] + ALLB, w=[f"pf{pi}"])
                pj = 2 + self.nxt("pf23", 2)
                p2 = self.pf[pj]
                for kt in range(8):
                    A("pe", lambda e, p2=p2, kt=kt, wz=wz, ts=ts: e.matmul(p2[:], lhsT=wz[:, kt, :], rhs=self.hT[:, kt, ts], start=(kt == 0), stop=(kt == 7)),
                      r=[wzk, "hT"], w=[f"pf{pj}"])
                t0, t1 = self.tmp[0], self.tmp[1]
                A("act", lambda e, p=p, ot=ot: e.activation(out=t0[:], in_=p[:], func=AF.Sigmoid, bias=self.bglu[:, ot:ot + 1], scale=1.0),
                  r=[f"pf{pi}", "bglu"], w=["tmp0"])
                A("act", lambda e, p2=p2: e.activation(out=t1[:], in_=p2[:], func=AF.Silu), r=[f"pf{pj}"], w=["tmp1"])
                A("dve", lambda e, ot=ot, ts=ts: e.tensor_tensor(out=t0[:], in0=t0[:], in1=gaT[:, ot, ts], op=ALU.mult), r=["tmp0", f"B{ot}"], w=["tmp0"])
                A("dve", lambda e, ot=ot, ts=ts: e.tensor_tensor(out=yT[:, ot, ts], in0=t0[:], in1=t1[:], op=ALU.mult), r=["tmp0", "tmp1"], w=[f"C{ot}"])

    def attn_path(self, grp):
        A, dr = self.A, self.dr
        isS = (grp == "S")
        yT = self.C[:].rearrange("p (k t) -> p k t", t=1024)
        TB = 13056
        qTm = [self.AB[:, 8192:9216], self.AB[:, 9216:10240]]
        kT = self.AB[:, 10240:10240 + 1280]
        vh = self.AB[:, 6144:6144 + 1280].rearrange("p (t e) -> p t e", e=128)
        gzb = [self.AB[:, 0:1024], self.AB[:, 11520:12544]]
        self._blk = 0
        self._pend = None
        qraw = self.AB[:, 1024:2048]
        PT = [self.AB[:, 2048 + i * 512:2048 + (i + 1) * 512] for i in range(4)]
        ropeC = self.AB[:, 4096:5120]
        ropeS = self.AB[:, 5120:6144]
        cst = self.xt[0]
        oacc = [self.tmp[2], self.tmp[3]]
        self.rfence("AB")
        A("dve", lambda e: e.memset(qTm[0][64:128, :], 0.0), w=["qT"])
        A("dve", lambda e: e.memset(qTm[1][0:64, :], 0.0), w=["qT"])
        nkt = 10 if isS else 2
        koff = 256 if isS else 0
        if isS:
            A("pool", lambda e: e.dma_start(out=ropeC, in_=dr["ropeC"]), w=["ropeC"], dk="ropeC")
            A("pool", lambda e: e.dma_start(out=ropeS, in_=dr["ropeS"]), w=["ropeS"], dk="ropeS")
        seqs = [(0, 1024)] if isS else [(i * 256, 256) for i in range(4)]
        for h in range(8):
            gz, gzk = gzb[h % 2], f"gz{h % 2}"
            wq, wqk = self.wload(dr["win_e"][:, 2048 + h * 128:2048 + (h + 1) * 128], 8, 128)
            wk_, wkk = self.wload(dr["win_e"][:, 3072 + h * 128:3072 + (h + 1) * 128], 8, 128)
            wv_, wvk = self.wload(dr["win_e"][:, 4096 + h * 128:4096 + (h + 1) * 128], 8, 128)
            wzb, wzbk = self.wload(dr["win_e"][:, 5120 + h * 128:5120 + (h + 1) * 128], 8, 128)
            for (wv, wkey, dst, dkey, off, kind) in [(wq, wqk, None, "qT", 0, "q"), (wk_, wkk, kT, "kT", koff, "k"), (wzb, wzbk, gz, gzk, 0, "z")]:
                for tb in range(2):
                    ts = slice(tb * 512, (tb + 1) * 512)
                    od = slice(off + tb * 512, off + (tb + 1) * 512)
                    pi = self.nxt("pf01", 2)
                    p = self.pf[pi]
                    for kt in range(8):
                        A("pe", lambda e, p=p, kt=kt, wv=wv, ts=ts: e.matmul(p[:], lhsT=wv[:, kt, :], rhs=self.hT[:, kt, ts], start=(kt == 0), stop=(kt == 7)),
                          r=[wkey, "hT"], w=[f"pf{pi}"])
                    if kind == "z":
                        A("act", lambda e, p=p, dst=dst, od=od: e.activation(out=dst[:, od], in_=p[:], func=AF.Silu), r=[f"pf{pi}"], w=[dkey, "attA"])
                    elif not isS and kind == "q":
                        for mm in range(2):
                            A("act", lambda e, p=p, od=od, mm=mm: e.activation(out=qTm[mm][64 * mm:64 * mm + 64, od], in_=p[64 * mm:64 * mm + 64, :], func=AF.Copy),
                              r=[f"pf{pi}"], w=[dkey])
                    elif not isS:
                        A("act", lambda e, p=p, dst=dst, od=od: e.activation(out=dst[:, od], in_=p[:], func=AF.Copy), r=[f"pf{pi}"], w=[dkey])
                    else:
                        A("act", lambda e, p=p, ts=ts: e.activation(out=qraw[:, 0:512], in_=p[:], func=AF.Copy), r=[f"pf{pi}"], w=["qraw", "attA"])
                        pj = 2 + self.nxt("pf23", 2)
                        p2 = self.pf[pj]
                        A("pe", lambda e, p2=p2: e.matmul(p2[:], lhsT=self.pswb[:], rhs=qraw[:, 0:512], start=True, stop=True),
                          r=["pswb", "qraw"], w=[f"pf{pj}"])
                        t0, t1 = self.tmp[0], self.tmp[1]
                        A("dve", lambda e, ts=ts: e.tensor_tensor(out=t0[:], in0=qraw[:, 0:512], in1=ropeC[:, ts], op=ALU.mult), r=["qraw", "ropeC"], w=["tmp0"])
                        A("dve", lambda e, p2=p2, ts=ts: e.tensor_tensor(out=t1[:], in0=p2[:], in1=ropeS[:, ts], op=ALU.mult), r=[f"pf{pj}", "ropeS"], w=["tmp1"])
                        if kind == "q":
                            for mm in range(2):
                                A("dve", lambda e, od=od, mm=mm: e.tensor_tensor(out=qTm[mm][64 * mm:64 * mm + 64, od], in0=t0[64 * mm:64 * mm + 64, :],
                                                                                in1=t1[64 * mm:64 * mm + 64, :], op=ALU.add), r=["tmp0", "tmp1"], w=[dkey])
                        else:
                            A("dve", lambda e, dst=dst, od=od: e.tensor_tensor(out=dst[:, od], in0=t0[:], in1=t1[:], op=ALU.add), r=["tmp0", "tmp1"], w=[dkey])
            if getattr(self, "att_lvl", 9) < 1:
                continue
            for half in range(2):
                pi = self.nxt("pf01", 2)
                p = self.pf[pi]
                for t4 in range(4):
                    tt = half * 4 + t4
                    for kt in range(8):
                        A("pe", lambda e, p=p, t4=t4, tt=tt, kt=kt, wv_=wv_: e.matmul(p[:, t4 * 128:(t4 + 1) * 128], lhsT=self.hT[:, kt, tt * 128:(tt + 1) * 128],
                                                                                     rhs=wv_[:, kt, :], start=(kt == 0), stop=(kt == 7)),
                          r=[wvk, "hT"], w=[f"pf{pi}"])
                vo = (2 if isS else 0) + half * 4
                A("act", lambda e, p=p, vo=vo: e.activation(out=vh[:, vo:vo + 4, :], in_=p[:].rearrange("q (t e) -> q t e", e=128), func=AF.Copy),
                  r=[f"pf{pi}", "attA"], w=["vh"])
                if not isS:
                    oi = self.nxt("ost", 2)
                    ost = self.ost[oi]
                    A("dve", lambda e, p=p, ost=ost: e.tensor_copy(out=ost[:, 0:512], in_=p[:]), r=[f"pf{pi}", "vh"], w=[f"ost{oi}"])
                    tk = A("sp", lambda e, ost=ost, half=half, h=h: e.dma_start(
                        out=dr["nv"][half * 512:(half + 1) * 512, h * 128:(h + 1) * 128].rearrange("(t p) e -> p t e", p=128),
                        in_=ost[:, 0:512].rearrange("q (t e) -> q t e", e=128)), r=[f"ost{oi}"], w=["nv"], dk=f"ostd{oi}")
                    self.out_tokens.append(tk)
                    pj = 2 + self.nxt("pf23", 2)
                    p2 = self.pf[pj]
                    for t4 in range(4):
                        tt = half * 4 + t4
                        for kt in range(8):
                            A("pe", lambda e, p2=p2, t4=t4, tt=tt, kt=kt, wk_=wk_: e.matmul(p2[:, t4 * 128:(t4 + 1) * 128], lhsT=self.hT[:, kt, tt * 128:(tt + 1) * 128],
                                                                                           rhs=wk_[:, kt, :], start=(kt == 0), stop=(kt == 7)),
                              r=[wkk, "hT"], w=[f"pf{pj}"])
                    A("dve", lambda e, p2=p2, ost=ost: e.tensor_copy(out=ost[:, 512:1024], in_=p2[:]), r=[f"pf{pj}"], w=[f"ost{oi}"])
                    tk = A("sp", lambda e, ost=ost, half=half, h=h: e.dma_start(
                        out=dr["nk"][half * 512:(half + 1) * 512, h * 128:(h + 1) * 128].rearrange("(t p) e -> p t e", p=128),
                        in_=ost[:, 512:1024].rearrange("q (t e) -> q t e", e=128)), r=[f"ost{oi}"], w=["nk"], dk=f"ostk{oi}")
                    self.out_tokens.append(tk)
            if isS:
                A("sp", lambda e, h=h: e.dma_start(out=cst[:, 0:256].rearrange("q (t e) -> q t e", e=128),
                                                   in_=dr["cache_k"][:, h].rearrange("(t p) m d -> p t (m d)", p=128)), w=["xt0"], dk="xt0")
                A("sp", lambda e, h=h: e.dma_start(out=cst[:, 256:512].rearrange("q (t e) -> q t e", e=128),
                                                   in_=dr["cache_v"][:, h].rearrange("(t p) e -> p t e", p=128)), w=["xt0"], dk="xt0b")
                A("dve", lambda e: e.tensor_copy(out=self.hb[0][:, 0:256], in_=cst[:, 0:256]), r=["xt0"], w=["hb0"])
                A("dve", lambda e: e.tensor_copy(out=vh[:, 0:2, :], in_=cst[:, 256:512].rearrange("q (t e) -> q t e", e=128)), r=["xt0", "attA"], w=["vh"])
                pti = self.nxt("pt", 2)
                pt = self.pt[pti]
                for t in range(2):
                    A("pe", lambda e, pt=pt, t=t: e.transpose(pt[:, t, :], self.hb[0][:, t * 128:(t + 1) * 128], self.identb[:]),
                      r=["hb0", "identb"], w=[f"pt{pti}"])
                A("act", lambda e, pt=pt: e.activation(out=kT[:, 0:256], in_=pt[:, 0:2, :].rearrange("q t c -> q (t c)"), func=AF.Copy),
                  r=[f"pt{pti}"], w=["kT"])
            if getattr(self, "att_lvl", 9) < 2:
                continue
            blocks = []
            if isS:
                for qb in range(2):
                    blocks.append((qb * 512, [(0, 512, [(kb * 128, kb) for kb in range(10)])]))
            else:
                for bb in range(2):
                    subs = []
                    for sl in range(2):
                        s0 = (2 * bb + sl) * 256
                        subs.append((sl * 256, 256, [(s0 + kb * 128, s0 // 128 + kb) for kb in range(2)]))
                    blocks.append((bb * 512, subs))
            accs = [(self.pf[4], "pf4", self.pf[5], "pf5"),
                    (self.pt[0][:].bitcast(F32).rearrange("q a b -> q (a b)"), "pt0", self.pt[1][:].bitcast(F32).rearrange("q a b -> q (a b)"), "pt1")]
            osets = [((self.tmp[2], "tmp2"), (self.tmp[3], "tmp3")),
                     ((self.xt[1][:, 0:512], "xt1"), (self.xt[1][:, 512:1024], "xt1"))]
            nq = 512

            def stageA(bi, q0, subs):
                oset = osets[bi % 2]
                work = []
                for m in range(2):
                    for (qoff, nqs, keys) in subs:
                        for ki, (kbase, vt) in enumerate(keys):
                            work.append((m, qoff, nqs, kbase, vt, ki == 0, ki == len(keys) - 1))
                last_of_map = {m: max(i for i, w in enumerate(work) if w[0] == m) for m in range(2)}
                sc_bank = {}

                def issue_scores(wi):
                    m, qoff, nqs, kbase, vt, first, last = work[wi]
                    pj = self.nxt("pf03", 4)
                    p2 = self.pf[pj]
                    sc_bank[wi] = (p2, pj)
                    A("pe", lambda e, p2=p2, kbase=kbase, m=m, qoff=qoff, nqs=nqs: e.matmul(
                        p2[:, 0:nqs], lhsT=kT[:, kbase:kbase + 128], rhs=qTm[m][:, q0 + qoff:q0 + qoff + nqs], start=True, stop=True),
                        r=["kT", "qT"], w=[f"pf{pj}"])
                for wi in range(min(3, len(work))):
                    issue_scores(wi)
                for wi, (m, qoff, nqs, kbase, vt, first, last) in enumerate(work):
                    if wi + 3 < len(work):
                        issue_scores(wi + 3)
                    po, pok, pz, pzk = accs[m]
                    p2, pj = sc_bank[wi]
                    pti = self.nxt("PT", 4)
                    A("act", lambda e, p2=p2, pti=pti, nqs=nqs: e.activation(out=PT[pti][:, 0:nqs], in_=p2[:, 0:nqs], func=AF.Exp, scale=0.125),
                      r=[f"pf{pj}"], w=[f"PT{pti}"])
                    A("pe", lambda e, po=po, vt=vt, pti=pti, qoff=qoff, nqs=nqs, first=first, last=last: e.matmul(
                        po[:, qoff:qoff + nqs], lhsT=vh[:, vt, :], rhs=PT[pti][:, 0:nqs], start=first, stop=last), r=["vh", f"PT{pti}"], w=[pok])
                    A("pe", lambda e, pz=pz, pti=pti, qoff=qoff, nqs=nqs, first=first, last=last: e.matmul(
                        pz[:, qoff:qoff + nqs], lhsT=self.onesb[:], rhs=PT[pti][:, 0:nqs], start=first, stop=last), r=["onesb", f"PT{pti}"], w=[pzk])
                    if wi == last_of_map[m]:
                        te = self.xn[m][:, 0:512]
                        tek = f"xn{m}"
                        oa, oak = oset[m]
                        A("act", lambda e, pz=pz, te=te: e.activation(out=te[:, 0:nq], in_=pz[:, 0:nq], func=AF.Ln), r=[pzk], w=[tek])
                        A("act", lambda e, te=te: e.activation(out=te[:, 0:nq], in_=te[:, 0:nq], func=AF.Exp, scale=-1.0), r=[tek], w=[tek])
                        A("dve", lambda e, po=po, te=te, oa=oa: e.tensor_tensor(out=oa[:, 0:nq], in0=po[:, 0:nq], in1=te[:, 0:nq], op=ALU.mult),
                          r=[pok, tek], w=[oak])

            def stageB(bi, q0, h=h, gz=gz, gzk=gzk):
                qs = slice(q0, q0 + 512)
                (o0, k0), (o1, k1) = osets[bi % 2]
                t0, t1 = self.tmp[0], self.tmp[1]
                A("dve", lambda e: e.scalar_tensor_tensor(out=o0[:, 0:nq], in0=o1[:, 0:nq], scalar=self.nlam[:, 0:1], in1=o0[:, 0:nq],
                                                          op0=ALU.mult, op1=ALU.add), r=[k0, k1, "nlam"], w=[k0])
                A("dve", lambda e: e.tensor_tensor(out=t0[:, 0:nq], in0=o0[:, 0:nq], in1=o0[:, 0:nq], op=ALU.mult), r=[k0], w=["tmp0"])
                pj = self.nxt("pf03", 4)
                p2 = self.pf[pj]
                A("pe", lambda e, p2=p2: e.matmul(p2[:, 0:nq], lhsT=self.onesf[:], rhs=t0[:, 0:nq], start=True, stop=True),
                  r=["onesf", "tmp0"], w=[f"pf{pj}"])
                A("act", lambda e, p2=p2: e.activation(out=t1[:, 0:nq], in_=p2[:, 0:nq], func=AF.Ln, bias=self.epsc[:, 0:1], scale=1.0), r=[f"pf{pj}", "epsc"], w=["tmp1"])
                A("act", lambda e: e.activation(out=t1[:, 0:nq], in_=t1[:, 0:nq], func=AF.Exp, scale=-0.5), r=["tmp1"], w=["tmp1"])
                A("dve", lambda e: e.scalar_tensor_tensor(out=t1[:, 0:nq], in0=t1[:, 0:nq], scalar=self.subg[:, 0:1], in1=o0[:, 0:nq],
                                                          op0=ALU.mult, op1=ALU.mult), r=["tmp1", "subg", k0], w=["tmp1"])
                A("dve", lambda e: e.tensor_tensor(out=yT[:, 8 + h, qs], in0=t1[:, 0:nq], in1=gz[:, qs], op=ALU.mult),
                  r=["tmp1", gzk], w=[f"C{8 + h}"])
            for (q0, subs) in blocks:
                cnt = self._blk
                self._blk += 1
                stageA(cnt, q0, subs)
                if self._pend is not None:
                    self._pend()
                self._pend = (lambda cnt=cnt, q0=q0, sb_=stageB: sb_(cnt, q0))
        if self._pend is not None:
            self._pend()
            self._pend = None
        self.rfence("AB")

    def out_proj_postnorm(self, wkey, x_src, mid, sink, xkey=None):
        A, dr = self.A, self.dr
        yT = self.C[:].rearrange("p (k t) -> p k t", t=1024)
        zall = self.AB[:].bitcast(F32).rearrange("p (t n) -> p t n", n=1024)
        ALLC = [f"C{k}" for k in range(16)]
        pre = [self.wload(dr[wkey][:, cb * 128:(cb + 1) * 128], 16, 128) for cb in range(4)]
        self.rfence("AB")
        for cb in range(8):
            wv, wk = pre[cb] if cb < 4 else self.wload(dr[wkey][:, cb * 128:(cb + 1) * 128], 16, 128)
            for half in range(2):
                pi = self.nxt("pf01", 2)
                p = self.pf[pi]
                for t4 in range(4):
                    tt = half * 4 + t4
                    for kt in range(16):
                        A("pe", lambda e, p=p, t4=t4, tt=tt, kt=kt, wv=wv: e.matmul(p[:, t4 * 128:(t4 + 1) * 128], lhsT=yT[:, kt, tt * 128:(tt + 1) * 128],
                                                                                   rhs=wv[:, kt, :], start=(kt == 0), stop=(kt == 15)),
                          r=[wk] + ALLC, w=[f"pf{pi}"])
                gate = self.mrep[:, 2048 + cb * 128:2048 + (cb + 1) * 128].unsqueeze(1).to_broadcast([128, 4, 128])
                A("dve", lambda e, p=p, half=half, cb=cb, gate=gate: e.tensor_tensor(
                    out=zall[:, half * 4:half * 4 + 4, cb * 128:(cb + 1) * 128], in0=p[:].rearrange("q (t n) -> q t n", n=128), in1=gate, op=ALU.mult),
                  r=[f"pf{pi}", "mrep"], w=[f"Z{half * 4 + t}" for t in range(4)])
        if mid is not None:
            mid()
        sts = []
        for tt in range(8):
            b = self.nxt("xt", 2)
            xt = self.xt[b]
            A("sp", lambda e, xt=xt, tt=tt: e.dma_start(out=xt[:], in_=x_src[tt * 128:(tt + 1) * 128, :]), r=([xkey] if xkey else []), w=[f"xt{b}"], dk=f"xt{b}")
            A("dve", lambda e, xt=xt, tt=tt: e.scalar_tensor_tensor(out=zall[:, tt, :], in0=xt[:], scalar=ALPHA, in1=zall[:, tt, :], op0=ALU.mult, op1=ALU.add),
              r=[f"xt{b}", f"Z{tt}"], w=[f"Z{tt}"])
            sts.append(self.ln_stats_t(zall[:, tt, :], f"Z{tt}", tt, 0))
        for tt in range(8):
            rstd, nmr, lk = sts[tt]
            zt = zall[:, tt, :]
            A("act", lambda e, zt=zt, nmr=nmr, rstd=rstd: e.activation(out=zt, in_=zt, func=AF.Identity, bias=nmr, scale=rstd), r=[f"Z{tt}"] + lk, w=[f"Z{tt}"])
            A("dve", lambda e, zt=zt: e.tensor_tensor(out=zt, in0=zt, in1=self.lngr[:], op=ALU.mult), r=[f"Z{tt}", "lngr"], w=[f"Z{tt}"])
            A("dve", lambda e, zt=zt: e.tensor_tensor(out=zt, in0=zt, in1=self.lnbr[:], op=ALU.add), r=[f"Z{tt}", "lnbr"], w=[f"Z{tt}"])
        sink([(zall[:, tt, :], f"Z{tt}", tt) for tt in range(8)])

    def fourier_path(self, grp):
        A, dr = self.A, self.dr
        isS = (grp == "S")
        L = 1024 if isS else 256
        ntl = L // 128
        mixedT = self.AB[:].rearrange("p (k t) -> p k t", t=1024)
        uT = [self.T[:, i * 2048:(i + 1) * 2048].rearrange("p (k t) -> p k t", t=1024) for i in range(2)]
        ucs = self.T[:, 4096:8192].rearrange("p (t n) -> p t n", n=512)
        cs256 = self.T[:, 8192:9216].rearrange("p (k n) -> p k n", n=512)
        CL = self.C[:, 0:ntl * L].rearrange("p (t n) -> p t n", n=L)
        SL = self.C[:, 8192:8192 + ntl * L].rearrange("p (t n) -> p t n", n=L)
        pre = [self.wload(dr["win_o"][:, fg * 256:(fg + 1) * 256], 8, 256) for fg in range(4)]
        self.rfence("AB"); self.rfence("C"); self.rfence("T")
        A("pool", lambda e: e.dma_start(out=cs256, in_=dr["cs256"].rearrange("(k p) n -> p k n", p=128)), w=["cs256"], dk="cs256")
        cn, sn = ("cl1024", "sl1024") if isS else ("cl256", "sl256")
        for tl in range(ntl):
            A("pool", lambda e, tl=tl: e.dma_start(out=CL[:, tl, :], in_=dr[cn][tl * 128:(tl + 1) * 128, :]), w=["Ctab"], dk="ctabC")
            A("pool", lambda e, tl=tl: e.dma_start(out=SL[:, tl, :], in_=dr[sn][tl * 128:(tl + 1) * 128, :]), w=["Ctab"], dk="ctabS")
        seqs = [(0, 1024)] if isS else [(i * 256, 256) for i in range(4)]
        for fg in range(8):
            wv, wk = pre[fg] if fg < 4 else self.wload(dr["win_o"][:, fg * 256:(fg + 1) * 256], 8, 256)
            ub = self.nxt("uT", 2)
            u = uT[ub]
            for kt2 in range(2):
                for tb in range(2):
                    ts = slice(tb * 512, (tb + 1) * 512)
                    pi = self.nxt("pf01", 2)
                    p = self.pf[pi]
                    for kt in range(8):
                        A("pe", lambda e, p=p, kt=kt, kt2=kt2, ts=ts, wv=wv: e.matmul(p[:], lhsT=wv[:, kt, kt2 * 128:(kt2 + 1) * 128], rhs=self.hT[:, kt, ts],
                                                                                     start=(kt == 0), stop=(kt == 7)), r=[wk, "hT"], w=[f"pf{pi}"])
                    A("act", lambda e, p=p, u=u, kt2=kt2, ts=ts: e.activation(out=u[:, kt2, ts], in_=p[:], func=AF.Copy), r=[f"pf{pi}"], w=[f"uT{ub}"])
            for tt in range(8):
                pj = 2 + self.nxt("pf23", 2)
                p2 = self.pf[pj]
                for kt2 in range(2):
                    A("pe", lambda e, p2=p2, kt2=kt2, tt=tt, u=u: e.matmul(p2[:], lhsT=u[:, kt2, tt * 128:(tt + 1) * 128], rhs=cs256[:, kt2, :],
                                                                          start=(kt2 == 0), stop=(kt2 == 1)), r=[f"uT{ub}", "cs256"], w=[f"pf{pj}"])
                if tt % 2 == 0:
                    A("dve", lambda e, p2=p2, tt=tt: e.tensor_copy(out=ucs[:, tt, :], in_=p2[:]), r=[f"pf{pj}"], w=["ucs"])
                else:
                    A("act", lambda e, p2=p2, tt=tt: e.activation(out=ucs[:, tt, :], in_=p2[:], func=AF.Copy), r=[f"pf{pj}"], w=["ucs"])
            for (s0, Ls) in seqs:
                nk1 = min(512, Ls)
                for f2 in range(2):
                    for kb in range(Ls // nk1):
                        ks = slice(kb * nk1, (kb + 1) * nk1)
                        pi = self.nxt("pf01", 2)
                        p = self.pf[pi]
                        n = 0
                        for tl in range(ntl):
                            tt = s0 // 128 + tl
                            for (co, tab) in [(0, CL), (256, SL)]:
                                A("pe", lambda e, p=p, tt=tt, tl=tl, f2=f2, co=co, tab=tab, ks=ks, nk1=nk1, n=n: e.matmul(
                                    p[:, 0:nk1], lhsT=ucs[:, tt, co + f2 * 128:co + (f2 + 1) * 128], rhs=tab[:, tl, ks],
                                    start=(n == 0), stop=(n == 2 * ntl - 1)), r=["ucs", "Ctab"], w=[f"pf{pi}"])
                                n += 1
                        A("act", lambda e, p=p, fg=fg, f2=f2, s0=s0, ks=ks, nk1=nk1: e.activation(
                            out=mixedT[:, fg * 2 + f2, s0 + ks.start:s0 + ks.stop], in_=p[:, 0:nk1], func=AF.Copy), r=[f"pf{pi}"], w=["mixA"])

    def fno_path(self):
        A, dr = self.A, self.dr
        mixedT = self.AB[:].rearrange("p (k t) -> p k t", t=1024)
        yT = self.C[:].rearrange("p (k t) -> p k t", t=1024)
        pre = []
        for ot in range(2):
            pre.append((self.wload(dr["wfno"][:, ot * 128:(ot + 1) * 128], 16, 128), self.wload(dr["win_o"][:, 2048 + ot * 128:2048 + (ot + 1) * 128], 8, 128)))
        self.rfence("C")
        for ot in range(16):
            if ot < 2:
                (wf, wfk), (wz, wzk) = pre[ot]
            else:
                wf, wfk = self.wload(dr["wfno"][:, ot * 128:(ot + 1) * 128], 16, 128)
                wz, wzk = self.wload(dr["win_o"][:, 2048 + ot * 128:2048 + (ot + 1) * 128], 8, 128)
            for tb in range(2):
                ts = slice(tb * 512, (tb + 1) * 512)
                pi = self.nxt("pf01", 2)
                p = self.pf[pi]
                for kt in range(16):
                    A("pe", lambda e, p=p, kt=kt, wf=wf, ts=ts: e.matmul(p[:], lhsT=wf[:, kt, :], rhs=mixedT[:, kt, ts], start=(kt == 0), stop=(kt == 15)),
                      r=[wfk, "mixA"], w=[f"pf{pi}"])
                pj = 2 + self.nxt("pf23", 2)
                p2 = self.pf[pj]
                for kt in range(8):
                    A("pe", lambda e, p2=p2, kt=kt, wz=wz, ts=ts: e.matmul(p2[:], lhsT=wz[:, kt, :], rhs=self.hT[:, kt, ts], start=(kt == 0), stop=(kt == 7)),
                      r=[wzk, "hT"], w=[f"pf{pj}"])
                t1 = self.tmp[1]
                A("act", lambda e, p2=p2: e.activation(out=t1[:], in_=p2[:], func=AF.Silu), r=[f"pf{pj}"], w=["tmp1"])
                A("dve", lambda e, p=p, ot=ot, ts=ts: e.scalar_tensor_tensor(out=yT[:, ot, ts], in0=p[:], scalar=self.bfno[:, ot:ot + 1], in1=t1[:],
                                                                            op0=ALU.add, op1=ALU.mult), r=[f"pf{pi}", "bfno", "tmp1"], w=[f"C{ot}"])

    def load_ln(self, layer):
        A, dr = self.A, self.dr
        A("sp", lambda e: e.dma_start(out=self.lngr[:], in_=dr["lng"][layer].partition_broadcast(128)), w=["lngr"], dk="lngr")
        A("sp", lambda e: e.dma_start(out=self.lnbr[:], in_=dr["lnb"][layer].partition_broadcast(128)), w=["lnbr"], dk="lnbr")

    def run_group(self, grp):
        A, dr = self.A, self.dr
        isS = (grp == "S")
        xin = dr["xs"] if isS else dr["xp"]
        cond = dr["csmp"] if isS else dr["cctx"]
        x1s = dr["x1s"][1 if isS else 0]
        yout = dr["ys"] if isS else dr["yp"]
        self.rfence("AB")
        xall = self.AB[:].bitcast(F32).rearrange("p (t n) -> p t n", n=1024)
        tiles = []
        for tt in range(8):
            A("sp", lambda e, tt=tt: e.dma_start(out=xall[:, tt, :], in_=xin[tt * 128:(tt + 1) * 128, :]), w=[f"Z{tt}"], dk=f"xin{tt % 4}")
            tiles.append((xall[:, tt, :], f"Z{tt}", tt))
        st = [self.ln_stats_t(src, key, tt, 1) for (src, key, tt) in tiles]
        self.load_mod(0, 1 if isS else 0)
        self.load_ln(0)
        self.ln_mod_tiles(tiles, st)
        upto = getattr(self, "upto", 99)
        if upto < 1:
            return
        self.s5_path(grp)
        if upto < 2:
            return
        self.glu_path()
        if upto < 3:
            return
        self.attn_path(grp)
        if upto < 4:
            return

        def sink0(tiles):
            for (ap, key, tt) in tiles:
                A("sp", lambda e, ap=ap, tt=tt: e.dma_start(out=x1s[tt * 128:(tt + 1) * 128, :], in_=ap), r=[key], w=["x1s"], dk=f"x1w{tt % 2}")
            self.ln_mod_tiles(tiles)
        self.out_proj_postnorm("wout_e", xin, lambda: self.load_mod(1, 1 if isS else 0), sink0)
        if upto < 5:
            return
        self.load_ln(1)
        self.fourier_path(grp)
        if upto < 6:
            return
        self.fno_path()
        if upto < 7:
            return

        def sink1(tiles):
            for (ap, key, tt) in tiles:
                tk = A("sp", lambda e, ap=ap, tt=tt: e.dma_start(out=yout[tt * 128:(tt + 1) * 128, :], in_=ap), r=[key], w=["yout"], dk=f"yw{tt % 2}")
                self.out_tokens.append(tk)
        self.out_proj_postnorm("wout_o", x1s, None, sink1, xkey="x1s")

    def emit(self):
        nc, tr = self.nc, self.tr
        pre = self.pre
        outs = self.out_tokens
        with nc.Block() as block:
            @block.sync
            def _(e):
                tr.emit_engine("sp", e)
                tr.final_waits(e, tr.all_tokens())

            @block.scalar
            def _(e):
                tr.emit_engine("act", e)

            @block.vector
            def _(e):
                tr.emit_engine("dve", e)

            @block.gpsimd
            def _(e):
                tr.emit_engine("pool", e)

            @block.tensor
            def _(e):
                tr.emit_engine("pe", e)


IN_SPECS = [
    ("xp", [1024, 1024]), ("xs", [1024, 1024]), ("cctx", [1024]), ("csmp", [1024]),
    ("wmod", [2, 1024, 3072]), ("bmod", [2, 3072]), ("lng", [2, 1024]), ("lnb", [2, 1024]),
    ("win_e", [1024, 6144]), ("wglu", [1024, 1024]), ("bglu", [1024]), ("wout_e", [2048, 1024]), ("subg", [128]),
    ("win_o", [1024, 4096]), ("wfno", [2048, 2048]), ("bfno", [2048]), ("wout_o", [2048, 1024]),
    ("cache_k", [256, 8, 2, 64]), ("cache_v", [256, 8, 128]), ("st_re", [2, 64, 64]), ("st_im", [2, 64, 64]),
    ("lam_re", [2, 64, 64]), ("lam_im", [2, 64, 64]), ("log_dt", [2, 64]),
    ("b_re", [2, 64, 64, 16]), ("b_im", [2, 64, 64, 16]), ("c_re", [2, 64, 16, 64]), ("c_im", [2, 64, 16, 64]), ("ssm_d", [1024]),
    ("lq1", [64]), ("lk1", [64]), ("lq2", [64]), ("lk2", [64]),
    ("ident", [128, 128]), ("maskF", [128, 128]), ("maskB", [128, 128]), ("pswap", [128, 128]),
    ("ropeC", [128, 1024]), ("ropeS", [128, 1024]), ("cs256", [256, 512]),
    ("cl256", [256, 256]), ("sl256", [256, 256]), ("cl1024", [1024, 1024]), ("sl1024", [1024, 1024]),
]
OUT_SPECS = [("yp", [1024, 1024]), ("ys", [1024, 1024]), ("nk", [1024, 1024]), ("nv", [1024, 1024]),
             ("fre", [4, 2, 64, 64]), ("fim", [4, 2, 64, 64])]


def build_nc(groups=("P", "S"), upto=99, att_lvl=9):
    nc = bass.Bass("TRN2", target_bir_lowering=False)
    dr = {}
    for n, shp in IN_SPECS:
        dr[n] = nc.dram_tensor(n, shp, F32, kind="ExternalInput").ap()
    for n, shp in OUT_SPECS:
        dr[n] = nc.dram_tensor(n, shp, F32, kind="ExternalOutput").ap()
    dr["s_opW"] = nc.dram_tensor("s_opW", [128, 64, 2, 128], BF16).ap()
    dr["s_opQ"] = nc.dram_tensor("s_opQ", [128, 64, 2, 128], BF16).ap()
    dr["s_opM"] = nc.dram_tensor("s_opM", [128, 64, 128], BF16).ap()
    dr["s_A8"] = nc.dram_tensor("s_A8", [128, 2, 64], F32).ap()
    dr["s_nlam"] = nc.dram_tensor("s_nlam", [128, 1], F32).ap()
    dr["s_mod"] = nc.dram_tensor("s_mod", [2, 2, 3072], F32).ap()
    dr["x1s"] = nc.dram_tensor("x1s", [2, 1024, 1024], F32).ap()
    with contextlib.ExitStack() as es:
        sems = [es.enter_context(nc.semaphore(f"s{i}")) for i in range(100)]
        tr0 = phase0(nc, dr, sems[:38])
        with contextlib.ExitStack() as es2:
            m = Main(nc, dr, sems[38:], tr0.all_tokens())
            m.upto = upto
            m.att_lvl = att_lvl
            m.alloc(es2)
            m.alloc2(es2)
            m.load_consts()
            for g in groups:
                m.run_group(g)
            m.emit()
    return nc


def host_consts():
    i8 = np.arange(128) // 16
    c = {"ident": np.eye(128, dtype=np.float32),
         "maskF": (i8[:, None] <= i8[None, :]).astype(np.float32),
         "maskB": (i8[:, None] >= i8[None, :]).astype(np.float32)}
    psw = np.zeros((128, 128), np.float32)
    for m in range(128):
        blk, j = divmod(m, 64)
        psw[blk * 64 + (j + 32) % 64, m] = 1.0
    c["pswap"] = psw
    L = 1024
    row = np.repeat(np.arange(L // 64), 64).astype(np.float64)
    col = np.tile(np.arange(64), L // 64).astype(np.float64)
    freqs = 10000.0 ** (-np.arange(16, dtype=np.float64) / 16)
    ang = np.concatenate([row[:, None] * freqs, col[:, None] * freqs], axis=-1)
    cosT = np.cos(ang).T
    sinT = np.sin(ang).T
    c["ropeC"] = np.concatenate([cosT, cosT, cosT, cosT], axis=0).astype(np.float32)
    c["ropeS"] = np.concatenate([-sinT, sinT, -sinT, sinT], axis=0).astype(np.float32)
    k = np.arange(256, dtype=np.float64)
    a = 2 * np.pi * np.outer(k, k) / 256
    c["cs256"] = (np.concatenate([np.cos(a), np.sin(a)], axis=1) / 16.0).astype(np.float32)
    for Ls, cn, sn in [(256, "cl256", "sl256"), (1024, "cl1024", "sl1024")]:
        t = np.arange(Ls, dtype=np.float64)
        a = 2 * np.pi * (np.outer(t, t) % Ls) / Ls
        c[cn] = (np.cos(a) / np.sqrt(Ls)).astype(np.float32)
        c[sn] = (-np.sin(a) / np.sqrt(Ls)).astype(np.float32)
    return c


def make_in_maps(x_prompt, x_sample, cache_k, cache_v, state_ssm_re, state_ssm_im, c, c_ctx,
                 w_mod, b_mod, ln_g, ln_b, w_in_e, ssm_lam_re, ssm_lam_im, ssm_log_dt,
                 ssm_b_re, ssm_b_im, ssm_c_re, ssm_c_im, ssm_d, w_glu, b_glu,
                 lam_q1, lam_k1, lam_q2, lam_k2, subln_g, w_out_e, w_in_o, w_fno, b_fno, w_out_o):
    f = lambda a: np.ascontiguousarray(np.asarray(a, dtype=np.float32))
    shared = {
        "cctx": f(c_ctx), "wmod": f(w_mod), "bmod": f(b_mod), "lng": f(ln_g), "lnb": f(ln_b),
        "win_e": f(w_in_e[0]), "wglu": f(w_glu[0]), "bglu": f(b_glu[0]), "wout_e": f(w_out_e[0]), "subg": f(subln_g[0]),
        "win_o": f(w_in_o[0]), "wfno": f(w_fno[0]), "bfno": f(b_fno[0]), "wout_o": f(w_out_o[0]),
        "lam_re": f(ssm_lam_re[0]), "lam_im": f(ssm_lam_im[0]), "log_dt": f(ssm_log_dt[0]),
        "b_re": f(ssm_b_re[0]), "b_im": f(ssm_b_im[0]), "c_re": f(ssm_c_re[0]), "c_im": f(ssm_c_im[0]), "ssm_d": f(ssm_d[0]),
        "lq1": f(lam_q1[0]), "lk1": f(lam_k1[0]), "lq2": f(lam_q2[0]), "lk2": f(lam_k2[0]),
    }
    shared.update(host_consts())
    xp = np.asarray(x_prompt, np.float32)
    xs = np.asarray(x_sample, np.float32)
    maps = []
    for i in range(NCORES):
        b = i % 4
        m = dict(shared)
        m["xp"] = f(xp[4 * i:4 * i + 4].reshape(1024, 1024))
        m["xs"] = f(xs[b])
        m["csmp"] = f(np.asarray(c)[b])
        m["cache_k"] = f(np.asarray(cache_k)[b, 0])
        m["cache_v"] = f(np.asarray(cache_v)[b, 0])
        m["st_re"] = f(np.asarray(state_ssm_re)[b, 0])
        m["st_im"] = f(np.asarray(state_ssm_im)[b, 0])
        maps.append(m)
    return maps


def gather(results):
    g = lambda n, shp: np.concatenate([np.asarray(r[n], np.float32).reshape(shp) for r in results], axis=0)
    y_prompt = g("yp", (4, 256, 1024))
    new_k = g("nk", (4, 1, 256, 8, 2, 64))
    new_v = g("nv", (4, 1, 256, 8, 128))
    s_re = g("fre", (4, 1, 2, 64, 64))
    s_im = g("fim", (4, 1, 2, 64, 64))
    y_sample = np.stack([np.asarray(results[b]["ys"], np.float32) for b in range(4)], axis=0)
    return (y_prompt, y_sample, new_k, new_v, s_re, s_im)


def kernel(**inputs):
    nc = build_nc()
    maps = make_in_maps(**inputs)
    res = run_bass_kernel_spmd(nc, maps, core_ids=list(range(NCORES)))
    return gather(res.results)
```

```python
import contextlib
import math
import numpy as np
import concourse.bass as bass
import concourse.mybir as mybir
from concourse.bass_utils import run_bass_kernel_spmd

F32 = mybir.dt.float32
BF16 = mybir.dt.bfloat16
ALU = mybir.AluOpType
AF = mybir.ActivationFunctionType
AX = mybir.AxisListType

D = 1024
NCORES = 8
TOK = 1024
LN_EPS = 1e-5
ALPHA = (2 * 2) ** 0.25
ENGS = ["sp", "act", "dve", "pool", "pe"]
EPOCH = 8000
T8 = 8
GELU_C = 2.0 * math.sqrt(2.0 / math.pi)


class Tracker:
    def __init__(self, sem_pool):
        self.sem_pool = list(sem_pool)
        self.ops = []
        self.buf = {}
        self.eng_cnt = {e: 0 for e in ENGS}
        self.eng_sems = {e: [] for e in ENGS}
        self.dma_sem = {}
        self.dma_cnt = {}

    def _st(self, k):
        if k not in self.buf:
            self.buf[k] = [None, []]
        return self.buf[k]

    def add(self, eng, emit, reads=(), writes=(), dma_key=None):
        reads = list(reads)
        writes = list(writes)
        if dma_key is not None:
            writes.append("#dma:" + dma_key)
        deps = set()
        for k in reads:
            st = self._st(k)
            if st[0] is not None:
                deps.add(st[0])
            if k.startswith("pf") or k.startswith("pt"):
                for r_ in st[1]:
                    if r_[4] != eng:
                        deps.add(r_)
        for k in writes:
            st = self._st(k)
            if st[0] is not None:
                deps.add(st[0])
            for r in st[1]:
                deps.add(r)
        if dma_key is not None:
            if dma_key not in self.dma_sem:
                self.dma_sem[dma_key] = self.sem_pool.pop()
                self.dma_cnt[dma_key] = 0
            self.dma_cnt[dma_key] += 16
            sem = self.dma_sem[dma_key]
            token = (id(sem), self.dma_cnt[dma_key], sem, 16, None)
        else:
            c = self.eng_cnt[eng]
            ep, v = divmod(c, EPOCH)
            if ep >= len(self.eng_sems[eng]):
                self.eng_sems[eng].append(self.sem_pool.pop())
            sem = self.eng_sems[eng][ep]
            self.eng_cnt[eng] = c + 1
            token = (id(sem), v + 1, sem, 1, eng)
        for k in reads:
            self._st(k)[1].append(token)
        for k in writes:
            st = self._st(k)
            st[0] = token
            st[1] = []
        self.ops.append((eng, emit, deps, token))
        return token

    def emit_engine(self, eng, e):
        waited = {}
        for (oe, emit, deps, token) in self.ops:
            if oe != eng:
                continue
            need = {}
            for (sid, val, sem, inc, deng) in deps:
                if deng == "pe" and eng == "pe":
                    continue
                if sid not in need or need[sid][0] < val:
                    need[sid] = (val, sem)
            for sid, (val, sem) in need.items():
                if waited.get(sid, 0) >= val:
                    continue
                e.wait_ge(sem, val)
                waited[sid] = val
            emit(e).then_inc(token[2], token[3])

    def all_tokens(self):
        toks = []
        for k, st in self.buf.items():
            if st[0] is not None:
                toks.append(st[0])
            toks += st[1]
        return toks

    def final_waits(self, e, tokens):
        need = {}
        for (sid, val, sem, inc, deng) in tokens:
            if sid not in need or need[sid][0] < val:
                need[sid] = (val, sem)
        for sid, (val, sem) in need.items():
            e.wait_ge(sem, val)


def run_block(nc, tr, final_tokens):
    with nc.Block() as block:
        @block.sync
        def _(e):
            tr.emit_engine("sp", e)
            tr.final_waits(e, final_tokens)

        @block.scalar
        def _(e):
            tr.emit_engine("act", e)

        @block.vector
        def _(e):
            tr.emit_engine("dve", e)

        @block.gpsimd
        def _(e):
            tr.emit_engine("pool", e)

        @block.tensor
        def _(e):
            tr.emit_engine("pe", e)


def phase0(nc, dr, sems):
    tr = Tracker(sems)
    es = contextlib.ExitStack()
    with es:
        def sb(name, shape, dt):
            return es.enter_context(nc.sbuf_tensor("p0_" + name, shape, dt))

        def ps(name, shape, dt):
            return es.enter_context(nc.psum_tensor("p0_" + name, shape, dt))
        uid = [0]

        def A(eng, fn, r=(), w=(), dk=None):
            return tr.add(eng, fn, reads=r, writes=w, dma_key=dk)

        identf = sb("identf", [128, 128], F32)
        identb = sb("identb", [128, 128], BF16)
        maskF = sb("maskF", [128, 128], F32)
        maskB = sb("maskB", [128, 128], F32)
        lin = [sb(f"lin{i}", [64, 2, 64], F32) for i in range(2)]
        LR = sb("LR", [128, 64], F32)
        LI = sb("LI", [128, 64], F32)
        LDT = sb("LDT", [128, 64], F32)
        BR = sb("BR", [128, 64, 16], F32)
        BI = sb("BI", [128, 64, 16], F32)
        CR = sb("CR", [128, 64, 16], F32)
        CI = sb("CI", [128, 64, 16], F32)
        nCR = sb("nCR", [128, 64, 16], F32)
        nCI = sb("nCI", [128, 64, 16], F32)
        cin = [sb(f"cin{i}", [128, 2, 64], F32) for i in range(2)]
        Dcol = sb("Dcol", [128, 64], F32)
        sm = {n: sb(n, [128, 64], F32) for n in
              ["dt", "ldr", "ldi", "c", "s", "mag", "ar", "ai", "e2", "ivr", "ivi", "t1", "t2", "t3", "t4",
               "cr", "ci", "nr", "ni", "am1", "den", "fr", "fi", "A8r", "A8i"]}
        EPr = sb("EPr", [128, 64, 8], F32)
        EPi = sb("EPi", [128, 64, 8], F32)
        ENr = sb("ENr", [128, 64, 8], F32)
        ENi = sb("ENi", [128, 64, 8], F32)
        bbr = sb("bbr", [128, 64, 16], F32)
        bbi = sb("bbi", [128, 64, 16], F32)
        bt1 = sb("bt1", [128, 64, 16], F32)
        bt2 = sb("bt2", [128, 64, 16], F32)
        T1 = sb("T1", [128, 16, 8, 16], F32)
        T2 = sb("T2", [128, 16, 8, 16], F32)
        T3 = sb("T3", [128, 16, 8, 16], F32)
        T4 = sb("T4", [128, 16, 8, 16], F32)
        PRb = sb("PRb", [128, 64, 128], BF16)
        PIb = sb("PIb", [128, 64, 128], BF16)
        Qrb = sb("Qrb", [128, 64, 128], BF16)
        nQib = sb("nQib", [128, 64, 128], BF16)
        Wop = sb("Wop", [128, 64, 2, 128], BF16)
        Mop = sb("Mop", [128, 64, 128], BF16)
        lq = sb("lq", [128, 4, 64], F32)
        lsum = sb("lsum", [128, 2], F32)
        lprod = sb("lprod", [128, 2, 64], F32)
        nlam = sb("nlam", [128, 1], F32)
        pT = ps("pT", [128, 128], F32)
        pW = [ps(f"pW{i}", [128, 8, 128], BF16) for i in range(2)]
        pMf = ps("pMf", [128, 4, 128], F32)
        pMb = ps("pMb", [128, 4, 128], F32)
        pmod = [ps(f"pmod{i}", [128, 256], F32) for i in range(2)]
        ccm = sb("ccm", [128, 2, 8], F32)
        scm = sb("scm", [128, 2, 8], F32)

        A("sp", lambda e: e.dma_start(out=identf[:], in_=dr["ident"]), w=["identf"], dk="identf")
        A("sp", lambda e: e.dma_start(out=maskF[:], in_=dr["maskF"]), w=["maskF"], dk="maskF")
        A("sp", lambda e: e.dma_start(out=maskB[:], in_=dr["maskB"]), w=["maskB"], dk="maskB")
        A("dve", lambda e: e.tensor_copy(out=identb[:], in_=identf[:]), r=["identf"], w=["identb"])
        if "wmod" in dr and "s_mod" in dr:
            for ci, cn in enumerate(["cctx", "csmp"]):
                A("sp", lambda e, cn=cn, ci=ci: e.dma_start(out=ccm[:, ci, :], in_=dr[cn].rearrange("(k p) -> p k", p=128), allow_slow_non_contiguous=True),
                  w=["ccm"], dk="ccm")
            A("act", lambda e: e.activation(out=scm[:], in_=ccm[:], func=AF.Silu), r=["ccm"], w=["scm"])
        for i, (src, dst, dn) in enumerate([("lam_re", LR, "LR"), ("lam_im", LI, "LI")]):
            A("sp", lambda e, i=i, src=src: e.dma_start(out=lin[i][:], in_=dr[src].rearrange("d g p -> g d p")),
              w=[f"lin{i}"], dk=f"lin{i}")
            A("pe", lambda e, i=i: e.transpose(pT[:, 0:64], lin[i][:].rearrange("g d p -> g (d p)"), identf[0:64, 0:64]),
              r=[f"lin{i}", "identf"], w=["pT"])
            A("dve", lambda e, dst=dst: e.tensor_copy(out=dst[:], in_=pT[:, 0:64]), r=["pT"], w=[dn])
        for d in range(2):
            A("sp", lambda e, d=d: e.dma_start(out=LDT[64 * d:64 * d + 64, :], in_=dr["log_dt"][d].partition_broadcast(64)),
              w=["LDT"], dk=f"LDT{d}")
            A("sp", lambda e, d=d: e.dma_start(out=BR[64 * d:64 * d + 64], in_=dr["b_re"][d].rearrange("g p m -> p g m")),
              w=["BR"], dk=f"BR{d}")
            A("act", lambda e, d=d: e.dma_start(out=BI[64 * d:64 * d + 64], in_=dr["b_im"][d].rearrange("g p m -> p g m")),
              w=["BI"], dk=f"BI{d}")
        k = 0
        for (src, dst, dn) in [("c_re", CR, "CR"), ("c_im", CI, "CI")]:
            for blk in range(8):
                b = k % 2
                k += 1
                A("sp", lambda e, b=b, src=src, blk=blk: e.dma_start(
                    out=cin[b][:], in_=dr[src][:, 8 * blk:8 * blk + 8].rearrange("d g n p -> (g n) d p")),
                  w=[f"cin{b}"], dk=f"cin{b}")
                A("pe", lambda e, b=b: e.transpose(pT[:], cin[b][:].rearrange("q d p -> q (d p)"), identf[:]),
                  r=[f"cin{b}", "identf"], w=["pT"])
                A("dve", lambda e, dst=dst, blk=blk: e.tensor_copy(
                    out=dst[:, 8 * blk:8 * blk + 8, :], in_=pT[:].rearrange("q (g n) -> q g n", n=16)), r=["pT"], w=[dn])
        S = sm

        def tt(o, a, b, op, eng="dve"):
            A(eng, lambda e: e.tensor_tensor(out=S[o][:], in0=S[a][:], in1=S[b][:], op=op), r=[a, b], w=[o])

        def cmul(zr, zi, xr, xi, yr, yi):
            tt("t1", xr, yr, ALU.mult); tt("t2", xi, yi, ALU.mult)
            tt("t3", xr, yi, ALU.mult); tt("t4", xi, yr, ALU.mult)
            tt(zr, "t1", "t2", ALU.subtract); tt(zi, "t3", "t4", ALU.add)

        A("act", lambda e: e.activation(out=S["dt"][:], in_=LDT[:], func=AF.Exp), r=["LDT"], w=["dt"])
        A("dve", lambda e: e.tensor_tensor(out=S["ldr"][:], in0=LR[:], in1=S["dt"][:], op=ALU.mult), r=["LR", "dt"], w=["ldr"])
        A("dve", lambda e: e.tensor_tensor(out=S["ldi"][:], in0=LI[:], in1=S["dt"][:], op=ALU.mult), r=["LI", "dt"], w=["ldi"])
        A("dve", lambda e: e.tensor_scalar(out=S["t1"][:], in0=S["ldi"][:], scalar1=1.0 / 32, scalar2=math.pi / 2,
                                           op0=ALU.mult, op1=ALU.add), r=["ldi"], w=["t1"])
        A("act", lambda e: e.activation(out=S["c"][:], in_=S["t1"][:], func=AF.Sin), r=["t1"], w=["c"])
        A("act", lambda e: e.activation(out=S["s"][:], in_=S["ldi"][:], func=AF.Sin, scale=1.0 / 32), r=["ldi"], w=["s"])
        for _ in range(5):
            tt("t1", "c", "c", ALU.mult); tt("t2", "s", "s", ALU.mult); tt("t3", "c", "s", ALU.mult)
            tt("c", "t1", "t2", ALU.subtract); tt("s", "t3", "t3", ALU.add)
        A("act", lambda e: e.activation(out=S["mag"][:], in_=S["ldr"][:], func=AF.Exp), r=["ldr"], w=["mag"])
        A("act", lambda e: e.activation(out=S["e2"][:], in_=S["ldr"][:], func=AF.Exp, scale=-2.0), r=["ldr"], w=["e2"])
        tt("ar", "mag", "c", ALU.mult); tt("ai", "mag", "s", ALU.mult)
        tt("ivr", "ar", "e2", ALU.mult)
        A("dve", lambda e: e.scalar_tensor_tensor(out=S["ivi"][:], in0=S["ai"][:], scalar=-1.0, in1=S["e2"][:],
                                                  op0=ALU.mult, op1=ALU.mult), r=["ai", "e2"], w=["ivi"])
        for (tr_, ti_, br_, bi_, cr_, ci_) in [(EPr, EPi, "ar", "ai", "cr", "ci"), (ENr, ENi, "ivr", "ivi", "nr", "ni")]:
            A("dve", lambda e, cr_=cr_, br_=br_: e.tensor_copy(out=S[cr_][:], in_=S[br_][:]), r=[br_], w=[cr_])
            A("dve", lambda e, ci_=ci_, bi_=bi_: e.tensor_copy(out=S[ci_][:], in_=S[bi_][:]), r=[bi_], w=[ci_])
            tn = "EP" if tr_ is EPr else "EN"
            for kk in range(1, 9):
                for (tab, cur) in [(tr_, cr_), (ti_, ci_)]:
                    A("act", lambda e, tab=tab, cur=cur, kk=kk: e.activation(out=tab[0:64, :, kk - 1], in_=S[cur][0:64, :], func=AF.Copy),
                      r=[cur], w=[tn])
                    A("act", lambda e, tab=tab, cur=cur, kk=kk: e.activation(out=tab[64:128, :, 8 - kk], in_=S[cur][64:128, :], func=AF.Copy),
                      r=[cur], w=[tn])
                if kk == 8 and tr_ is EPr:
                    A("dve", lambda e: e.tensor_copy(out=S["A8r"][:], in_=S["cr"][:]), r=["cr"], w=["A8r"])
                    A("dve", lambda e: e.tensor_copy(out=S["A8i"][:], in_=S["ci"][:]), r=["ci"], w=["A8i"])
                if kk < 8:
                    cmul(cr_, ci_, cr_, ci_, br_, bi_)
        A("sp", lambda e: e.dma_start(out=dr["s_A8"][:, 0, :], in_=S["A8r"][:]), r=["A8r"], w=["s_A8r"], dk="s_A8r")
        A("sp", lambda e: e.dma_start(out=dr["s_A8"][:, 1, :], in_=S["A8i"][:]), r=["A8i"], w=["s_A8i"], dk="s_A8i")
        for i in range(8):
            A("sp", lambda e, i=i: e.dma_start(out=Dcol[16 * i:16 * i + 16, :], in_=dr["ssm_d"].rearrange("(g m) -> m g", m=16),
                                               allow_slow_non_contiguous=True), w=["Dcol"], dk="Dcol")
        for i, nm in enumerate(["lq1", "lk1", "lq2", "lk2"]):
            A("sp", lambda e, i=i, nm=nm: e.dma_start(out=lq[:, i, :], in_=dr[nm].partition_broadcast(128)), w=["lq"], dk="lq")
        A("dve", lambda e: e.tensor_tensor(out=lprod[:, 0, :], in0=lq[:, 0, :], in1=lq[:, 1, :], op=ALU.mult), r=["lq"], w=["lprod"])
        A("dve", lambda e: e.tensor_tensor(out=lprod[:, 1, :], in0=lq[:, 2, :], in1=lq[:, 3, :], op=ALU.mult), r=["lq"], w=["lprod"])
        A("dve", lambda e: e.reduce_sum(out=lsum[:], in_=lprod[:], axis=AX.X), r=["lprod"], w=["lsum"])
        A("act", lambda e: e.activation(out=lsum[:], in_=lsum[:], func=AF.Exp), r=["lsum"], w=["lsum"])
        A("dve", lambda e: e.scalar_tensor_tensor(out=nlam[:], in0=lsum[:, 1:2], scalar=-0.2, in1=lsum[:, 0:1],
                                                  op0=ALU.add, op1=ALU.subtract), r=["lsum"], w=["nlam"])
        A("sp", lambda e: e.dma_start(out=dr["s_nlam"], in_=nlam[:]), r=["nlam"], w=["s_nlam"], dk="s_nlam")

        A("dve", lambda e: e.tensor_scalar_add(out=S["am1"][:], in0=S["ar"][:], scalar1=-1.0), r=["ar"], w=["am1"])
        A("dve", lambda e: e.tensor_tensor(out=S["t1"][:], in0=LR[:], in1=LR[:], op=ALU.mult), r=["LR"], w=["t1"])
        A("dve", lambda e: e.tensor_tensor(out=S["t2"][:], in0=LI[:], in1=LI[:], op=ALU.mult), r=["LI"], w=["t2"])
        tt("den", "t1", "t2", ALU.add)
        A("dve", lambda e: e.reciprocal(out=S["den"][:], in_=S["den"][:]), r=["den"], w=["den"])
        A("dve", lambda e: e.tensor_tensor(out=S["t1"][:], in0=S["am1"][:], in1=LR[:], op=ALU.mult), r=["am1", "LR"], w=["t1"])
        A("dve", lambda e: e.tensor_tensor(out=S["t2"][:], in0=S["ai"][:], in1=LI[:], op=ALU.mult), r=["ai", "LI"], w=["t2"])
        tt("t3", "t1", "t2", ALU.add); tt("fr", "t3", "den", ALU.mult)
        A("dve", lambda e: e.tensor_tensor(out=S["t1"][:], in0=S["ai"][:], in1=LR[:], op=ALU.mult), r=["ai", "LR"], w=["t1"])
        A("dve", lambda e: e.tensor_tensor(out=S["t2"][:], in0=S["am1"][:], in1=LI[:], op=ALU.mult), r=["am1", "LI"], w=["t2"])
        tt("t3", "t1", "t2", ALU.subtract); tt("fi", "t3", "den", ALU.mult)
        frb = S["fr"][:].unsqueeze(2).to_broadcast([128, 64, 16])
        fib = S["fi"][:].unsqueeze(2).to_broadcast([128, 64, 16])
        A("dve", lambda e: e.tensor_tensor(out=bt1[:], in0=BR[:], in1=frb, op=ALU.mult), r=["BR", "fr"], w=["bt1"])
        A("dve", lambda e: e.tensor_tensor(out=bt2[:], in0=BI[:], in1=fib, op=ALU.mult), r=["BI", "fi"], w=["bt2"])
        A("dve", lambda e: e.tensor_tensor(out=bbr[:], in0=bt1[:], in1=bt2[:], op=ALU.subtract), r=["bt1", "bt2"], w=["bbr"])
        A("dve", lambda e: e.tensor_tensor(out=bt1[:], in0=BI[:], in1=frb, op=ALU.mult), r=["BI", "fr"], w=["bt1"])
        A("dve", lambda e: e.tensor_tensor(out=bt2[:], in0=BR[:], in1=fib, op=ALU.mult), r=["BR", "fi"], w=["bt2"])
        A("dve", lambda e: e.tensor_tensor(out=bbi[:], in0=bt1[:], in1=bt2[:], op=ALU.add), r=["bt1", "bt2"], w=["bbi"])
        A("dve", lambda e: e.tensor_scalar_mul(out=nCR[:], in0=CR[:], scalar1=-1.0), r=["CR"], w=["nCR"])
        A("dve", lambda e: e.tensor_scalar_mul(out=nCI[:], in0=CI[:], scalar1=-1.0), r=["CI"], w=["nCI"])
        for gb in range(4):
            gs = slice(16 * gb, 16 * gb + 16)

            def prod(o, tab, vec, tn, vn, gs=gs, eng="dve"):
                a0 = tab[:, gs, :].unsqueeze(3).to_broadcast([128, 16, 8, 16])
                a1 = vec[:, gs, :].unsqueeze(2).to_broadcast([128, 16, 8, 16])
                A(eng, lambda e, o=o, a0=a0, a1=a1: e.tensor_tensor(out=o[:], in0=a0, in1=a1, op=ALU.mult), r=[tn, vn], w=[o.name])

            def comb(dst, dn, op, gs=gs, ta=T1, tb=T2, eng="dve"):
                ov = dst[:, gs, :].rearrange("q g (i m) -> q g i m", m=16)
                A(eng, lambda e, ov=ov, op=op, ta=ta, tb=tb: e.tensor_tensor(out=ov, in0=ta[:], in1=tb[:], op=op), r=[ta.name, tb.name], w=[dn])
            prod(T1, ENr, bbr, "EN", "bbr"); prod(T2, ENi, bbi, "EN", "bbi"); comb(PRb, "PRb", ALU.subtract)
            prod(T1, ENr, bbi, "EN", "bbi"); prod(T2, ENi, bbr, "EN", "bbr"); comb(PIb, "PIb", ALU.add)
            if gb < 2:
                qe, qa, qb_ = "pool", T3, T4
            else:
                qe, qa, qb_ = "dve", T1, T2
            prod(qa, EPr, CR, "EP", "CR", eng=qe); prod(qb_, EPi, CI, "EP", "CI", eng=qe); comb(Qrb, "Qrb", ALU.subtract, ta=qa, tb=qb_, eng=qe)
            prod(qa, EPr, nCI, "EP", "nCI", eng=qe); prod(qb_, EPi, nCR, "EP", "nCR", eng=qe); comb(nQib, "nQib", ALU.add, ta=qa, tb=qb_, eng=qe)
        A("sp", lambda e: e.dma_start(out=dr["s_opQ"][:, :, 0, :], in_=Qrb[:]), r=["Qrb"], w=["s_opQ0"], dk="s_opQ0")
        A("sp", lambda e: e.dma_start(out=dr["s_opQ"][:, :, 1, :], in_=nQib[:]), r=["nQib"], w=["s_opQ1"], dk="s_opQ1")
        k = 0
        for ri, (P_, pn) in enumerate([(PRb, "PRb"), (PIb, "PIb")]):
            for blk in range(8):
                b = k % 2
                k += 1
                for gl in range(8):
                    A("pe", lambda e, b=b, gl=gl, P_=P_, blk=blk: e.transpose(pW[b][:, gl, :], P_[:, 8 * blk + gl, :], identb[:]),
                      r=[pn, "identb"], w=[f"pW{b}"])
                A("act", lambda e, b=b, ri=ri, blk=blk: e.activation(out=Wop[:, 8 * blk:8 * blk + 8, ri, :], in_=pW[b][:], func=AF.Copy),
                  r=[f"pW{b}"], w=["Wop"])
        A("sp", lambda e: e.dma_start(out=dr["s_opW"], in_=Wop[:]), r=["Wop"], w=["s_opW"], dk="s_opW")
        do_mod = "wmod" in dr and "s_mod" in dr
        if do_mod:
            screpv = [t[:].bitcast(BF16).rearrange("p a b -> p (a b)").rearrange("p (k n) -> p k n", n=128) for t in (ENr, ENi)]
            wslots = [t[:].bitcast(BF16).rearrange("p a b -> p (a b)").rearrange("p (k n) -> p k n", n=256) for t in (BR, BI, CR, CI, nCR, nCI)]
            wkeys = ["BR", "BI", "CR", "CI", "nCR", "nCI"]
            NSL = 6
            biasf = EPr[:].rearrange("p a b -> p (a b)")
            rows = [[(T3[:].rearrange("p a b c -> p (a b c)"), T3.name), (bt1[:].rearrange("p a b -> p (a b)"), "bt1")],
                    [(T4[:].rearrange("p a b c -> p (a b c)"), T4.name), (bt2[:].rearrange("p a b -> p (a b)"), "bt2")]]
            for ci in range(2):
                A("dve", lambda e, ci=ci: e.tensor_copy(out=screpv[ci], in_=scm[:, ci, :].unsqueeze(2).to_broadcast([128, 8, 128])), r=["scm"], w=["EN"])

            def emit_mod_dma(idx):
                if idx >= 24:
                    return
                layer, n = divmod(idx, 12)
                slot = idx % NSL
                wv, wk = wslots[slot], wkeys[slot]
                A("pool", lambda e, wv=wv, layer=layer, n=n: e.dma_start(
                    out=wv, in_=dr["wmod"][layer][:, n * 256:(n + 1) * 256].rearrange("(k p) n -> p k n", p=128)), w=[wk], dk=f"mw{slot}")

            def emit_mod_chunk(idx):
                layer, n = divmod(idx, 12)
                slot = idx % NSL
                wv, wk = wslots[slot], wkeys[slot]
                if n % 2 == 0:
                    A("sp", lambda e, layer=layer, n=n: e.dma_start(out=biasf, in_=dr["bmod"][layer][(n // 2) * 512:(n // 2 + 1) * 512].partition_broadcast(128)),
                      w=["EP"], dk="mbias")
                for ci in range(2):
                    p = pmod[ci]
                    for kt in range(8):
                        A("pe", lambda e, kt=kt, p=p, wv=wv, ci=ci: e.matmul(p[:], lhsT=screpv[ci][:, kt, :], rhs=wv[:, kt, :], start=(kt == 0), stop=(kt == 7)),
                          r=["EN", wk], w=[f"pmod{ci}"])
                    c0 = n * 256
                    rb, rk = rows[ci][0] if c0 < 2048 else rows[ci][1]
                    cc0 = c0 if c0 < 2048 else c0 - 2048
                    A("dve", lambda e, p=p, rb=rb, cc0=cc0, n=n: e.tensor_tensor(out=rb[:, cc0:cc0 + 256], in0=p[:], in1=biasf[:, (n % 2) * 256:(n % 2) * 256 + 256], op=ALU.add),
                      r=[f"pmod{ci}", "EP"], w=[rk])
                emit_mod_dma(idx + NSL)
                if n == 11:
                    for ci in range(2):
                        (ra, rak), (rb2, rbk) = rows[ci]
                        A("sp", lambda e, ra=ra, layer=layer, ci=ci: e.dma_start(out=dr["s_mod"][layer, ci:ci + 1, 0:2048], in_=ra[0:1, :]),
                          r=[rak], w=["s_mod"], dk="smw")
                        A("sp", lambda e, rb2=rb2, layer=layer, ci=ci: e.dma_start(out=dr["s_mod"][layer, ci:ci + 1, 2048:3072], in_=rb2[0:1, :]),
                          r=[rbk], w=["s_mod"], dk="smw")
            for i0_ in range(NSL):
                emit_mod_dma(i0_)
        mod_next = [0]
        mF4 = maskF[:].unsqueeze(1).to_broadcast([128, 4, 128])
        mB4 = maskB[:].unsqueeze(1).to_broadcast([128, 4, 128])
        T1v = T1[:].rearrange("q a b c -> q (a b c)")[:, 0:512].rearrange("q (g n) -> q g n", n=128)
        T2v = T2[:].rearrange("q a b c -> q (a b c)")[:, 0:512].rearrange("q (g n) -> q g n", n=128)
        for blk in range(16):
            for gl in range(4):
                g = 4 * blk + gl
                for (pp, lo) in [(pMf, 0), (pMb, 64)]:
                    A("pe", lambda e, pp=pp, lo=lo, g=g, gl=gl: e.matmul(pp[:, gl, :], lhsT=PRb[lo:lo + 64, g, :], rhs=Qrb[lo:lo + 64, g, :],
                                                                       start=True, stop=False),
                      r=["PRb", "Qrb"], w=[pp.name])
                    A("pe", lambda e, pp=pp, lo=lo, g=g, gl=gl: e.matmul(pp[:, gl, :], lhsT=PIb[lo:lo + 64, g, :], rhs=nQib[lo:lo + 64, g, :],
                                                                       start=False, stop=True),
                      r=["PIb", "nQib"], w=[pp.name])
            A("dve", lambda e: e.tensor_tensor(out=T1v, in0=pMf[:], in1=mF4, op=ALU.mult), r=[pMf.name, "maskF"], w=[T1.name])
            A("dve", lambda e: e.tensor_tensor(out=T2v, in0=pMb[:], in1=mB4, op=ALU.mult), r=[pMb.name, "maskB"], w=[T2.name])
            A("dve", lambda e: e.tensor_tensor(out=T1v, in0=T1v, in1=T2v, op=ALU.add), r=[T1.name, T2.name], w=[T1.name])
            for gl in range(4):
                g = 4 * blk + gl
                A("dve", lambda e, g=g, gl=gl: e.scalar_tensor_tensor(out=Mop[:, g, :], in0=identf[:], scalar=Dcol[:, g:g + 1], in1=T1v[:, gl, :],
                                                                      op0=ALU.mult, op1=ALU.add), r=["identf", "Dcol", T1.name], w=["Mop"])
            if do_mod:
                for _ in range(2):
                    if mod_next[0] < 24:
                        emit_mod_chunk(mod_next[0])
                        mod_next[0] += 1
        A("sp", lambda e: e.dma_start(out=dr["s_opM"], in_=Mop[:]), r=["Mop"], w=["s_opM"], dk="s_opM")
        run_block(nc, tr, tr.all_tokens())
    return tr


class Main:
    def __init__(self, nc, dr, sems, pre_tokens, debug=None):
        self.nc = nc
        self.dr = dr
        self.tr = Tracker(sems)
        self.pre = pre_tokens
        self.rot = {}
        self.out_tokens = []
        self.debug = debug or {}

    def A(self, eng, fn, r=(), w=(), dk=None):
        return self.tr.add(eng, fn, reads=r, writes=w, dma_key=dk)

    def nxt(self, name, n):
        i = self.rot.get(name, 0)
        self.rot[name] = (i + 1) % n
        return i

    def alloc(self, es):
        nc = self.nc

        def sb(name, shape, dt):
            return es.enter_context(nc.sbuf_tensor("m_" + name, shape, dt))

        def ps(name, shape, dt):
            return es.enter_context(nc.psum_tensor("m_" + name, shape, dt))
        self.hT = sb("hT", [128, 8, 1024], BF16)
        self.AB = sb("AB", [128, 16384], BF16)
        self.C = sb("C", [128, 16384], BF16)
        self.T = sb("T", [128, 16384], BF16)
        self.wr = [sb(f"wr{i}", [128, 2048], BF16) for i in range(6)]
        self.mrep = sb("mrep", [128, 3072], F32)
        self.lngr = sb("lngr", [128, 1024], F32)
        self.lnbr = sb("lnbr", [128, 1024], F32)
        self.xt = [sb(f"xt{i}", [128, 1024], F32) for i in range(2)]
        self.xn = [sb(f"xn{i}", [128, 1024], F32) for i in range(2)]
        self.hb = [sb(f"hb{i}", [128, 1024], BF16) for i in range(2)]
        self.tmp = [sb(f"tmp{i}", [128, 512], F32) for i in range(4)]
        self.ost = [sb(f"ost{i}", [128, 1024], F32) for i in range(2)]
        self.statsT = sb("statsT", [128, 2, 8, 2, 6], F32)
        self.mvT = sb("mvT", [128, 2, 8, 2], F32)
        self.rstdT = sb("rstdT", [128, 2, 8], F32)
        self.nmrT = sb("nmrT", [128, 2, 8], F32)
        self.identf = sb("identf", [128, 128], F32)
        self.identb = sb("identb", [128, 128], BF16)
        self.onesb = sb("onesb", [128, 128], BF16)
        self.onesf = sb("onesf", [128, 128], F32)
        self.pswf = sb("pswf", [128, 128], F32)
        self.pswb = sb("pswb", [128, 128], BF16)
        self.A8 = sb("A8", [128, 2, 64], F32)
        self.nlam = sb("nlam", [128, 1], F32)
        self.subg = sb("subg", [128, 1], F32)
        self.bglu = sb("bglu", [128, 8], F32)
        self.bfno = sb("bfno", [128, 16], F32)
        self.cc = sb("cc", [128, 8], F32)
        self.sc = sb("sc", [128, 8], F32)
        self.screp = sb("screp", [128, 8, 128], BF16)
        self.st = sb("st", [128, 2, 32, 4], F32)
        self.st2 = sb("st2", [128, 2, 32, 4], F32)
        self.zz = sb("zz", [128, 2, 32, 4], F32)
        self.q12 = sb("q12", [128, 2, 2, 32, 4], F32)
        self.a8 = sb("a8", [128, 2, 2, 32, 4], F32)
        self.epsc = sb("epsc", [128, 1], F32)
        self.fin = sb("fin", [128, 2, 4, 64], F32)
        self.pf = [ps(f"pf{i}", [128, 512], F32) for i in range(6)]
        self.pt = [ps(f"pt{i}", [128, 8, 128], BF16) for i in range(2)]

    def load_consts(self):
        A, dr = self.A, self.dr
        A("sp", lambda e: e.dma_start(out=self.identf[:], in_=dr["ident"]), w=["identf"], dk="identf")
        A("dve", lambda e: e.tensor_copy(out=self.identb[:], in_=self.identf[:]), r=["identf"], w=["identb"])
        A("dve", lambda e: e.memset(self.onesb[:], 1.0), w=["onesb"])
        A("dve", lambda e: e.memset(self.epsc[:], LN_EPS), w=["epsc"])
        A("dve", lambda e: e.memset(self.onesf[:], 1.0 / 128), w=["onesf"])
        A("sp", lambda e: e.dma_start(out=self.pswf[:], in_=dr["pswap"]), w=["pswf"], dk="pswf")
        A("dve", lambda e: e.tensor_copy(out=self.pswb[:], in_=self.pswf[:]), r=["pswf"], w=["pswb"])
        A("sp", lambda e: e.dma_start(out=self.A8[:], in_=dr["s_A8"]), w=["A8"], dk="A8")
        A("sp", lambda e: e.dma_start(out=self.nlam[:], in_=dr["s_nlam"]), w=["nlam"], dk="nlam")
        A("sp", lambda e: e.dma_start(out=self.subg[:], in_=dr["subg"].rearrange("(p o) -> p o", o=1)), w=["subg"], dk="subg")
        A("dve", lambda e: e.tensor_scalar_mul(out=self.subg[:], in0=self.subg[:], scalar1=0.8), r=["subg"], w=["subg"])
        A("sp", lambda e: e.dma_start(out=self.bglu[:], in_=dr["bglu"].rearrange("(k p) -> p k", p=128), allow_slow_non_contiguous=True),
          w=["bglu"], dk="bglu")
        A("sp", lambda e: e.dma_start(out=self.bfno[:], in_=dr["bfno"].rearrange("(k p) -> p k", p=128), allow_slow_non_contiguous=True),
          w=["bfno"], dk="bfno")

    def wload(self, src2d, ktiles, ncols):
        i = self.nxt("wr", 6)
        assert ktiles * ncols <= 2048
        view = self.wr[i][:, 0:ktiles * ncols].rearrange("p (k n) -> p k n", n=ncols)
        srcv = src2d.rearrange("(k p) n -> p k n", p=128)
        self.A("pool", lambda e: e.dma_start(out=view, in_=srcv), w=[f"wr{i}"], dk=f"wr{i}")
        return view, f"wr{i}"

    def mod_vectors(self, layer, cond_ap):
        A, dr = self.A, self.dr
        A("sp", lambda e: e.dma_start(out=self.cc[:], in_=cond_ap.rearrange("(k p) -> p k", p=128), allow_slow_non_contiguous=True),
          w=["cc"], dk="cc")
        A("act", lambda e: e.activation(out=self.sc[:], in_=self.cc[:], func=AF.Silu), r=["cc"], w=["sc"])
        A("dve", lambda e: e.tensor_copy(out=self.screp[:], in_=self.sc[:].unsqueeze(2).to_broadcast([128, 8, 128])), r=["sc"], w=["screp"])
        for n in range(12):
            wv, wk = self.wload(dr["wmod"][layer][:, n * 256:(n + 1) * 256], 8, 256)
            pi = self.nxt("pf01", 2)
            p = self.pf[pi]
            if n % 2 == 0:
                nn = n // 2
                A("sp", lambda e, nn=nn: e.dma_start(out=self.tmp[3][:], in_=dr["bmod"][layer][nn * 512:(nn + 1) * 512].partition_broadcast(128)),
                  w=["tmp3"], dk="brep")
            for kt in range(8):
                A("pe", lambda e, kt=kt, p=p, wv=wv: e.matmul(p[:, 0:256], lhsT=self.screp[:, kt, :], rhs=wv[:, kt, :], start=(kt == 0), stop=(kt == 7)),
                  r=["screp", wk], w=[f"pf{pi}"])
            A("dve", lambda e, n=n, p=p: e.tensor_tensor(out=self.mrep[:, n * 256:(n + 1) * 256], in0=p[:, 0:256],
                                                         in1=self.tmp[3][:, (n % 2) * 256:(n % 2) * 256 + 256], op=ALU.add),
              r=[f"pf{pi}", "tmp3"], w=["mrep"])
        A("dve", lambda e: e.tensor_scalar_add(out=self.mrep[:, 1024:2048], in0=self.mrep[:, 1024:2048], scalar1=1.0), r=["mrep"], w=["mrep"])

    def precompute_mod(self):
        A, dr = self.A, self.dr
        screps = [self.screp[:], self.hb[1][:].rearrange("p (k n) -> p k n", n=128)]
        skeys = ["screp", "hb1"]
        for ci, cond_ap in enumerate([dr["cctx"], dr["csmp"]]):
            A("sp", lambda e, cond_ap=cond_ap: e.dma_start(out=self.cc[:], in_=cond_ap.rearrange("(k p) -> p k", p=128), allow_slow_non_contiguous=True),
              w=["cc"], dk="cc")
            A("act", lambda e: e.activation(out=self.sc[:], in_=self.cc[:], func=AF.Silu), r=["cc"], w=["sc"])
            A("dve", lambda e, ci=ci: e.tensor_copy(out=screps[ci], in_=self.sc[:].unsqueeze(2).to_broadcast([128, 8, 128])), r=["sc"], w=[skeys[ci]])
        def rowdst(ci, n):
            c0 = n * 256
            if ci == 0:
                return self.mrep[:, c0:c0 + 256], "mrep"
            t, k = [(self.xn[0], "xn0"), (self.xn[1], "xn1"), (self.xt[0], "xt0")][c0 // 1024]
            return t[:, c0 % 1024:c0 % 1024 + 256], k
        for layer in range(2):
            for n in range(12):
                wv, wk = self.wload(dr["wmod"][layer][:, n * 256:(n + 1) * 256], 8, 256)
                if n % 2 == 0:
                    nn = n // 2
                    A("sp", lambda e, nn=nn, layer=layer: e.dma_start(out=self.tmp[3][:], in_=dr["bmod"][layer][nn * 512:(nn + 1) * 512].partition_broadcast(128)),
                      w=["tmp3"], dk="brep")
                for ci in range(2):
                    pi = self.nxt("pf01", 2)
                    p = self.pf[pi]
                    for kt in range(8):
                        A("pe", lambda e, kt=kt, p=p, wv=wv, ci=ci: e.matmul(p[:, 0:256], lhsT=screps[ci][:, kt, :], rhs=wv[:, kt, :], start=(kt == 0), stop=(kt == 7)),
                          r=[skeys[ci], wk], w=[f"pf{pi}"])
                    dst, dk_ = rowdst(ci, n)
                    A("dve", lambda e, n=n, p=p, dst=dst: e.tensor_tensor(out=dst, in0=p[:, 0:256], in1=self.tmp[3][:, (n % 2) * 256:(n % 2) * 256 + 256], op=ALU.add),
                      r=[f"pf{pi}", "tmp3"], w=[dk_])
            A("sp", lambda e, layer=layer: e.dma_start(out=dr["s_mod"][layer, 0:1, :], in_=self.mrep[0:1, :]), r=["mrep"], w=["s_mod"], dk="smw0")
            for j, (t, k) in enumerate([(self.xn[0], "xn0"), (self.xn[1], "xn1"), (self.xt[0], "xt0")]):
                A("sp", lambda e, layer=layer, j=j, t=t: e.dma_start(out=dr["s_mod"][layer, 1:2, j * 1024:(j + 1) * 1024], in_=t[0:1, :]),
                  r=[k], w=["s_mod"], dk=f"smw{j + 1}")

    def load_mod(self, layer, ci):
        A, dr = self.A, self.dr
        A("sp", lambda e: e.dma_start(out=self.mrep[:], in_=dr["s_mod"][layer, ci].partition_broadcast(128)), r=["s_mod"], w=["mrep"], dk="mrepld")
        A("dve", lambda e: e.tensor_scalar_add(out=self.mrep[:, 1024:2048], in0=self.mrep[:, 1024:2048], scalar1=1.0), r=["mrep"], w=["mrep"])


    def ln_stats_t(self, src, skey, tt, S):
        A = self.A
        k = f"ln{S}_{tt}"
        stats, mv = self.statsT[:, S, tt], self.mvT[:, S, tt]
        rstd, nmr = self.rstdT[:, S, tt:tt + 1], self.nmrT[:, S, tt:tt + 1]
        for c in range(2):
            A("dve", lambda e, c=c: e.bn_stats(out=stats[:, c, :], in_=src[:, c * 512:(c + 1) * 512]), r=[skey], w=[k + "s"])
        A("dve", lambda e: e.bn_aggr(out=mv, in_=stats), r=[k + "s"], w=[k + "m"])
        A("act", lambda e: e.activation(out=rstd, in_=mv[:, 1:2], func=AF.Sqrt, bias=LN_EPS, scale=1.0), r=[k + "m"], w=[k + "r"])
        A("dve", lambda e: e.reciprocal(out=rstd, in_=rstd), r=[k + "r"], w=[k + "r"])
        A("dve", lambda e: e.tensor_scalar(out=nmr, in0=mv[:, 0:1], scalar1=-1.0, scalar2=rstd, op0=ALU.mult, op1=ALU.mult),
          r=[k + "m", k + "r"], w=[k + "n"])
        return rstd, nmr, [k + "r", k + "n"]

    def ln_mod_tiles(self, tiles, st=None):
        A = self.A
        if st is None:
            st = [self.ln_stats_t(src, key, tt, 1) for (src, key, tt) in tiles]
        n = len(tiles)
        info = {}

        def norm(i):
            (src, key, tt), (rstd, nmr, lk) = tiles[i], st[i]
            xb = self.nxt("xn", 2)
            xn = self.xn[xb]
            A("act", lambda e, xn=xn, src=src, nmr=nmr, rstd=rstd: e.activation(out=xn[:], in_=src, func=AF.Identity, bias=nmr, scale=rstd),
              r=[key] + lk, w=[f"xn{xb}"])
            info[i] = (xn, xb)
        norm(0)
        for i in range(n):
            (src, key, tt) = tiles[i]
            xn, xb = info[i]
            b = self.nxt("hb", 2)
            hb = self.hb[b]
            A("dve", lambda e, xn=xn: e.tensor_tensor(out=xn[:], in0=xn[:], in1=self.mrep[:, 1024:2048], op=ALU.mult), r=[f"xn{xb}", "mrep"], w=[f"xn{xb}"])
            A("dve", lambda e, xn=xn, hb=hb: e.tensor_tensor(out=hb[:], in0=xn[:], in1=self.mrep[:, 0:1024], op=ALU.add), r=[f"xn{xb}", "mrep"], w=[f"hb{b}"])
            if i + 1 < n:
                norm(i + 1)
            pi = self.nxt("pt", 2)
            pt = self.pt[pi]
            for kt in range(8):
                A("pe", lambda e, kt=kt, pt=pt, hb=hb: e.transpose(pt[:, kt, :], hb[:, kt * 128:(kt + 1) * 128], self.identb[:]),
                  r=[f"hb{b}", "identb"], w=[f"pt{pi}"])
            A("act", lambda e, pt=pt, tt=tt: e.activation(out=self.hT[:, :, tt * 128:(tt + 1) * 128], in_=pt[:], func=AF.Copy), r=[f"pt{pi}"], w=["hT"])

    def fence(self, reads, writes):
        if not hasattr(self, "_dummy"):
            raise RuntimeError("alloc dummy first")
        self.A("dve", lambda e: e.memset(self._dummy[:], 0.0), r=list(reads), w=list(writes) + ["_dummy"])

    REG = {
        "AB": ["A", "gaA", "attA", "mixA"] + [f"Z{t}" for t in range(8)] + [ "ropeC", "ropeS", "qraw", "gz0", "gz1", "vh", "qT", "kT"] + [f"PT{i}" for i in range(4)] + [f"B{k}" for k in range(8)],
        "C": ["CX", "Ctab"] + [f"C{k}" for k in range(16)],
        "T": ["Wb0", "Wb1", "MQb0", "MQb1", "MQb0q", "MQb1q", "Sb0", "Sb1", "qT", "kT", "cs256", "ucs", "uT0", "uT1"],
    }

    def rfence(self, reg):
        ks = self.REG[reg]
        self.fence(ks, ks)

    def alloc2(self, es):
        self._dummy = es.enter_context(self.nc.sbuf_tensor("m_dummy", [128, 2], F32))

    def s5_path(self, grp):
        A, dr = self.A, self.dr
        isS = (grp == "S")
        ns = 1 if isS else 4
        NS = 128 // ns
        ua = self.AB[:, 0:8192].rearrange("p (g s m) -> p g s m", s=8, m=16)
        ga_t = self.AB[:, 0:8192].rearrange("p (j n) -> p j n", n=1024)
        ug = self.AB[:, 8192:16384].rearrange("p (g c) -> p g c", c=128)
        gaT = self.AB[:, 8192:16384].rearrange("p (k t) -> p k t", t=1024)
        Xp = self.C[:].bitcast(F32).rearrange("p (r g c) -> p r g c", r=2, g=32)
        Wb = [self.T[:, i * 1024:(i + 1) * 1024].rearrange("p (g r n) -> p g r n", g=4, r=2) for i in range(2)]
        MQb = [self.T[:, 2048 + i * 1536:2048 + (i + 1) * 1536] for i in range(2)]
        Sb = self.T[:, 5120:5120 + 8192].rearrange("p (r g c) -> p r g c", r=2, g=32)
        ALLB = [f"B{k}" for k in range(8)]
        pre = [self.wload(dr["win_e"][:, cb * 256:(cb + 1) * 256], 8, 256) for cb in range(4)]
        self.rfence("AB"); self.rfence("C"); self.rfence("T")
        for cb in range(4):
            wv, wk = pre[cb]
            for s in range(8):
                pi = self.nxt("pf01", 2)
                p = self.pf[pi]
                for kt in range(8):
                    A("pe", lambda e, kt=kt, s=s, p=p, wv=wv: e.matmul(p[:, 0:256], lhsT=self.hT[:, kt, s:1024:8], rhs=wv[:, kt, :],
                                                                      start=(kt == 0), stop=(kt == 7)), r=["hT", wk], w=[f"pf{pi}"])
                eng = "act" if s % 2 == 0 else "dve"
                if eng == "act":
                    A("act", lambda e, s=s, cb=cb, p=p: e.activation(out=ua[:, cb * 16:(cb + 1) * 16, s, :],
                                                                     in_=p[:, 0:256].rearrange("q (g m) -> q g m", m=16), func=AF.Copy),
                      r=[f"pf{pi}"], w=["A"])
                else:
                    A("dve", lambda e, s=s, cb=cb, p=p: e.tensor_copy(out=ua[:, cb * 16:(cb + 1) * 16, s, :],
                                                                      in_=p[:, 0:256].rearrange("q (g m) -> q g m", m=16)),
                      r=[f"pf{pi}"], w=["A"])
        for blk in range(8):
            pi = self.nxt("pt", 2)
            pt = self.pt[pi]
            for gl in range(8):
                g = 8 * blk + gl
                A("pe", lambda e, gl=gl, g=g, pt=pt: e.transpose(pt[:, gl, :], ua[:, g, :, :].rearrange("q s m -> q (s m)"), self.identb[:]),
                  r=["A", "identb"], w=[f"pt{pi}"])
            A("act", lambda e, blk=blk, pt=pt: e.activation(out=ug[:, 8 * blk:8 * blk + 8, :], in_=pt[:], func=AF.Copy),
              r=[f"pt{pi}"], w=[f"B{blk}"])
        self.fence(["A"], ["gaA"])
        for half in range(2):
            g0 = 32 * half
            self.fence([f"C{k}" for k in range(16)], ["CX"])
            for blk in range(8):
                bi = self.nxt("Wb", 2)
                A("sp", lambda e, bi=bi, blk=blk, g0=g0: e.dma_start(out=Wb[bi], in_=dr["s_opW"][:, g0 + 4 * blk:g0 + 4 * blk + 4]),
                  w=[f"Wb{bi}"], dk=f"Wb{bi}")
                for ri in range(2):
                    pi = 2 + self.nxt("pf23", 2)
                    p = self.pf[pi]
                    for gl in range(4):
                        g = g0 + 4 * blk + gl
                        A("pe", lambda e, p=p, gl=gl, g=g, ri=ri, bi=bi: e.matmul(p[:, gl * 128:(gl + 1) * 128], lhsT=Wb[bi][:, gl, ri, :],
                                                                                  rhs=ug[:, g, :], start=True, stop=True),
                          r=[f"Wb{bi}", f"B{g // 8}"], w=[f"pf{pi}"])
                    A("act", lambda e, p=p, ri=ri, blk=blk: e.activation(out=Xp[0:64, ri, 4 * blk:4 * blk + 4, :],
                                                                         in_=p[0:64, :].rearrange("q (g c) -> q g c", c=128), func=AF.Copy),
                      r=[f"pf{pi}"], w=["CX"])
                    A("act", lambda e, p=p, ri=ri, blk=blk: e.activation(
                        out=Xp[64:128, ri, 4 * blk:4 * blk + 4, :].rearrange("q g (s c) -> q g s c", s=ns),
                        in_=p[64:128, :].rearrange("q (g s c) -> q g s c", g=4, s=ns)[:, :, :, ::-1], func=AF.Copy),
                      r=[f"pf{pi}"], w=["CX"])
            a8r_b = self.A8[:, 0, g0:g0 + 32].unsqueeze(1).unsqueeze(3).to_broadcast([128, 2, 32, ns])
            A("dve", lambda e, a8r_b=a8r_b: e.tensor_copy(out=self.a8[:, 0, :, :, 0:ns], in_=a8r_b), r=["A8"], w=["a8"])
            a8i_b = self.A8[:, 1, g0:g0 + 32].unsqueeze(2).to_broadcast([128, 32, ns])
            A("dve", lambda e, a8i_b=a8i_b: e.tensor_copy(out=self.a8[:, 1, 0, :, 0:ns], in_=a8i_b), r=["A8"], w=["a8"])
            A("dve", lambda e: e.tensor_scalar_mul(out=self.a8[:, 1, 1, :, 0:ns], in0=self.a8[:, 1, 0, :, 0:ns], scalar1=-1.0), r=["a8"], w=["a8"])
            if isS:
                for (ri, nm) in [(0, "st_re"), (1, "st_im")]:
                    for d in range(2):
                        A("sp", lambda e, ri=ri, nm=nm, d=d, g0=g0: e.dma_start(
                            out=self.st[64 * d:64 * d + 64, ri, :, 0:1],
                            in_=dr[nm][d, g0:g0 + 32, :].rearrange("g (p o) -> p g o", o=1), allow_slow_non_contiguous=True),
                          w=["st"], dk=f"st{ri}{d}")
            else:
                A("dve", lambda e: e.memset(self.st[:], 0.0), w=["st"])
            def cs(k):
                return slice(k, 128, NS) if ns > 1 else slice(k, k + 1)

            def csb(k):
                return cs(NS - 1 - k)
            stb = [self.st, self.st2]
            zz = self.zz[:, :, :, 0:ns]
            zsw = self.zz[:, ::-1, :, 0:ns]
            q12 = self.q12[:, :, :, :, 0:ns]
            q1 = self.q12[:, 0, :, :, 0:ns]
            q2s = self.q12[:, 1, ::-1, :, 0:ns]
            a8v = self.a8[:, :, :, :, 0:ns]
            zz2 = self.zz[:, :, :, 0:ns].unsqueeze(1).to_broadcast([128, 2, 2, 32, ns])

            def store(k, sbuf, skey):
                cf, cb_ = cs(k), csb(k)
                A("act", lambda e, cf=cf, sbuf=sbuf: e.activation(out=Sb[0:64, :, :, cf], in_=sbuf[0:64, :, :, 0:ns], func=AF.Copy), r=[skey], w=["Sb0"])
                A("pool", lambda e, cb_=cb_, sbuf=sbuf: e.tensor_copy(out=Sb[64:128, :, :, cb_], in_=sbuf[64:128, :, :, 0:ns]), r=[skey], w=["Sb1"])
            store(0, self.st, "st")
            A("dve", lambda e, c0=cs(0): e.tensor_tensor(out=zz, in0=self.st[:, :, :, 0:ns], in1=Xp[:, :, :, c0], op=ALU.add), r=["st", "CX"], w=["zz"])
            cur, curk = self.st, "st"
            for k in range(NS):
                nxt_, nxtk = (self.st2, "st2") if cur is self.st else (self.st, "st")
                if k == NS - 1:
                    nxt_, nxtk = self.st, "st"
                    if cur is self.st:
                        pass
                A("dve", lambda e: e.tensor_tensor(out=q12, in0=a8v, in1=zz2, op=ALU.mult), r=["a8", "zz"], w=["q12"])
                A("dve", lambda e, nxt_=nxt_: e.tensor_tensor(out=nxt_[:, :, :, 0:ns], in0=q1, in1=q2s, op=ALU.add), r=["q12"], w=[nxtk])
                cur, curk = nxt_, nxtk
                if k < NS - 1:
                    store(k + 1, cur, curk)
                    A("dve", lambda e, cn=cs(k + 1), cur=cur: e.tensor_tensor(out=zz, in0=cur[:, :, :, 0:ns], in1=Xp[:, :, :, cn], op=ALU.add), r=[curk, "CX"], w=["zz"])
            if not isS:
                A("act", lambda e, g0=g0: e.activation(out=self.fin[:, :, :, g0:g0 + 32].rearrange("q r s g -> q r g s"),
                                                     in_=self.st[:, :, :, 0:4], func=AF.Copy), r=["st"], w=["fin"])
            ga = ga_t
            for blk in range(8):
                bi = self.nxt("MQb", 2)
                Mv = MQb[bi][:, 0:512].rearrange("p (g n) -> p g n", n=128)
                Qv = MQb[bi][:, 512:1536].rearrange("p (g r n) -> p g r n", g=4, r=2)
                gq = g0 + 4 * blk
                A("sp", lambda e, Mv=Mv, gq=gq: e.dma_start(out=Mv, in_=dr["s_opM"][:, gq:gq + 4]), w=[f"MQb{bi}"], dk=f"Mb{bi}")
                A("sp", lambda e, Qv=Qv, gq=gq: e.dma_start(out=Qv, in_=dr["s_opQ"][:, gq:gq + 4]), w=[f"MQb{bi}q"], dk=f"Qb{bi}")
                pi = self.nxt("pf01", 2)
                p = self.pf[pi]
                pv = p[:].rearrange("q (g j n) -> q g j n", j=8, n=16)
                for gl in range(4):
                    g = gq + gl
                    gh = g - g0
                    A("pe", lambda e, p=p, gl=gl, g=g, Mv=Mv: e.matmul(p[:, gl * 128:(gl + 1) * 128], lhsT=ug[:, g, :],
                                                                        rhs=Mv[:, gl, :], start=True, stop=False),
                      r=[f"B{g // 8}", f"MQb{bi}"], w=[f"pf{pi}"])
                    for ri in range(2):
                        A("pe", lambda e, p=p, gl=gl, gh=gh, ri=ri, Qv=Qv: e.matmul(p[:, gl * 128:(gl + 1) * 128], lhsT=Sb[:, ri, gh, :],
                                                                                     rhs=Qv[:, gl, ri, :],
                                                                                     start=False, stop=(ri == 1)),
                          r=["Sb0", "Sb1", f"MQb{bi}q"], w=[f"pf{pi}"])
                t0, t1 = self.tmp[0], self.tmp[1]
                A("act", lambda e, p=p: e.activation(out=t0[:], in_=p[:], func=AF.Square), r=[f"pf{pi}"], w=["tmp0"])
                A("dve", lambda e: e.tensor_scalar(out=t0[:], in0=t0[:], scalar1=0.044715, scalar2=1.0, op0=ALU.mult, op1=ALU.add), r=["tmp0"], w=["tmp0"])
                A("dve", lambda e, p=p: e.tensor_tensor(out=t0[:], in0=t0[:], in1=p[:], op=ALU.mult), r=["tmp0", f"pf{pi}"], w=["tmp0"])
                A("act", lambda e: e.activation(out=t1[:], in_=t0[:], func=AF.Sigmoid, scale=GELU_C), r=["tmp0"], w=["tmp1"])
                gav = ga[:, :, gq * 16:gq * 16 + 64].rearrange("q j (g n) -> q g j n", n=16)
                A("dve", lambda e, pv=pv, gav=gav: e.tensor_tensor(out=gav, in0=t1[:].rearrange("q (g j n) -> q g j n", j=8, n=16), in1=pv, op=ALU.mult),
                  r=["tmp1", f"pf{pi}"], w=["gaA"])
                if blk % 2 == 1:
                    kt = gq // 8
                    pti = self.nxt("pt", 2)
                    pt = self.pt[pti]
                    for j in range(8):
                        A("pe", lambda e, pt=pt, j=j, kt=kt: e.transpose(pt[:, j, :], ga[:, j, kt * 128:(kt + 1) * 128], self.identb[:]),
                          r=["gaA", "identb"], w=[f"pt{pti}"])
                    A("act", lambda e, pt=pt, kt=kt: e.activation(out=gaT[:, kt, :].rearrange("q (c j) -> q j c", j=8), in_=pt[:], func=AF.Copy),
                      r=[f"pt{pti}"], w=[f"B{kt}"])
            self.fence(["CX"], [f"C{k}" for k in range(16)])
        if not isS:
            for (ri, nm) in [(0, "fre"), (1, "fim")]:
                for sp2 in range(2):
                    pj = 2 + self.nxt("pf23", 2)
                    p2 = self.pf[pj]
                    A("pe", lambda e, p2=p2, ri=ri, sp2=sp2: e.transpose(p2[:, 0:128], self.fin[:, ri, 2 * sp2:2 * sp2 + 2, :].rearrange("q s g -> q (s g)"),
                                                                       self.identf[:]), r=["fin", "identf"], w=[f"pf{pj}"])
                    oi = self.nxt("ost", 2)
                    ost = self.ost[oi]
                    A("dve", lambda e, p2=p2, ost=ost: e.tensor_copy(out=ost[:, 0:128], in_=p2[:, 0:128]), r=[f"pf{pj}"], w=[f"ost{oi}"])
                    for sl in range(2):
                        sq = 2 * sp2 + sl
                        tk = A("sp", lambda e, ost=ost, sl=sl, sq=sq, nm=nm: e.dma_start(
                            out=dr[nm][sq].rearrange("d g p -> g d p"), in_=ost[64 * sl:64 * sl + 64, 0:128].rearrange("q (d p) -> q d p", p=64)),
                            r=[f"ost{oi}"], w=[nm], dk=f"ostd{oi}")
                        self.out_tokens.append(tk)

    def glu_path(self):
        A, dr = self.A, self.dr
        gaT = self.AB[:, 8192:16384].rearrange("p (k t) -> p k t", t=1024)
        yT = self.C[:].rearrange("p (k t) -> p k t", t=1024)
        ALLB = [f"B{k}" for k in range(8)]
        for ot in range(8):
            wg, wgk = self.wload(dr["wglu"][:, ot * 128:(ot + 1) * 128], 8, 128)
            wz, wzk = self.wload(dr["win_e"][:, 1024 + ot * 128:1024 + (ot + 1) * 128], 8, 128)
            for tb in range(2):
                ts = slice(tb * 512, (tb + 1) * 512)
                pi = self.nxt("pf01", 2)
                p = self.pf[pi]
                for kt in range(8):
                    A("pe", lambda e, p=p, kt=kt, wg=wg, ts=ts: e.matmul(p[:], lhsT=wg[:, kt, :], rhs=gaT[:, kt, ts], start=(kt == 0), stop=(kt == 7)),
                      r=[wgk] + ALLB, w=[f"pf{pi}"])
                pj = 2 + self.nxt("pf23", 2)
                p2 = self.pf[pj]
                for kt in range(8):
                    A("pe", lambda e, p2=p2, kt=kt, wz=wz, ts=ts: e.matmul(p2[:], lhsT=wz[:, kt, :], rhs=self.hT[:, kt, ts], start=(kt == 0), stop=(kt == 7)),
                      r=[wzk, "hT"], w=[f"pf{pj}"])
                t0, t1 = self.tmp[0], self.tmp[1]
                A("act", lambda e, p=p, ot=ot: e.activation(out=t0[:], in_=p[:], func=AF.Sigmoid, bias=self.bglu[:, ot:ot + 1], scale=1.0),
                  r=[f"pf{pi}", "bglu"], w=["tmp0"])
                A("act", lambda e, p2=p2: e.activation(out=t1[:], in_=p2[:], func=AF.Sigmoid), r=[f"pf{pj}"], w=["tmp1"])
                A("dve", lambda e, p2=p2: e.tensor_tensor(out=t1[:], in0=t1[:], in1=p2[:], op=ALU.mult), r=["tmp1", f"pf{pj}"], w=["tmp1"])
                A("dve", lambda e, ot=ot, ts=ts: e.tensor_tensor(out=t0[:], in0=t0[:], in1=gaT[:, ot, ts], op=ALU.mult), r=["tmp0", f"B{ot}"], w=["tmp0"])
                A("dve", lambda e, ot=ot, ts=ts: e.tensor_tensor(out=yT[:, ot, ts], in0=t0[:], in1=t1[:], op=ALU.mult), r=["tmp0", "tmp1"], w=[f"C{ot}"])

    def attn_path(self, grp):
        A, dr = self.A, self.dr
        isS = (grp == "S")
        yT = self.C[:].rearrange("p (k t) -> p k t", t=1024)
        TB = 13056
        qTm = [self.AB[:, 8192:9216], self.AB[:, 9216:10240]]
        kT = self.AB[:, 10240:10240 + 1280]
        vh = self.AB[:, 6144:6144 + 1280].rearrange("p (t e) -> p t e", e=128)
        gzb = [self.AB[:, 0:1024], self.AB[:, 11520:12544]]
        self._blk = 0
        self._pend = None
        qraw = self.AB[:, 1024:2048]
        PT = [self.AB[:, 2048 + i * 512:2048 + (i + 1) * 512] for i in range(4)]
        ropeC = self.AB[:, 4096:5120]
        ropeS = self.AB[:, 5120:6144]
        cst = self.xt[0]
        oacc = [self.tmp[2], self.tmp[3]]
        self.rfence("AB")
        A("dve", lambda e: e.memset(qTm[0][64:128, :], 0.0), w=["qT"])
        A("dve", lambda e: e.memset(qTm[1][0:64, :], 0.0), w=["qT"])
        nkt = 10 if isS else 2
        koff = 256 if isS else 0
        if isS:
            A("pool", lambda e: e.dma_start(out=ropeC, in_=dr["ropeC"]), w=["ropeC"], dk="ropeC")
            A("pool", lambda e: e.dma_start(out=ropeS, in_=dr["ropeS"]), w=["ropeS"], dk="ropeS")
        seqs = [(0, 1024)] if isS else [(i * 256, 256) for i in range(4)]
        for h in range(8):
            gz, gzk = gzb[h % 2], f"gz{h % 2}"
            wq, wqk = self.wload(dr["win_e"][:, 2048 + h * 128:2048 + (h + 1) * 128], 8, 128)
            wk_, wkk = self.wload(dr["win_e"][:, 3072 + h * 128:3072 + (h + 1) * 128], 8, 128)
            wv_, wvk = self.wload(dr["win_e"][:, 4096 + h * 128:4096 + (h + 1) * 128], 8, 128)
            wzb, wzbk = self.wload(dr["win_e"][:, 5120 + h * 128:5120 + (h + 1) * 128], 8, 128)
            for (wv, wkey, dst, dkey, off, kind) in [(wq, wqk, None, "qT", 0, "q"), (wk_, wkk, kT, "kT", koff, "k"), (wzb, wzbk, gz, gzk, 0, "z")]:
                for tb in range(2):
                    ts = slice(tb * 512, (tb + 1) * 512)
                    od = slice(off + tb * 512, off + (tb + 1) * 512)
                    pi = self.nxt("pf01", 2)
                    p = self.pf[pi]
                    for kt in range(8):
                        A("pe", lambda e, p=p, kt=kt, wv=wv, ts=ts: e.matmul(p[:], lhsT=wv[:, kt, :], rhs=self.hT[:, kt, ts], start=(kt == 0), stop=(kt == 7)),
                          r=[wkey, "hT"], w=[f"pf{pi}"])
                    if kind == "z":
                        A("act", lambda e, p=p, dst=dst, od=od: e.activation(out=dst[:, od], in_=p[:], func=AF.Silu), r=[f"pf{pi}"], w=[dkey, "attA"])
                    elif not isS and kind == "q":
                        for mm in range(2):
                            A("act", lambda e, p=p, od=od, mm=mm: e.activation(out=qTm[mm][64 * mm:64 * mm + 64, od], in_=p[64 * mm:64 * mm + 64, :], func=AF.Copy),
                              r=[f"pf{pi}"], w=[dkey])
                    elif not isS:
                        A("act", lambda e, p=p, dst=dst, od=od: e.activation(out=dst[:, od], in_=p[:], func=AF.Copy), r=[f"pf{pi}"], w=[dkey])
                    else:
                        A("act", lambda e, p=p, ts=ts: e.activation(out=qraw[:, 0:512], in_=p[:], func=AF.Copy), r=[f"pf{pi}"], w=["qraw", "attA"])
                        pj = 2 + self.nxt("pf23", 2)
                        p2 = self.pf[pj]
                        A("pe", lambda e, p2=p2: e.matmul(p2[:], lhsT=self.pswb[:], rhs=qraw[:, 0:512], start=True, stop=True),
                          r=["pswb", "qraw"], w=[f"pf{pj}"])
                        t0, t1 = self.tmp[0], self.tmp[1]
                        A("dve", lambda e, ts=ts: e.tensor_tensor(out=t0[:], in0=qraw[:, 0:512], in1=ropeC[:, ts], op=ALU.mult), r=["qraw", "ropeC"], w=["tmp0"])
                        A("dve", lambda e, p2=p2, ts=ts: e.tensor_tensor(out=t1[:], in0=p2[:], in1=ropeS[:, ts], op=ALU.mult), r=[f"pf{pj}", "ropeS"], w=["tmp1"])
                        if kind == "q":
                            for mm in range(2):
                                A("dve", lambda e, od=od, mm=mm: e.tensor_tensor(out=qTm[mm][64 * mm:64 * mm + 64, od], in0=t0[64 * mm:64 * mm + 64, :],
                                                                                in1=t1[64 * mm:64 * mm + 64, :], op=ALU.add), r=["tmp0", "tmp1"], w=[dkey])
                        else:
                            A("dve", lambda e, dst=dst, od=od: e.tensor_tensor(out=dst[:, od], in0=t0[:], in1=t1[:], op=ALU.add), r=["tmp0", "tmp1"], w=[dkey])
            if getattr(self, "att_lvl", 9) < 1:
                continue
            for half in range(2):
                pi = self.nxt("pf01", 2)
                p = self.pf[pi]
                for t4 in range(4):
                    tt = half * 4 + t4
                    for kt in range(8):
                        A("pe", lambda e, p=p, t4=t4, tt=tt, kt=kt, wv_=wv_: e.matmul(p[:, t4 * 128:(t4 + 1) * 128], lhsT=self.hT[:, kt, tt * 128:(tt + 1) * 128],
                                                                                     rhs=wv_[:, kt, :], start=(kt == 0), stop=(kt == 7)),
                          r=[wvk, "hT"], w=[f"pf{pi}"])
                vo = (2 if isS else 0) + half * 4
                A("act", lambda e, p=p, vo=vo: e.activation(out=vh[:, vo:vo + 4, :], in_=p[:].rearrange("q (t e) -> q t e", e=128), func=AF.Copy),
                  r=[f"pf{pi}", "attA"], w=["vh"])
                if not isS:
                    oi = self.nxt("ost", 2)
                    ost = self.ost[oi]
                    A("dve", lambda e, p=p, ost=ost: e.tensor_copy(out=ost[:, 0:512], in_=p[:]), r=[f"pf{pi}", "vh"], w=[f"ost{oi}"])
                    tk = A("sp", lambda e, ost=ost, half=half, h=h: e.dma_start(
                        out=dr["nv"][half * 512:(half + 1) * 512, h * 128:(h + 1) * 128].rearrange("(t p) e -> p t e", p=128),
                        in_=ost[:, 0:512].rearrange("q (t e) -> q t e", e=128)), r=[f"ost{oi}"], w=["nv"], dk=f"ostd{oi}")
                    self.out_tokens.append(tk)
                    pj = 2 + self.nxt("pf23", 2)
                    p2 = self.pf[pj]
                    for t4 in range(4):
                        tt = half * 4 + t4
                        for kt in range(8):
                            A("pe", lambda e, p2=p2, t4=t4, tt=tt, kt=kt, wk_=wk_: e.matmul(p2[:, t4 * 128:(t4 + 1) * 128], lhsT=self.hT[:, kt, tt * 128:(tt + 1) * 128],
                                                                                           rhs=wk_[:, kt, :], start=(kt == 0), stop=(kt == 7)),
                              r=[wkk, "hT"], w=[f"pf{pj}"])
                    A("dve", lambda e, p2=p2, ost=ost: e.tensor_copy(out=ost[:, 512:1024], in_=p2[:]), r=[f"pf{pj}"], w=[f"ost{oi}"])
                    tk = A("sp", lambda e, ost=ost, half=half, h=h: e.dma_start(
                        out=dr["nk"][half * 512:(half + 1) * 512, h * 128:(h + 1) * 128].rearrange("(t p) e -> p t e", p=128),
                        in_=ost[:, 512:1024].rearrange("q (t e) -> q t e", e=128)), r=[f"ost{oi}"], w=["nk"], dk=f"ostk{oi}")
                    self.out_tokens.append(tk)
            if isS:
                A("sp", lambda e, h=h: e.dma_start(out=cst[:, 0:256].rearrange("q (t e) -> q t e", e=128),
                                                   in_=dr["cache_k"][:, h].rearrange("(t p) m d -> p t (m d)", p=128)), w=["xt0"], dk="xt0")
                A("sp", lambda e, h=h: e.dma_start(out=cst[:, 256:512].rearrange("q (t e) -> q t e", e=128),
                                                   in_=dr["cache_v"][:, h].rearrange("(t p) e -> p t e", p=128)), w=["xt0"], dk="xt0b")
                A("dve", lambda e: e.tensor_copy(out=self.hb[0][:, 0:256], in_=cst[:, 0:256]), r=["xt0"], w=["hb0"])
                A("dve", lambda e: e.tensor_copy(out=vh[:, 0:2, :], in_=cst[:, 256:512].rearrange("q (t e) -> q t e", e=128)), r=["xt0", "attA"], w=["vh"])
                pti = self.nxt("pt", 2)
                pt = self.pt[pti]
                for t in range(2):
                    A("pe", lambda e, pt=pt, t=t: e.transpose(pt[:, t, :], self.hb[0][:, t * 128:(t + 1) * 128], self.identb[:]),
                      r=["hb0", "identb"], w=[f"pt{pti}"])
                A("act", lambda e, pt=pt: e.activation(out=kT[:, 0:256], in_=pt[:, 0:2, :].rearrange("q t c -> q (t c)"), func=AF.Copy),
                  r=[f"pt{pti}"], w=["kT"])
            if getattr(self, "att_lvl", 9) < 2:
                continue
            blocks = []
            if isS:
                for qb in range(2):
                    blocks.append((qb * 512, [(0, 512, [(kb * 128, kb) for kb in range(10)])]))
            else:
                for bb in range(2):
                    subs = []
                    for sl in range(2):
                        s0 = (2 * bb + sl) * 256
                        subs.append((sl * 256, 256, [(s0 + kb * 128, s0 // 128 + kb) for kb in range(2)]))
                    blocks.append((bb * 512, subs))
            accs = [(self.pf[4], "pf4", self.pf[5], "pf5"),
                    (self.pt[0][:].bitcast(F32).rearrange("q a b -> q (a b)"), "pt0", self.pt[1][:].bitcast(F32).rearrange("q a b -> q (a b)"), "pt1")]
            osets = [((self.tmp[2], "tmp2"), (self.tmp[3], "tmp3")),
                     ((self.xt[1][:, 0:512], "xt1"), (self.xt[1][:, 512:1024], "xt1"))]
            nq = 512

            def stageA(bi, q0, subs):
                oset = osets[bi % 2]
                work = []
                for m in range(2):
                    for (qoff, nqs, keys) in subs:
                        for ki, (kbase, vt) in enumerate(keys):
                            work.append((m, qoff, nqs, kbase, vt, ki == 0, ki == len(keys) - 1))
                last_of_map = {m: max(i for i, w in enumerate(work) if w[0] == m) for m in range(2)}
                sc_bank = {}

                def issue_scores(wi):
                    m, qoff, nqs, kbase, vt, first, last = work[wi]
                    pj = self.nxt("pf03", 4)
                    p2 = self.pf[pj]
                    sc_bank[wi] = (p2, pj)
                    A("pe", lambda e, p2=p2, kbase=kbase, m=m, qoff=qoff, nqs=nqs: e.matmul(
                        p2[:, 0:nqs], lhsT=kT[:, kbase:kbase + 128], rhs=qTm[m][:, q0 + qoff:q0 + qoff + nqs], start=True, stop=True),
                        r=["kT", "qT"], w=[f"pf{pj}"])
                for wi in range(min(3, len(work))):
                    issue_scores(wi)
                for wi, (m, qoff, nqs, kbase, vt, first, last) in enumerate(work):
                    if wi + 3 < len(work):
                        issue_scores(wi + 3)
                    po, pok, pz, pzk = accs[m]
                    p2, pj = sc_bank[wi]
                    pti = self.nxt("PT", 4)
                    A("act", lambda e, p2=p2, pti=pti, nqs=nqs: e.activation(out=PT[pti][:, 0:nqs], in_=p2[:, 0:nqs], func=AF.Exp, scale=0.125),
                      r=[f"pf{pj}"], w=[f"PT{pti}"])
                    A("pe", lambda e, po=po, vt=vt, pti=pti, qoff=qoff, nqs=nqs, first=first, last=last: e.matmul(
                        po[:, qoff:qoff + nqs], lhsT=vh[:, vt, :], rhs=PT[pti][:, 0:nqs], start=first, stop=last), r=["vh", f"PT{pti}"], w=[pok])
                    A("pe", lambda e, pz=pz, pti=pti, qoff=qoff, nqs=nqs, first=first, last=last: e.matmul(
                        pz[:, qoff:qoff + nqs], lhsT=self.onesb[:], rhs=PT[pti][:, 0:nqs], start=first, stop=last), r=["onesb", f"PT{pti}"], w=[pzk])
                    if wi == last_of_map[m]:
                        te = self.xn[m][:, 0:512]
                        tek = f"xn{m}"
                        oa, oak = oset[m]
                        A("act", lambda e, pz=pz, te=te: e.activation(out=te[:, 0:nq], in_=pz[:, 0:nq], func=AF.Ln), r=[pzk], w=[tek])
                        A("act", lambda e, te=te: e.activation(out=te[:, 0:nq], in_=te[:, 0:nq], func=AF.Exp, scale=-1.0), r=[tek], w=[tek])
                        A("dve", lambda e, po=po, te=te, oa=oa: e.tensor_tensor(out=oa[:, 0:nq], in0=po[:, 0:nq], in1=te[:, 0:nq], op=ALU.mult),
                          r=[pok, tek], w=[oak])

            def stageB(bi, q0, h=h, gz=gz, gzk=gzk):
                qs = slice(q0, q0 + 512)
                (o0, k0), (o1, k1) = osets[bi % 2]
                t0, t1 = self.tmp[0], self.tmp[1]
                A("dve", lambda e: e.scalar_tensor_tensor(out=o0[:, 0:nq], in0=o1[:, 0:nq], scalar=self.nlam[:, 0:1], in1=o0[:, 0:nq],
                                                          op0=ALU.mult, op1=ALU.add), r=[k0, k1, "nlam"], w=[k0])
                A("dve", lambda e: e.tensor_tensor(out=t0[:, 0:nq], in0=o0[:, 0:nq], in1=o0[:, 0:nq], op=ALU.mult), r=[k0], w=["tmp0"])
                pj = self.nxt("pf03", 4)
                p2 = self.pf[pj]
                A("pe", lambda e, p2=p2: e.matmul(p2[:, 0:nq], lhsT=self.onesf[:], rhs=t0[:, 0:nq], start=True, stop=True),
                  r=["onesf", "tmp0"], w=[f"pf{pj}"])
                A("act", lambda e, p2=p2: e.activation(out=t1[:, 0:nq], in_=p2[:, 0:nq], func=AF.Ln, bias=self.epsc[:, 0:1], scale=1.0), r=[f"pf{pj}", "epsc"], w=["tmp1"])
                A("act", lambda e: e.activation(out=t1[:, 0:nq], in_=t1[:, 0:nq], func=AF.Exp, scale=-0.5), r=["tmp1"], w=["tmp1"])
                A("dve", lambda e: e.scalar_tensor_tensor(out=t1[:, 0:nq], in0=t1[:, 0:nq], scalar=self.subg[:, 0:1], in1=o0[:, 0:nq],
                                                          op0=ALU.mult, op1=ALU.mult), r=["tmp1", "subg", k0], w=["tmp1"])
                A("dve", lambda e: e.tensor_tensor(out=yT[:, 8 + h, qs], in0=t1[:, 0:nq], in1=gz[:, qs], op=ALU.mult),
                  r=["tmp1", gzk], w=[f"C{8 + h}"])
            for (q0, subs) in blocks:
                cnt = self._blk
                self._blk += 1
                stageA(cnt, q0, subs)
                if self._pend is not None:
                    self._pend()
                self._pend = (lambda cnt=cnt, q0=q0, sb_=stageB: sb_(cnt, q0))
        if self._pend is not None:
            self._pend()
            self._pend = None
        self.rfence("AB")

    def out_proj_postnorm(self, wkey, x_src, mid, sink, xkey=None):
        A, dr = self.A, self.dr
        yT = self.C[:].rearrange("p (k t) -> p k t", t=1024)
        zall = self.AB[:].bitcast(F32).rearrange("p (t n) -> p t n", n=1024)
        ALLC = [f"C{k}" for k in range(16)]
        pre = [self.wload(dr[wkey][:, cb * 128:(cb + 1) * 128], 16, 128) for cb in range(4)]
        self.rfence("AB")
        for cb in range(8):
            wv, wk = pre[cb] if cb < 4 else self.wload(dr[wkey][:, cb * 128:(cb + 1) * 128], 16, 128)
            for half in range(2):
                pi = self.nxt("pf01", 2)
                p = self.pf[pi]
                for t4 in range(4):
                    tt = half * 4 + t4
                    for kt in range(16):
                        A("pe", lambda e, p=p, t4=t4, tt=tt, kt=kt, wv=wv: e.matmul(p[:, t4 * 128:(t4 + 1) * 128], lhsT=yT[:, kt, tt * 128:(tt + 1) * 128],
                                                                                   rhs=wv[:, kt, :], start=(kt == 0), stop=(kt == 15)),
                          r=[wk] + ALLC, w=[f"pf{pi}"])
                gate = self.mrep[:, 2048 + cb * 128:2048 + (cb + 1) * 128].unsqueeze(1).to_broadcast([128, 4, 128])
                A("dve", lambda e, p=p, half=half, cb=cb, gate=gate: e.tensor_tensor(
                    out=zall[:, half * 4:half * 4 + 4, cb * 128:(cb + 1) * 128], in0=p[:].rearrange("q (t n) -> q t n", n=128), in1=gate, op=ALU.mult),
                  r=[f"pf{pi}", "mrep"], w=[f"Z{half * 4 + t}" for t in range(4)])
        if mid is not None:
            mid()
        sts = []
        for tt in range(8):
            b = self.nxt("xt", 2)
            xt = self.xt[b]
            A("sp", lambda e, xt=xt, tt=tt: e.dma_start(out=xt[:], in_=x_src[tt * 128:(tt + 1) * 128, :]), r=([xkey] if xkey else []), w=[f"xt{b}"], dk=f"xt{b}")
            A("dve", lambda e, xt=xt, tt=tt: e.scalar_tensor_tensor(out=zall[:, tt, :], in0=xt[:], scalar=ALPHA, in1=zall[:, tt, :], op0=ALU.mult, op1=ALU.add),
              r=[f"xt{b}", f"Z{tt}"], w=[f"Z{tt}"])
            sts.append(self.ln_stats_t(zall[:, tt, :], f"Z{tt}", tt, 0))
        for tt in range(8):
            rstd, nmr, lk = sts[tt]
            zt = zall[:, tt, :]
            A("act", lambda e, zt=zt, nmr=nmr, rstd=rstd: e.activation(out=zt, in_=zt, func=AF.Identity, bias=nmr, scale=rstd), r=[f"Z{tt}"] + lk, w=[f"Z{tt}"])
            A("dve", lambda e, zt=zt: e.tensor_tensor(out=zt, in0=zt, in1=self.lngr[:], op=ALU.mult), r=[f"Z{tt}", "lngr"], w=[f"Z{tt}"])
            A("dve", lambda e, zt=zt: e.tensor_tensor(out=zt, in0=zt, in1=self.lnbr[:], op=ALU.add), r=[f"Z{tt}", "lnbr"], w=[f"Z{tt}"])
        sink([(zall[:, tt, :], f"Z{tt}", tt) for tt in range(8)])

    def fourier_path(self, grp):
        A, dr = self.A, self.dr
        isS = (grp == "S")
        L = 1024 if isS else 256
        ntl = L // 128
        mixedT = self.AB[:].rearrange("p (k t) -> p k t", t=1024)
        uT = [self.T[:, i * 2048:(i + 1) * 2048].rearrange("p (k t) -> p k t", t=1024) for i in range(2)]
        ucs = self.T[:, 4096:8192].rearrange("p (t n) -> p t n", n=512)
        cs256 = self.T[:, 8192:9216].rearrange("p (k n) -> p k n", n=512)
        CL = self.C[:, 0:ntl * L].rearrange("p (t n) -> p t n", n=L)
        SL = self.C[:, 8192:8192 + ntl * L].rearrange("p (t n) -> p t n", n=L)
        pre = [self.wload(dr["win_o"][:, fg * 256:(fg + 1) * 256], 8, 256) for fg in range(4)]
        self.rfence("AB"); self.rfence("C"); self.rfence("T")
        A("pool", lambda e: e.dma_start(out=cs256, in_=dr["cs256"].rearrange("(k p) n -> p k n", p=128)), w=["cs256"], dk="cs256")
        cn, sn = ("cl1024", "sl1024") if isS else ("cl256", "sl256")
        for tl in range(ntl):
            A("pool", lambda e, tl=tl: e.dma_start(out=CL[:, tl, :], in_=dr[cn][tl * 128:(tl + 1) * 128, :]), w=["Ctab"], dk="ctabC")
            A("pool", lambda e, tl=tl: e.dma_start(out=SL[:, tl, :], in_=dr[sn][tl * 128:(tl + 1) * 128, :]), w=["Ctab"], dk="ctabS")
        seqs = [(0, 1024)] if isS else [(i * 256, 256) for i in range(4)]
        for fg in range(8):
            wv, wk = pre[fg] if fg < 4 else self.wload(dr["win_o"][:, fg * 256:(fg + 1) * 256], 8, 256)
            ub = self.nxt("uT", 2)
            u = uT[ub]
            for kt2 in range(2):
                for tb in range(2):
                    ts = slice(tb * 512, (tb + 1) * 512)
                    pi = self.nxt("pf01", 2)
                    p = self.pf[pi]
                    for kt in range(8):
                        A("pe", lambda e, p=p, kt=kt, kt2=kt2, ts=ts, wv=wv: e.matmul(p[:], lhsT=wv[:, kt, kt2 * 128:(kt2 + 1) * 128], rhs=self.hT[:, kt, ts],
                                                                                     start=(kt == 0), stop=(kt == 7)), r=[wk, "hT"], w=[f"pf{pi}"])
                    A("act", lambda e, p=p, u=u, kt2=kt2, ts=ts: e.activation(out=u[:, kt2, ts], in_=p[:], func=AF.Copy), r=[f"pf{pi}"], w=[f"uT{ub}"])
            for tt in range(8):
                pj = 2 + self.nxt("pf23", 2)
                p2 = self.pf[pj]
                for kt2 in range(2):
                    A("pe", lambda e, p2=p2, kt2=kt2, tt=tt, u=u: e.matmul(p2[:], lhsT=u[:, kt2, tt * 128:(tt + 1) * 128], rhs=cs256[:, kt2, :],
                                                                          start=(kt2 == 0), stop=(kt2 == 1)), r=[f"uT{ub}", "cs256"], w=[f"pf{pj}"])
                if tt % 2 == 0:
                    A("dve", lambda e, p2=p2, tt=tt: e.tensor_copy(out=ucs[:, tt, :], in_=p2[:]), r=[f"pf{pj}"], w=["ucs"])
                else:
                    A("act", lambda e, p2=p2, tt=tt: e.activation(out=ucs[:, tt, :], in_=p2[:], func=AF.Copy), r=[f"pf{pj}"], w=["ucs"])
            for (s0, Ls) in seqs:
                nk1 = min(512, Ls)
                for f2 in range(2):
                    for kb in range(Ls // nk1):
                        ks = slice(kb * nk1, (kb + 1) * nk1)
                        pi = self.nxt("pf01", 2)
                        p = self.pf[pi]
                        n = 0
                        for tl in range(ntl):
                            tt = s0 // 128 + tl
                            for (co, tab) in [(0, CL), (256, SL)]:
                                A("pe", lambda e, p=p, tt=tt, tl=tl, f2=f2, co=co, tab=tab, ks=ks, nk1=nk1, n=n: e.matmul(
                                    p[:, 0:nk1], lhsT=ucs[:, tt, co + f2 * 128:co + (f2 + 1) * 128], rhs=tab[:, tl, ks],
                                    start=(n == 0), stop=(n == 2 * ntl - 1)), r=["ucs", "Ctab"], w=[f"pf{pi}"])
                                n += 1
                        A("act", lambda e, p=p, fg=fg, f2=f2, s0=s0, ks=ks, nk1=nk1: e.activation(
                            out=mixedT[:, fg * 2 + f2, s0 + ks.start:s0 + ks.stop], in_=p[:, 0:nk1], func=AF.Copy), r=[f"pf{pi}"], w=["mixA"])

    def fno_path(self):
        A, dr = self.A, self.dr
        mixedT = self.AB[:].rearrange("p (k t) -> p k t", t=1024)
        yT = self.C[:].rearrange("p (k t) -> p k t", t=1024)
        pre = []
        for ot in range(2):
            pre.append((self.wload(dr["wfno"][:, ot * 128:(ot + 1) * 128], 16, 128), self.wload(dr["win_o"][:, 2048 + ot * 128:2048 + (ot + 1) * 128], 8, 128)))
        self.rfence("C")
        for ot in range(16):
            if ot < 2:
                (wf, wfk), (wz, wzk) = pre[ot]
            else:
                wf, wfk = self.wload(dr["wfno"][:, ot * 128:(ot + 1) * 128], 16, 128)
                wz, wzk = self.wload(dr["win_o"][:, 2048 + ot * 128:2048 + (ot + 1) * 128], 8, 128)
            for tb in range(2):
                ts = slice(tb * 512, (tb + 1) * 512)
                pi = self.nxt("pf01", 2)
                p = self.pf[pi]
                for kt in range(16):
                    A("pe", lambda e, p=p, kt=kt, wf=wf, ts=ts: e.matmul(p[:], lhsT=wf[:, kt, :], rhs=mixedT[:, kt, ts], start=(kt == 0), stop=(kt == 15)),
                      r=[wfk, "mixA"], w=[f"pf{pi}"])
                pj = 2 + self.nxt("pf23", 2)
                p2 = self.pf[pj]
                for kt in range(8):
                    A("pe", lambda e, p2=p2, kt=kt, wz=wz, ts=ts: e.matmul(p2[:], lhsT=wz[:, kt, :], rhs=self.hT[:, kt, ts], start=(kt == 0), stop=(kt == 7)),
                      r=[wzk, "hT"], w=[f"pf{pj}"])
                t1 = self.tmp[1]
                A("act", lambda e, p2=p2: e.activation(out=t1[:], in_=p2[:], func=AF.Silu), r=[f"pf{pj}"], w=["tmp1"])
                A("dve", lambda e, p=p, ot=ot, ts=ts: e.scalar_tensor_tensor(out=yT[:, ot, ts], in0=p[:], scalar=self.bfno[:, ot:ot + 1], in1=t1[:],
                                                                            op0=ALU.add, op1=ALU.mult), r=[f"pf{pi}", "bfno", "tmp1"], w=[f"C{ot}"])

    def load_ln(self, layer):
        A, dr = self.A, self.dr
        A("sp", lambda e: e.dma_start(out=self.lngr[:], in_=dr["lng"][layer].partition_broadcast(128)), w=["lngr"], dk="lngr")
        A("sp", lambda e: e.dma_start(out=self.lnbr[:], in_=dr["lnb"][layer].partition_broadcast(128)), w=["lnbr"], dk="lnbr")

    def run_group(self, grp):
        A, dr = self.A, self.dr
        isS = (grp == "S")
        xin = dr["xs"] if isS else dr["xp"]
        cond = dr["csmp"] if isS else dr["cctx"]
        x1s = dr["x1s"][1 if isS else 0]
        yout = dr["ys"] if isS else dr["yp"]
        self.rfence("AB")
        xall = self.AB[:].bitcast(F32).rearrange("p (t n) -> p t n", n=1024)
        tiles = []
        for tt in range(8):
            A("sp", lambda e, tt=tt: e.dma_start(out=xall[:, tt, :], in_=xin[tt * 128:(tt + 1) * 128, :]), w=[f"Z{tt}"], dk=f"xin{tt % 4}")
            tiles.append((xall[:, tt, :], f"Z{tt}", tt))
        st = [self.ln_stats_t(src, key, tt, 1) for (src, key, tt) in tiles]
        self.load_mod(0, 1 if isS else 0)
        self.load_ln(0)
        self.ln_mod_tiles(tiles, st)
        upto = getattr(self, "upto", 99)
        if upto < 1:
            return
        self.s5_path(grp)
        if upto < 2:
            return
        self.glu_path()
        if upto < 3:
            return
        self.attn_path(grp)
        if upto < 4:
            return

        def sink0(tiles):
            for (ap, key, tt) in tiles:
                A("sp", lambda e, ap=ap, tt=tt: e.dma_start(out=x1s[tt * 128:(tt + 1) * 128, :], in_=ap), r=[key], w=["x1s"], dk=f"x1w{tt % 2}")
            self.ln_mod_tiles(tiles)
        self.out_proj_postnorm("wout_e", xin, lambda: self.load_mod(1, 1 if isS else 0), sink0)
        if upto < 5:
            return
        self.load_ln(1)
        self.fourier_path(grp)
        if upto < 6:
            return
        self.fno_path()
        if upto < 7:
            return

        def sink1(tiles):
            for (ap, key, tt) in tiles:
                tk = A("sp", lambda e, ap=ap, tt=tt: e.dma_start(out=yout[tt * 128:(tt + 1) * 128, :], in_=ap), r=[key], w=["yout"], dk=f"yw{tt % 2}")
                self.out_tokens.append(tk)
        self.out_proj_postnorm("wout_o", x1s, None, sink1, xkey="x1s")

    def emit(self):
        nc, tr = self.nc, self.tr
        pre = self.pre
        outs = self.out_tokens
        with nc.Block() as block:
            @block.sync
            def _(e):
                tr.emit_engine("sp", e)
                tr.final_waits(e, tr.all_tokens())

            @block.scalar
            def _(e):
                tr.emit_engine("act", e)

            @block.vector
            def _(e):
                tr.emit_engine("dve", e)

            @block.gpsimd
            def _(e):
                tr.emit_engine("pool", e)

            @block.tensor
            def _(e):
                tr.emit_engine("pe", e)


IN_SPECS = [
    ("xp", [1024, 1024]), ("xs", [1024, 1024]), ("cctx", [1024]), ("csmp", [1024]),
    ("wmod", [2, 1024, 3072]), ("bmod", [2, 3072]), ("lng", [2, 1024]), ("lnb", [2, 1024]),
    ("win_e", [1024, 6144]), ("wglu", [1024, 1024]), ("bglu", [1024]), ("wout_e", [2048, 1024]), ("subg", [128]),
    ("win_o", [1024, 4096]), ("wfno", [2048, 2048]), ("bfno", [2048]), ("wout_o", [2048, 1024]),
    ("cache_k", [256, 8, 2, 64]), ("cache_v", [256, 8, 128]), ("st_re", [2, 64, 64]), ("st_im", [2, 64, 64]),
    ("lam_re", [2, 64, 64]), ("lam_im", [2, 64, 64]), ("log_dt", [2, 64]),
    ("b_re", [2, 64, 64, 16]), ("b_im", [2, 64, 64, 16]), ("c_re", [2, 64, 16, 64]), ("c_im", [2, 64, 16, 64]), ("ssm_d", [1024]),
    ("lq1", [64]), ("lk1", [64]), ("lq2", [64]), ("lk2", [64]),
    ("ident", [128, 128]), ("maskF", [128, 128]), ("maskB", [128, 128]), ("pswap", [128, 128]),
    ("ropeC", [128, 1024]), ("ropeS", [128, 1024]), ("cs256", [256, 512]),
    ("cl256", [256, 256]), ("sl256", [256, 256]), ("cl1024", [1024, 1024]), ("sl1024", [1024, 1024]),
]
OUT_SPECS = [("yp", [1024, 1024]), ("ys", [1024, 1024]), ("nk", [1024, 1024]), ("nv", [1024, 1024]),
             ("fre", [4, 2, 64, 64]), ("fim", [4, 2, 64, 64])]


def build_nc(groups=("P", "S"), upto=99, att_lvl=9):
    nc = bass.Bass("TRN2", target_bir_lowering=False)
    dr = {}
    for n, shp in IN_SPECS:
        dr[n] = nc.dram_tensor(n, shp, F32, kind="ExternalInput").ap()
    for n, shp in OUT_SPECS:
        dr[n] = nc.dram_tensor(n, shp, F32, kind="ExternalOutput").ap()
    dr["s_opW"] = nc.dram_tensor("s_opW", [128, 64, 2, 128], BF16).ap()
    dr["s_opQ"] = nc.dram_tensor("s_opQ", [128, 64, 2, 128], BF16).ap()
    dr["s_opM"] = nc.dram_tensor("s_opM", [128, 64, 128], BF16).ap()
    dr["s_A8"] = nc.dram_tensor("s_A8", [128, 2, 64], F32).ap()
    dr["s_nlam"] = nc.dram_tensor("s_nlam", [128, 1], F32).ap()
    dr["s_mod"] = nc.dram_tensor("s_mod", [2, 2, 3072], F32).ap()
    dr["x1s"] = nc.dram_tensor("x1s", [2, 1024, 1024], F32).ap()
    with contextlib.ExitStack() as es:
        sems = [es.enter_context(nc.semaphore(f"s{i}")) for i in range(100)]
        tr0 = phase0(nc, dr, sems[:38])
        with contextlib.ExitStack() as es2:
            m = Main(nc, dr, sems[38:], tr0.all_tokens())
            m.upto = upto
            m.att_lvl = att_lvl
            m.alloc(es2)
            m.alloc2(es2)
            m.load_consts()
            for g in groups:
                m.run_group(g)
            m.emit()
    return nc


def host_consts():
    i8 = np.arange(128) // 16
    c = {"ident": np.eye(128, dtype=np.float32),
         "maskF": (i8[:, None] <= i8[None, :]).astype(np.float32),
         "maskB": (i8[:, None] >= i8[None, :]).astype(np.float32)}
    psw = np.zeros((128, 128), np.float32)
    for m in range(128):
        blk, j = divmod(m, 64)
        psw[blk * 64 + (j + 32) % 64, m] = 1.0
    c["pswap"] = psw
    L = 1024
    row = np.repeat(np.arange(L // 64), 64).astype(np.float64)
    col = np.tile(np.arange(64), L // 64).astype(np.float64)
    freqs = 10000.0 ** (-np.arange(16, dtype=np.float64) / 16)
    ang = np.concatenate([row[:, None] * freqs, col[:, None] * freqs], axis=-1)
    cosT = np.cos(ang).T
    sinT = np.sin(ang).T
    c["ropeC"] = np.concatenate([cosT, cosT, cosT, cosT], axis=0).astype(np.float32)
    c["ropeS"] = np.concatenate([-sinT, sinT, -sinT, sinT], axis=0).astype(np.float32)
    k = np.arange(256, dtype=np.float64)
    a = 2 * np.pi * np.outer(k, k) / 256
    c["cs256"] = (np.concatenate([np.cos(a), np.sin(a)], axis=1) / 16.0).astype(np.float32)
    for Ls, cn, sn in [(256, "cl256", "sl256"), (1024, "cl1024", "sl1024")]:
        t = np.arange(Ls, dtype=np.float64)
        a = 2 * np.pi * (np.outer(t, t) % Ls) / Ls
        c[cn] = (np.cos(a) / np.sqrt(Ls)).astype(np.float32)
        c[sn] = (-np.sin(a) / np.sqrt(Ls)).astype(np.float32)
    return c


def make_in_maps(x_prompt, x_sample, cache_k, cache_v, state_ssm_re, state_ssm_im, c, c_ctx,
                 w_mod, b_mod, ln_g, ln_b, w_in_e, ssm_lam_re, ssm_lam_im, ssm_log_dt,
                 ssm_b_re, ssm_b_im, ssm_c_re, ssm_c_im, ssm_d, w_glu, b_glu,
                 lam_q1, lam_k1, lam_q2, lam_k2, subln_g, w_out_e, w_in_o, w_fno, b_fno, w_out_o):
    f = lambda a: np.ascontiguousarray(np.asarray(a, dtype=np.float32))
    shared = {
        "cctx": f(c_ctx), "wmod": f(w_mod), "bmod": f(b_mod), "lng": f(ln_g), "lnb": f(ln_b),
        "win_e": f(w_in_e[0]), "wglu": f(w_glu[0]), "bglu": f(b_glu[0]), "wout_e": f(w_out_e[0]), "subg": f(subln_g[0]),
        "win_o": f(w_in_o[0]), "wfno": f(w_fno[0]), "bfno": f(b_fno[0]), "wout_o": f(w_out_o[0]),
        "lam_re": f(ssm_lam_re[0]), "lam_im": f(ssm_lam_im[0]), "log_dt": f(ssm_log_dt[0]),
        "b_re": f(ssm_b_re[0]), "b_im": f(ssm_b_im[0]), "c_re": f(ssm_c_re[0]), "c_im": f(ssm_c_im[0]), "ssm_d": f(ssm_d[0]),
        "lq1": f(lam_q1[0]), "lk1": f(lam_k1[0]), "lq2": f(lam_q2[0]), "lk2": f(lam_k2[0]),
    }
    shared.update(host_consts())
    xp = np.asarray(x_prompt, np.float32)
    xs = np.asarray(x_sample, np.float32)
    maps = []
    for i in range(NCORES):
        b = i % 4
        m = dict(shared)
        m["xp"] = f(xp[4 * i:4 * i + 4].reshape(1024, 1024))
        m["xs"] = f(xs[b])
        m["csmp"] = f(np.asarray(c)[b])
        m["cache_k"] = f(np.asarray(cache_k)[b, 0])
        m["cache_v"] = f(np.asarray(cache_v)[b, 0])
        m["st_re"] = f(np.asarray(state_ssm_re)[b, 0])
        m["st_im"] = f(np.asarray(state_ssm_im)[b, 0])
        maps.append(m)
    return maps


def gather(results):
    g = lambda n, shp: np.concatenate([np.asarray(r[n], np.float32).reshape(shp) for r in results], axis=0)
    y_prompt = g("yp", (4, 256, 1024))
    new_k = g("nk", (4, 1, 256, 8, 2, 64))
    new_v = g("nv", (4, 1, 256, 8, 128))
    s_re = g("fre", (4, 1, 2, 64, 64))
    s_im = g("fim", (4, 1, 2, 64, 64))
    y_sample = np.stack([np.asarray(results[b]["ys"], np.float32) for b in range(4)], axis=0)
    return (y_prompt, y_sample, new_k, new_v, s_re, s_im)


def kernel(**inputs):
    nc = build_nc()
    maps = make_in_maps(**inputs)
    res = run_bass_kernel_spmd(nc, maps, core_ids=list(range(NCORES)))
    return gather(res.results)
```

```python
import contextlib
import math
import numpy as np
import concourse.bass as bass
import concourse.mybir as mybir
from concourse.bass_utils import run_bass_kernel_spmd

F32 = mybir.dt.float32
BF16 = mybir.dt.bfloat16
ALU = mybir.AluOpType
AF = mybir.ActivationFunctionType
AX = mybir.AxisListType

D = 1024
NCORES = 8
TOK = 1024
LN_EPS = 1e-5
ALPHA = (2 * 2) ** 0.25
ENGS = ["sp", "act", "dve", "pool", "pe"]
EPOCH = 8000
T8 = 8
GELU_C = 2.0 * math.sqrt(2.0 / math.pi)


class Tracker:
    def __init__(self, sem_pool):
        self.sem_pool = list(sem_pool)
        self.ops = []
        self.buf = {}
        self.eng_cnt = {e: 0 for e in ENGS}
        self.eng_sems = {e: [] for e in ENGS}
        self.dma_sem = {}
        self.dma_cnt = {}

    def _st(self, k):
        if k not in self.buf:
            self.buf[k] = [None, []]
        return self.buf[k]

    def add(self, eng, emit, reads=(), writes=(), dma_key=None):
        reads = list(reads)
        writes = list(writes)
        if dma_key is not None:
            writes.append("#dma:" + dma_key)
        deps = set()
        for k in reads:
            st = self._st(k)
            if st[0] is not None:
                deps.add(st[0])
            if k.startswith("pf") or k.startswith("pt"):
                for r_ in st[1]:
                    if r_[4] != eng:
                        deps.add(r_)
        for k in writes:
            st = self._st(k)
            if st[0] is not None:
                deps.add(st[0])
            for r in st[1]:
                deps.add(r)
        if dma_key is not None:
            if dma_key not in self.dma_sem:
                self.dma_sem[dma_key] = self.sem_pool.pop()
                self.dma_cnt[dma_key] = 0
            self.dma_cnt[dma_key] += 16
            sem = self.dma_sem[dma_key]
            token = (id(sem), self.dma_cnt[dma_key], sem, 16, None)
        else:
            c = self.eng_cnt[eng]
            ep, v = divmod(c, EPOCH)
            if ep >= len(self.eng_sems[eng]):
                self.eng_sems[eng].append(self.sem_pool.pop())
            sem = self.eng_sems[eng][ep]
            self.eng_cnt[eng] = c + 1
            token = (id(sem), v + 1, sem, 1, eng)
        for k in reads:
            self._st(k)[1].append(token)
        for k in writes:
            st = self._st(k)
            st[0] = token
            st[1] = []
        self.ops.append((eng, emit, deps, token))
        return token

    def emit_engine(self, eng, e):
        waited = {}
        for (oe, emit, deps, token) in self.ops:
            if oe != eng:
                continue
            need = {}
            for (sid, val, sem, inc, deng) in deps:
                if deng == "pe" and eng == "pe":
                    continue
                if sid not in need or need[sid][0] < val:
                    need[sid] = (val, sem)
            for sid, (val, sem) in need.items():
                if waited.get(sid, 0) >= val:
                    continue
                e.wait_ge(sem, val)
                waited[sid] = val
            emit(e).then_inc(token[2], token[3])

    def all_tokens(self):
        toks = []
        for k, st in self.buf.items():
            if st[0] is not None:
                toks.append(st[0])
            toks += st[1]
        return toks

    def final_waits(self, e, tokens):
        need = {}
        for (sid, val, sem, inc, deng) in tokens:
            if sid not in need or need[sid][0] < val:
                need[sid] = (val, sem)
        for sid, (val, sem) in need.items():
            e.wait_ge(sem, val)


def run_block(nc, tr, final_tokens):
    with nc.Block() as block:
        @block.sync
        def _(e):
            tr.emit_engine("sp", e)
            tr.final_waits(e, final_tokens)

        @block.scalar
        def _(e):
            tr.emit_engine("act", e)

        @block.vector
        def _(e):
            tr.emit_engine("dve", e)

        @block.gpsimd
        def _(e):
            tr.emit_engine("pool", e)

        @block.tensor
        def _(e):
            tr.emit_engine("pe", e)


def phase0(nc, dr, sems):
    tr = Tracker(sems)
    es = contextlib.ExitStack()
    with es:
        def sb(name, shape, dt):
            return es.enter_context(nc.sbuf_tensor("p0_" + name, shape, dt))

        def ps(name, shape, dt):
            return es.enter_context(nc.psum_tensor("p0_" + name, shape, dt))
        uid = [0]

        def A(eng, fn, r=(), w=(), dk=None):
            return tr.add(eng, fn, reads=r, writes=w, dma_key=dk)

        identf = sb("identf", [128, 128], F32)
        identb = sb("identb", [128, 128], BF16)
        maskF = sb("maskF", [128, 128], F32)
        maskB = sb("maskB", [128, 128], F32)
        lin = [sb(f"lin{i}", [64, 2, 64], F32) for i in range(2)]
        LR = sb("LR", [128, 64], F32)
        LI = sb("LI", [128, 64], F32)
        LDT = sb("LDT", [128, 64], F32)
        BR = sb("BR", [128, 64, 16], F32)
        BI = sb("BI", [128, 64, 16], F32)
        CR = sb("CR", [128, 64, 16], F32)
        CI = sb("CI", [128, 64, 16], F32)
        nCR = sb("nCR", [128, 64, 16], F32)
        nCI = sb("nCI", [128, 64, 16], F32)
        cin = [sb(f"cin{i}", [128, 2, 64], F32) for i in range(2)]
        Dcol = sb("Dcol", [128, 64], F32)
        sm = {n: sb(n, [128, 64], F32) for n in
              ["dt", "ldr", "ldi", "c", "s", "mag", "ar", "ai", "e2", "ivr", "ivi", "t1", "t2", "t3", "t4",
               "cr", "ci", "nr", "ni", "am1", "den", "fr", "fi", "A8r", "A8i"]}
        EPr = sb("EPr", [128, 64, 8], F32)
        EPi = sb("EPi", [128, 64, 8], F32)
        ENr = sb("ENr", [128, 64, 8], F32)
        ENi = sb("ENi", [128, 64, 8], F32)
        bbr = sb("bbr", [128, 64, 16], F32)
        bbi = sb("bbi", [128, 64, 16], F32)
        bt1 = sb("bt1", [128, 64, 16], F32)
        bt2 = sb("bt2", [128, 64, 16], F32)
        T1 = sb("T1", [128, 16, 8, 16], F32)
        T2 = sb("T2", [128, 16, 8, 16], F32)
        T3 = sb("T3", [128, 16, 8, 16], F32)
        T4 = sb("T4", [128, 16, 8, 16], F32)
        PRb = sb("PRb", [128, 64, 128], BF16)
        PIb = sb("PIb", [128, 64, 128], BF16)
        Qrb = sb("Qrb", [128, 64, 128], BF16)
        nQib = sb("nQib", [128, 64, 128], BF16)
        Wop = sb("Wop", [128, 64, 2, 128], BF16)
        Mop = sb("Mop", [128, 64, 128], BF16)
        lq = sb("lq", [128, 4, 64], F32)
        lsum = sb("lsum", [128, 2], F32)
        lprod = sb("lprod", [128, 2, 64], F32)
        nlam = sb("nlam", [128, 1], F32)
        pT = ps("pT", [128, 128], F32)
        pW = [ps(f"pW{i}", [128, 8, 128], BF16) for i in range(2)]
        pMf = ps("pMf", [128, 4, 128], F32)
        pMb = ps("pMb", [128, 4, 128], F32)
        pmod = [ps(f"pmod{i}", [128, 256], F32) for i in range(2)]
        ccm = sb("ccm", [128, 2, 8], F32)
        scm = sb("scm", [128, 2, 8], F32)

        A("sp", lambda e: e.dma_start(out=identf[:], in_=dr["ident"]), w=["identf"], dk="identf")
        A("sp", lambda e: e.dma_start(out=maskF[:], in_=dr["maskF"]), w=["maskF"], dk="maskF")
        A("sp", lambda e: e.dma_start(out=maskB[:], in_=dr["maskB"]), w=["maskB"], dk="maskB")
        A("dve", lambda e: e.tensor_copy(out=identb[:], in_=identf[:]), r=["identf"], w=["identb"])
        if "wmod" in dr and "s_mod" in dr:
            for ci, cn in enumerate(["cctx", "csmp"]):
                A("sp", lambda e, cn=cn, ci=ci: e.dma_start(out=ccm[:, ci, :], in_=dr[cn].rearrange("(k p) -> p k", p=128), allow_slow_non_contiguous=True),
                  w=["ccm"], dk="ccm")
            A("act", lambda e: e.activation(out=scm[:], in_=ccm[:], func=AF.Silu), r=["ccm"], w=["scm"])
        for i, (src, dst, dn) in enumerate([("lam_re", LR, "LR"), ("lam_im", LI, "LI")]):
            A("sp", lambda e, i=i, src=src: e.dma_start(out=lin[i][:], in_=dr[src].rearrange("d g p -> g d p")),
              w=[f"lin{i}"], dk=f"lin{i}")
            A("pe", lambda e, i=i: e.transpose(pT[:, 0:64], lin[i][:].rearrange("g d p -> g (d p)"), identf[0:64, 0:64]),
              r=[f"lin{i}", "identf"], w=["pT"])
            A("dve", lambda e, dst=dst: e.tensor_copy(out=dst[:], in_=pT[:, 0:64]), r=["pT"], w=[dn])
        for d in range(2):
            A("sp", lambda e, d=d: e.dma_start(out=LDT[64 * d:64 * d + 64, :], in_=dr["log_dt"][d].partition_broadcast(64)),
              w=["LDT"], dk=f"LDT{d}")
            A("sp", lambda e, d=d: e.dma_start(out=BR[64 * d:64 * d + 64], in_=dr["b_re"][d].rearrange("g p m -> p g m")),
              w=["BR"], dk=f"BR{d}")
            A("act", lambda e, d=d: e.dma_start(out=BI[64 * d:64 * d + 64], in_=dr["b_im"][d].rearrange("g p m -> p g m")),
              w=["BI"], dk=f"BI{d}")
        k = 0
        for (src, dst, dn) in [("c_re", CR, "CR"), ("c_im", CI, "CI")]:
            for blk in range(8):
                b = k % 2
                k += 1
                A("sp", lambda e, b=b, src=src, blk=blk: e.dma_start(
                    out=cin[b][:], in_=dr[src][:, 8 * blk:8 * blk + 8].rearrange("d g n p -> (g n) d p")),
                  w=[f"cin{b}"], dk=f"cin{b}")
                A("pe", lambda e, b=b: e.transpose(pT[:], cin[b][:].rearrange("q d p -> q (d p)"), identf[:]),
                  r=[f"cin{b}", "identf"], w=["pT"])
                A("dve", lambda e, dst=dst, blk=blk: e.tensor_copy(
                    out=dst[:, 8 * blk:8 * blk + 8, :], in_=pT[:].rearrange("q (g n) -> q g n", n=16)), r=["pT"], w=[dn])
        S = sm

        def tt(o, a, b, op, eng="dve"):
            A(eng, lambda e: e.tensor_tensor(out=S[o][:], in0=S[a][:], in1=S[b][:], op=op), r=[a, b], w=[o])

        def cmul(zr, zi, xr, xi, yr, yi):
            tt("t1", xr, yr, ALU.mult); tt("t2", xi, yi, ALU.mult)
            tt("t3", xr, yi, ALU.mult); tt("t4", xi, yr, ALU.mult)
            tt(zr, "t1", "t2", ALU.subtract); tt(zi, "t3", "t4", ALU.add)

        A("act", lambda e: e.activation(out=S["dt"][:], in_=LDT[:], func=AF.Exp), r=["LDT"], w=["dt"])
        A("dve", lambda e: e.tensor_tensor(out=S["ldr"][:], in0=LR[:], in1=S["dt"][:], op=ALU.mult), r=["LR", "dt"], w=["ldr"])
        A("dve", lambda e: e.tensor_tensor(out=S["ldi"][:], in0=LI[:], in1=S["dt"][:], op=ALU.mult), r=["LI", "dt"], w=["ldi"])
        A("dve", lambda e: e.tensor_scalar(out=S["t1"][:], in0=S["ldi"][:], scalar1=1.0 / 32, scalar2=math.pi / 2,
                                           op0=ALU.mult, op1=ALU.add), r=["ldi"], w=["t1"])
        A("act", lambda e: e.activation(out=S["c"][:], in_=S["t1"][:], func=AF.Sin), r=["t1"], w=["c"])
        A("act", lambda e: e.activation(out=S["s"][:], in_=S["ldi"][:], func=AF.Sin, scale=1.0 / 32), r=["ldi"], w=["s"])
        for _ in range(5):
            tt("t1", "c", "c", ALU.mult); tt("t2", "s", "s", ALU.mult); tt("t3", "c", "s", ALU.mult)
            tt("c", "t1", "t2", ALU.subtract); tt("s", "t3", "t3", ALU.add)
        A("act", lambda e: e.activation(out=S["mag"][:], in_=S["ldr"][:], func=AF.Exp), r=["ldr"], w=["mag"])
        A("act", lambda e: e.activation(out=S["e2"][:], in_=S["ldr"][:], func=AF.Exp, scale=-2.0), r=["ldr"], w=["e2"])
        tt("ar", "mag", "c", ALU.mult); tt("ai", "mag", "s", ALU.mult)
        tt("ivr", "ar", "e2", ALU.mult)
        A("dve", lambda e: e.scalar_tensor_tensor(out=S["ivi"][:], in0=S["ai"][:], scalar=-1.0, in1=S["e2"][:],
                                                  op0=ALU.mult, op1=ALU.mult), r=["ai", "e2"], w=["ivi"])
        for (tr_, ti_, br_, bi_, cr_, ci_) in [(EPr, EPi, "ar", "ai", "cr", "ci"), (ENr, ENi, "ivr", "ivi", "nr", "ni")]:
            A("dve", lambda e, cr_=cr_, br_=br_: e.tensor_copy(out=S[cr_][:], in_=S[br_][:]), r=[br_], w=[cr_])
            A("dve", lambda e, ci_=ci_, bi_=bi_: e.tensor_copy(out=S[ci_][:], in_=S[bi_][:]), r=[bi_], w=[ci_])
            tn = "EP" if tr_ is EPr else "EN"
            for kk in range(1, 9):
                for (tab, cur) in [(tr_, cr_), (ti_, ci_)]:
                    A("act", lambda e, tab=tab, cur=cur, kk=kk: e.activation(out=tab[0:64, :, kk - 1], in_=S[cur][0:64, :], func=AF.Copy),
                      r=[cur], w=[tn])
                    A("act", lambda e, tab=tab, cur=cur, kk=kk: e.activation(out=tab[64:128, :, 8 - kk], in_=S[cur][64:128, :], func=AF.Copy),
                      r=[cur], w=[tn])
                if kk == 8 and tr_ is EPr:
                    A("dve", lambda e: e.tensor_copy(out=S["A8r"][:], in_=S["cr"][:]), r=["cr"], w=["A8r"])
                    A("dve", lambda e: e.tensor_copy(out=S["A8i"][:], in_=S["ci"][:]), r=["ci"], w=["A8i"])
                if kk < 8:
                    cmul(cr_, ci_, cr_, ci_, br_, bi_)
        A("sp", lambda e: e.dma_start(out=dr["s_A8"][:, 0, :], in_=S["A8r"][:]), r=["A8r"], w=["s_A8r"], dk="s_A8r")
        A("sp", lambda e: e.dma_start(out=dr["s_A8"][:, 1, :], in_=S["A8i"][:]), r=["A8i"], w=["s_A8i"], dk="s_A8i")
        for i in range(8):
            A("sp", lambda e, i=i: e.dma_start(out=Dcol[16 * i:16 * i + 16, :], in_=dr["ssm_d"].rearrange("(g m) -> m g", m=16),
                                               allow_slow_non_contiguous=True), w=["Dcol"], dk="Dcol")
        for i, nm in enumerate(["lq1", "lk1", "lq2", "lk2"]):
            A("sp", lambda e, i=i, nm=nm: e.dma_start(out=lq[:, i, :], in_=dr[nm].partition_broadcast(128)), w=["lq"], dk="lq")
        A("dve", lambda e: e.tensor_tensor(out=lprod[:, 0, :], in0=lq[:, 0, :], in1=lq[:, 1, :], op=ALU.mult), r=["lq"], w=["lprod"])
        A("dve", lambda e: e.tensor_tensor(out=lprod[:, 1, :], in0=lq[:, 2, :], in1=lq[:, 3, :], op=ALU.mult), r=["lq"], w=["lprod"])
        A("dve", lambda e: e.reduce_sum(out=lsum[:], in_=lprod[:], axis=AX.X), r=["lprod"], w=["lsum"])
        A("act", lambda e: e.activation(out=lsum[:], in_=lsum[:], func=AF.Exp), r=["lsum"], w=["lsum"])
        A("dve", lambda e: e.scalar_tensor_tensor(out=nlam[:], in0=lsum[:, 1:2], scalar=-0.2, in1=lsum[:, 0:1],
                                                  op0=ALU.add, op1=ALU.subtract), r=["lsum"], w=["nlam"])
        A("sp", lambda e: e.dma_start(out=dr["s_nlam"], in_=nlam[:]), r=["nlam"], w=["s_nlam"], dk="s_nlam")

        A("dve", lambda e: e.tensor_scalar_add(out=S["am1"][:], in0=S["ar"][:], scalar1=-1.0), r=["ar"], w=["am1"])
        A("dve", lambda e: e.tensor_tensor(out=S["t1"][:], in0=LR[:], in1=LR[:], op=ALU.mult), r=["LR"], w=["t1"])
        A("dve", lambda e: e.tensor_tensor(out=S["t2"][:], in0=LI[:], in1=LI[:], op=ALU.mult), r=["LI"], w=["t2"])
        tt("den", "t1", "t2", ALU.add)
        A("dve", lambda e: e.reciprocal(out=S["den"][:], in_=S["den"][:]), r=["den"], w=["den"])
        A("dve", lambda e: e.tensor_tensor(out=S["t1"][:], in0=S["am1"][:], in1=LR[:], op=ALU.mult), r=["am1", "LR"], w=["t1"])
        A("dve", lambda e: e.tensor_tensor(out=S["t2"][:], in0=S["ai"][:], in1=LI[:], op=ALU.mult), r=["ai", "LI"], w=["t2"])
        tt("t3", "t1", "t2", ALU.add); tt("fr", "t3", "den", ALU.mult)
        A("dve", lambda e: e.tensor_tensor(out=S["t1"][:], in0=S["ai"][:], in1=LR[:], op=ALU.mult), r=["ai", "LR"], w=["t1"])
        A("dve", lambda e: e.tensor_tensor(out=S["t2"][:], in0=S["am1"][:], in1=LI[:], op=ALU.mult), r=["am1", "LI"], w=["t2"])
        tt("t3", "t1", "t2", ALU.subtract); tt("fi", "t3", "den", ALU.mult)
        frb = S["fr"][:].unsqueeze(2).to_broadcast([128, 64, 16])
        fib = S["fi"][:].unsqueeze(2).to_broadcast([128, 64, 16])
        A("dve", lambda e: e.tensor_tensor(out=bt1[:], in0=BR[:], in1=frb, op=ALU.mult), r=["BR", "fr"], w=["bt1"])
        A("dve", lambda e: e.tensor_tensor(out=bt2[:], in0=BI[:], in1=fib, op=ALU.mult), r=["BI", "fi"], w=["bt2"])
        A("dve", lambda e: e.tensor_tensor(out=bbr[:], in0=bt1[:], in1=bt2[:], op=ALU.subtract), r=["bt1", "bt2"], w=["bbr"])
        A("dve", lambda e: e.tensor_tensor(out=bt1[:], in0=BI[:], in1=frb, op=ALU.mult), r=["BI", "fr"], w=["bt1"])
        A("dve", lambda e: e.tensor_tensor(out=bt2[:], in0=BR[:], in1=fib, op=ALU.mult), r=["BR", "fi"], w=["bt2"])
        A("dve", lambda e: e.tensor_tensor(out=bbi[:], in0=bt1[:], in1=bt2[:], op=ALU.add), r=["bt1", "bt2"], w=["bbi"])
        A("dve", lambda e: e.tensor_scalar_mul(out=nCR[:], in0=CR[:], scalar1=-1.0), r=["CR"], w=["nCR"])
        A("dve", lambda e: e.tensor_scalar_mul(out=nCI[:], in0=CI[:], scalar1=-1.0), r=["CI"], w=["nCI"])
        for gb in range(4):
            gs = slice(16 * gb, 16 * gb + 16)

            def prod(o, tab, vec, tn, vn, gs=gs, eng="dve"):
                a0 = tab[:, gs, :].unsqueeze(3).to_broadcast([128, 16, 8, 16])
                a1 = vec[:, gs, :].unsqueeze(2).to_broadcast([128, 16, 8, 16])
                A(eng, lambda e, o=o, a0=a0, a1=a1: e.tensor_tensor(out=o[:], in0=a0, in1=a1, op=ALU.mult), r=[tn, vn], w=[o.name])

            def comb(dst, dn, op, gs=gs, ta=T1, tb=T2, eng="dve"):
                ov = dst[:, gs, :].rearrange("q g (i m) -> q g i m", m=16)
                A(eng, lambda e, ov=ov, op=op, ta=ta, tb=tb: e.tensor_tensor(out=ov, in0=ta[:], in1=tb[:], op=op), r=[ta.name, tb.name], w=[dn])
            prod(T1, ENr, bbr, "EN", "bbr"); prod(T2, ENi, bbi, "EN", "bbi"); comb(PRb, "PRb", ALU.subtract)
            prod(T1, ENr, bbi, "EN", "bbi"); prod(T2, ENi, bbr, "EN", "bbr"); comb(PIb, "PIb", ALU.add)
            if gb < 2:
                qe, qa, qb_ = "pool", T3, T4
            else:
                qe, qa, qb_ = "dve", T1, T2
            prod(qa, EPr, CR, "EP", "CR", eng=qe); prod(qb_, EPi, CI, "EP", "CI", eng=qe); comb(Qrb, "Qrb", ALU.subtract, ta=qa, tb=qb_, eng=qe)
            prod(qa, EPr, nCI, "EP", "nCI", eng=qe); prod(qb_, EPi, nCR, "EP", "nCR", eng=qe); comb(nQib, "nQib", ALU.add, ta=qa, tb=qb_, eng=qe)
        A("sp", lambda e: e.dma_start(out=dr["s_opQ"][:, :, 0, :], in_=Qrb[:]), r=["Qrb"], w=["s_opQ0"], dk="s_opQ0")
        A("sp", lambda e: e.dma_start(out=dr["s_opQ"][:, :, 1, :], in_=nQib[:]), r=["nQib"], w=["s_opQ1"], dk="s_opQ1")
        k = 0
        for ri, (P_, pn) in enumerate([(PRb, "PRb"), (PIb, "PIb")]):
            for blk in range(8):
                b = k % 2
                k += 1
                for gl in range(8):
                    A("pe", lambda e, b=b, gl=gl, P_=P_, blk=blk: e.transpose(pW[b][:, gl, :], P_[:, 8 * blk + gl, :], identb[:]),
                      r=[pn, "identb"], w=[f"pW{b}"])
                A("act", lambda e, b=b, ri=ri, blk=blk: e.activation(out=Wop[:, 8 * blk:8 * blk + 8, ri, :], in_=pW[b][:], func=AF.Copy),
                  r=[f"pW{b}"], w=["Wop"])
        A("sp", lambda e: e.dma_start(out=dr["s_opW"], in_=Wop[:]), r=["Wop"], w=["s_opW"], dk="s_opW")
        do_mod = "wmod" in dr and "s_mod" in dr
        if do_mod:
            screpv = [t[:].bitcast(BF16).rearrange("p a b -> p (a b)").rearrange("p (k n) -> p k n", n=128) for t in (ENr, ENi)]
            wslots = [t[:].bitcast(BF16).rearrange("p a b -> p (a b)").rearrange("p (k n) -> p k n", n=256) for t in (BR, BI, CR, CI, nCR, nCI)]
            wkeys = ["BR", "BI", "CR", "CI", "nCR", "nCI"]
            NSL = 6
            biasf = EPr[:].rearrange("p a b -> p (a b)")
            rows = [[(T3[:].rearrange("p a b c -> p (a b c)"), T3.name), (bt1[:].rearrange("p a b -> p (a b)"), "bt1")],
                    [(T4[:].rearrange("p a b c -> p (a b c)"), T4.name), (bt2[:].rearrange("p a b -> p (a b)"), "bt2")]]
            for ci in range(2):
                A("dve", lambda e, ci=ci: e.tensor_copy(out=screpv[ci], in_=scm[:, ci, :].unsqueeze(2).to_broadcast([128, 8, 128])), r=["scm"], w=["EN"])

            def emit_mod_dma(idx):
                if idx >= 24:
                    return
                layer, n = divmod(idx, 12)
                slot = idx % NSL
                wv, wk = wslots[slot], wkeys[slot]
                A("pool", lambda e, wv=wv, layer=layer, n=n: e.dma_start(
                    out=wv, in_=dr["wmod"][layer][:, n * 256:(n + 1) * 256].rearrange("(k p) n -> p k n", p=128)), w=[wk], dk=f"mw{slot}")

            def emit_mod_chunk(idx):
                layer, n = divmod(idx, 12)
                slot = idx % NSL
                wv, wk = wslots[slot], wkeys[slot]
                if n % 2 == 0:
                    A("sp", lambda e, layer=layer, n=n: e.dma_start(out=biasf, in_=dr["bmod"][layer][(n // 2) * 512:(n // 2 + 1) * 512].partition_broadcast(128)),
                      w=["EP"], dk="mbias")
                for ci in range(2):
                    p = pmod[ci]
                    for kt in range(8):
                        A("pe", lambda e, kt=kt, p=p, wv=wv, ci=ci: e.matmul(p[:], lhsT=screpv[ci][:, kt, :], rhs=wv[:, kt, :], start=(kt == 0), stop=(kt == 7)),
                          r=["EN", wk], w=[f"pmod{ci}"])
                    c0 = n * 256
                    rb, rk = rows[ci][0] if c0 < 2048 else rows[ci][1]
                    cc0 = c0 if c0 < 2048 else c0 - 2048
                    A("dve", lambda e, p=p, rb=rb, cc0=cc0, n=n: e.tensor_tensor(out=rb[:, cc0:cc0 + 256], in0=p[:], in1=biasf[:, (n % 2) * 256:(n % 2) * 256 + 256], op=ALU.add),
                      r=[f"pmod{ci}", "EP"], w=[rk])
                emit_mod_dma(idx + NSL)
                if n == 11:
                    for ci in range(2):
                        (ra, rak), (rb2, rbk) = rows[ci]
                        A("sp", lambda e, ra=ra, layer=layer, ci=ci: e.dma_start(out=dr["s_mod"][layer, ci:ci + 1, 0:2048], in_=ra[0:1, :]),
                          r=[rak], w=["s_mod"], dk="smw")
                        A("sp", lambda e, rb2=rb2, layer=layer, ci=ci: e.dma_start(out=dr["s_mod"][layer, ci:ci + 1, 2048:3072], in_=rb2[0:1, :]),
                          r=[rbk], w=["s_mod"], dk="smw")
            for i0_ in range(NSL):
                emit_mod_dma(i0_)
        mod_next = [0]
        mF4 = maskF[:].unsqueeze(1).to_broadcast([128, 4, 128])
        mB4 = maskB[:].unsqueeze(1).to_broadcast([128, 4, 128])
        T1v = T1[:].rearrange("q a b c -> q (a b c)")[:, 0:512].rearrange("q (g n) -> q g n", n=128)
        T2v = T2[:].rearrange("q a b c -> q (a b c)")[:, 0:512].rearrange("q (g n) -> q g n", n=128)
        for blk in range(16):
            for gl in range(4):
                g = 4 * blk + gl
                for (pp, lo) in [(pMf, 0), (pMb, 64)]:
                    A("pe", lambda e, pp=pp, lo=lo, g=g, gl=gl: e.matmul(pp[:, gl, :], lhsT=PRb[lo:lo + 64, g, :], rhs=Qrb[lo:lo + 64, g, :],
                                                                       start=True, stop=False),
                      r=["PRb", "Qrb"], w=[pp.name])
                    A("pe", lambda e, pp=pp, lo=lo, g=g, gl=gl: e.matmul(pp[:, gl, :], lhsT=PIb[lo:lo + 64, g, :], rhs=nQib[lo:lo + 64, g, :],
                                                                       start=False, stop=True),
                      r=["PIb", "nQib"], w=[pp.name])
            A("dve", lambda e: e.tensor_tensor(out=T1v, in0=pMf[:], in1=mF4, op=ALU.mult), r=[pMf.name, "maskF"], w=[T1.name])
            A("dve", lambda e: e.tensor_tensor(out=T2v, in0=pMb[:], in1=mB4, op=ALU.mult), r=[pMb.name, "maskB"], w=[T2.name])
            A("dve", lambda e: e.tensor_tensor(out=T1v, in0=T1v, in1=T2v, op=ALU.add), r=[T1.name, T2.name], w=[T1.name])
            for gl in range(4):
                g = 4 * blk + gl
                A("dve", lambda e, g=g, gl=gl: e.scalar_tensor_tensor(out=Mop[:, g, :], in0=identf[:], scalar=Dcol[:, g:g + 1], in1=T1v[:, gl, :],
                                                                      op0=ALU.mult, op1=ALU.add), r=["identf", "Dcol", T1.name], w=["Mop"])
            if do_mod:
                for _ in range(2):
                    if mod_next[0] < 24:
                        emit_mod_chunk(mod_next[0])
                        mod_next[0] += 1
        A("sp", lambda e: e.dma_start(out=dr["s_opM"], in_=Mop[:]), r=["Mop"], w=["s_opM"], dk="s_opM")
        run_block(nc, tr, tr.all_tokens())
    return tr


class Main:
    def __init__(self, nc, dr, sems, pre_tokens, debug=None):
        self.nc = nc
        self.dr = dr
        self.tr = Tracker(sems)
        self.pre = pre_tokens
        self.rot = {}
        self.out_tokens = []
        self.debug = debug or {}

    def A(self, eng, fn, r=(), w=(), dk=None):
        return self.tr.add(eng, fn, reads=r, writes=w, dma_key=dk)

    def nxt(self, name, n):
        i = self.rot.get(name, 0)
        self.rot[name] = (i + 1) % n
        return i

    def alloc(self, es):
        nc = self.nc

        def sb(name, shape, dt):
            return es.enter_context(nc.sbuf_tensor("m_" + name, shape, dt))

        def ps(name, shape, dt):
            return es.enter_context(nc.psum_tensor("m_" + name, shape, dt))
        self.hT = sb("hT", [128, 8, 1024], BF16)
        self.AB = sb("AB", [128, 16384], BF16)
        self.C = sb("C", [128, 16384], BF16)
        self.T = sb("T", [128, 16384], BF16)
        self.wr = [sb(f"wr{i}", [128, 2048], BF16) for i in range(6)]
        self.mrep = sb("mrep", [128, 3072], F32)
        self.lngr = sb("lngr", [128, 1024], F32)
        self.lnbr = sb("lnbr", [128, 1024], F32)
        self.xt = [sb(f"xt{i}", [128, 1024], F32) for i in range(2)]
        self.xn = [sb(f"xn{i}", [128, 1024], F32) for i in range(2)]
        self.hb = [sb(f"hb{i}", [128, 1024], BF16) for i in range(2)]
        self.tmp = [sb(f"tmp{i}", [128, 512], F32) for i in range(4)]
        self.ost = [sb(f"ost{i}", [128, 1024], F32) for i in range(2)]
        self.statsT = sb("statsT", [128, 2, 8, 2, 6], F32)
        self.mvT = sb("mvT", [128, 2, 8, 2], F32)
        self.rstdT = sb("rstdT", [128, 2, 8], F32)
        self.nmrT = sb("nmrT", [128, 2, 8], F32)
        self.identf = sb("identf", [128, 128], F32)
        self.identb = sb("identb", [128, 128], BF16)
        self.onesb = sb("onesb", [128, 128], BF16)
        self.onesf = sb("onesf", [128, 128], F32)
        self.pswf = sb("pswf", [128, 128], F32)
        self.pswb = sb("pswb", [128, 128], BF16)
        self.A8 = sb("A8", [128, 2, 64], F32)
        self.nlam = sb("nlam", [128, 1], F32)
        self.subg = sb("subg", [128, 1], F32)
        self.bglu = sb("bglu", [128, 8], F32)
        self.bfno = sb("bfno", [128, 16], F32)
        self.cc = sb("cc", [128, 8], F32)
        self.sc = sb("sc", [128, 8], F32)
        self.screp = sb("screp", [128, 8, 128], BF16)
        self.st = sb("st", [128, 2, 32, 4], F32)
        self.st2 = sb("st2", [128, 2, 32, 4], F32)
        self.zz = sb("zz", [128, 2, 32, 4], F32)
        self.q12 = sb("q12", [128, 2, 2, 32, 4], F32)
        self.a8 = sb("a8", [128, 2, 2, 32, 4], F32)
        self.epsc = sb("epsc", [128, 1], F32)
        self.fin = sb("fin", [128, 2, 4, 64], F32)
        self.pf = [ps(f"pf{i}", [128, 512], F32) for i in range(6)]
        self.pt = [ps(f"pt{i}", [128, 8, 128], BF16) for i in range(2)]

    def load_consts(self):
        A, dr = self.A, self.dr
        A("sp", lambda e: e.dma_start(out=self.identf[:], in_=dr["ident"]), w=["identf"], dk="identf")
        A("dve", lambda e: e.tensor_copy(out=self.identb[:], in_=self.identf[:]), r=["identf"], w=["identb"])
        A("dve", lambda e: e.memset(self.onesb[:], 1.0), w=["onesb"])
        A("dve", lambda e: e.memset(self.epsc[:], LN_EPS), w=["epsc"])
        A("dve", lambda e: e.memset(self.onesf[:], 1.0 / 128), w=["onesf"])
        A("sp", lambda e: e.dma_start(out=self.pswf[:], in_=dr["pswap"]), w=["pswf"], dk="pswf")
        A("dve", lambda e: e.tensor_copy(out=self.pswb[:], in_=self.pswf[:]), r=["pswf"], w=["pswb"])
        A("sp", lambda e: e.dma_start(out=self.A8[:], in_=dr["s_A8"]), w=["A8"], dk="A8")
        A("sp", lambda e: e.dma_start(out=self.nlam[:], in_=dr["s_nlam"]), w=["nlam"], dk="nlam")
        A("sp", lambda e: e.dma_start(out=self.subg[:], in_=dr["subg"].rearrange("(p o) -> p o", o=1)), w=["subg"], dk="subg")
        A("dve", lambda e: e.tensor_scalar_mul(out=self.subg[:], in0=self.subg[:], scalar1=0.8), r=["subg"], w=["subg"])
        A("sp", lambda e: e.dma_start(out=self.bglu[:], in_=dr["bglu"].rearrange("(k p) -> p k", p=128), allow_slow_non_contiguous=True),
          w=["bglu"], dk="bglu")
        A("sp", lambda e: e.dma_start(out=self.bfno[:], in_=dr["bfno"].rearrange("(k p) -> p k", p=128), allow_slow_non_contiguous=True),
          w=["bfno"], dk="bfno")

    def wload(self, src2d, ktiles, ncols):
        i = self.nxt("wr", 6)
        assert ktiles * ncols <= 2048
        view = self.wr[i][:, 0:ktiles * ncols].rearrange("p (k n) -> p k n", n=ncols)
        srcv = src2d.rearrange("(k p) n -> p k n", p=128)
        self.A("pool", lambda e: e.dma_start(out=view, in_=srcv), w=[f"wr{i}"], dk=f"wr{i}")
        return view, f"wr{i}"

    def mod_vectors(self, layer, cond_ap):
        A, dr = self.A, self.dr
        A("sp", lambda e: e.dma_start(out=self.cc[:], in_=cond_ap.rearrange("(k p) -> p k", p=128), allow_slow_non_contiguous=True),
          w=["cc"], dk="cc")
        A("act", lambda e: e.activation(out=self.sc[:], in_=self.cc[:], func=AF.Silu), r=["cc"], w=["sc"])
        A("dve", lambda e: e.tensor_copy(out=self.screp[:], in_=self.sc[:].unsqueeze(2).to_broadcast([128, 8, 128])), r=["sc"], w=["screp"])
        for n in range(12):
            wv, wk = self.wload(dr["wmod"][layer][:, n * 256:(n + 1) * 256], 8, 256)
            pi = self.nxt("pf01", 2)
            p = self.pf[pi]
            if n % 2 == 0:
                nn = n // 2
                A("sp", lambda e, nn=nn: e.dma_start(out=self.tmp[3][:], in_=dr["bmod"][layer][nn * 512:(nn + 1) * 512].partition_broadcast(128)),
                  w=["tmp3"], dk="brep")
            for kt in range(8):
                A("pe", lambda e, kt=kt, p=p, wv=wv: e.matmul(p[:, 0:256], lhsT=self.screp[:, kt, :], rhs=wv[:, kt, :], start=(kt == 0), stop=(kt == 7)),
                  r=["screp", wk], w=[f"pf{pi}"])
            A("dve", lambda e, n=n, p=p: e.tensor_tensor(out=self.mrep[:, n * 256:(n + 1) * 256], in0=p[:, 0:256],
                                                         in1=self.tmp[3][:, (n % 2) * 256:(n % 2) * 256 + 256], op=ALU.add),
              r=[f"pf{pi}", "tmp3"], w=["mrep"])
        A("dve", lambda e: e.tensor_scalar_add(out=self.mrep[:, 1024:2048], in0=self.mrep[:, 1024:2048], scalar1=1.0), r=["mrep"], w=["mrep"])

    def precompute_mod(self):
        A, dr = self.A, self.dr
        screps = [self.screp[:], self.hb[1][:].rearrange("p (k n) -> p k n", n=128)]
        skeys = ["screp", "hb1"]
        for ci, cond_ap in enumerate([dr["cctx"], dr["csmp"]]):
            A("sp", lambda e, cond_ap=cond_ap: e.dma_start(out=self.cc[:], in_=cond_ap.rearrange("(k p) -> p k", p=128), allow_slow_non_contiguous=True),
              w=["cc"], dk="cc")
            A("act", lambda e: e.activation(out=self.sc[:], in_=self.cc[:], func=AF.Silu), r=["cc"], w=["sc"])
            A("dve", lambda e, ci=ci: e.tensor_copy(out=screps[ci], in_=self.sc[:].unsqueeze(2).to_broadcast([128, 8, 128])), r=["sc"], w=[skeys[ci]])
        def rowdst(ci, n):
            c0 = n * 256
            if ci == 0:
                return self.mrep[:, c0:c0 + 256], "mrep"
            t, k = [(self.xn[0], "xn0"), (self.xn[1], "xn1"), (self.xt[0], "xt0")][c0 // 1024]
            return t[:, c0 % 1024:c0 % 1024 + 256], k
        for layer in range(2):
            for n in range(12):
                wv, wk = self.wload(dr["wmod"][layer][:, n * 256:(n + 1) * 256], 8, 256)
                if n % 2 == 0:
                    nn = n // 2
                    A("sp", lambda e, nn=nn, layer=layer: e.dma_start(out=self.tmp[3][:], in_=dr["bmod"][layer][nn * 512:(nn + 1) * 512].partition_broadcast(128)),
                      w=["tmp3"], dk="brep")
                for ci in range(2):
                    pi = self.nxt("pf01", 2)
                    p = self.pf[pi]
                    for kt in range(8):
                        A("pe", lambda e, kt=kt, p=p, wv=wv, ci=ci: e.matmul(p[:, 0:256], lhsT=screps[ci][:, kt, :], rhs=wv[:, kt, :], start=(kt == 0), stop=(kt == 7)),
                          r=[skeys[ci], wk], w=[f"pf{pi}"])
                    dst, dk_ = rowdst(ci, n)
                    A("dve", lambda e, n=n, p=p, dst=dst: e.tensor_tensor(out=dst, in0=p[:, 0:256], in1=self.tmp[3][:, (n % 2) * 256:(n % 2) * 256 + 256], op=ALU.add),
                      r=[f"pf{pi}", "tmp3"], w=[dk_])
            A("sp", lambda e, layer=layer: e.dma_start(out=dr["s_mod"][layer, 0:1, :], in_=self.mrep[0:1, :]), r=["mrep"], w=["s_mod"], dk="smw0")
            for j, (t, k) in enumerate([(self.xn[0], "xn0"), (self.xn[1], "xn1"), (self.xt[0], "xt0")]):
                A("sp", lambda e, layer=layer, j=j, t=t: e.dma_start(out=dr["s_mod"][layer, 1:2, j * 1024:(j + 1) * 1024], in_=t[0:1, :]),
                  r=[k], w=["s_mod"], dk=f"smw{j + 1}")

    def load_mod(self, layer, ci):
        A, dr = self.A, self.dr
        A("sp", lambda e: e.dma_start(out=self.mrep[:], in_=dr["s_mod"][layer, ci].partition_broadcast(128)), r=["s_mod"], w=["mrep"], dk="mrepld")
        A("dve", lambda e: e.tensor_scalar_add(out=self.mrep[:, 1024:2048], in0=self.mrep[:, 1024:2048], scalar1=1.0), r=["mrep"], w=["mrep"])


    def ln_stats_t(self, src, skey, tt, S):
        A = self.A
        k = f"ln{S}_{tt}"
        stats, mv = self.statsT[:, S, tt], self.mvT[:, S, tt]
        rstd, nmr = self.rstdT[:, S, tt:tt + 1], self.nmrT[:, S, tt:tt + 1]
        for c in range(2):
            A("dve", lambda e, c=c: e.bn_stats(out=stats[:, c, :], in_=src[:, c * 512:(c + 1) * 512]), r=[skey], w=[k + "s"])
        A("dve", lambda e: e.bn_aggr(out=mv, in_=stats), r=[k + "s"], w=[k + "m"])
        A("act", lambda e: e.activation(out=rstd, in_=mv[:, 1:2], func=AF.Sqrt, bias=LN_EPS, scale=1.0), r=[k + "m"], w=[k + "r"])
        A("dve", lambda e: e.reciprocal(out=rstd, in_=rstd), r=[k + "r"], w=[k + "r"])
        A("dve", lambda e: e.tensor_scalar(out=nmr, in0=mv[:, 0:1], scalar1=-1.0, scalar2=rstd, op0=ALU.mult, op1=ALU.mult),
          r=[k + "m", k + "r"], w=[k + "n"])
        return rstd, nmr, [k + "r", k + "n"]

    def ln_mod_tiles(self, tiles, st=None):
        A = self.A
        if st is None:
            st = [self.ln_stats_t(src, key, tt, 1) for (src, key, tt) in tiles]
        n = len(tiles)
        info = {}

        def norm(i):
            (src, key, tt), (rstd, nmr, lk) = tiles[i], st[i]
            xb = self.nxt("xn", 2)
            xn = self.xn[xb]
            A("act", lambda e, xn=xn, src=src, nmr=nmr, rstd=rstd: e.activation(out=xn[:], in_=src, func=AF.Identity, bias=nmr, scale=rstd),
              r=[key] + lk, w=[f"xn{xb}"])
            info[i] = (xn, xb)
        norm(0)
        for i in range(n):
            (src, key, tt) = tiles[i]
            xn, xb = info[i]
            b = self.nxt("hb", 2)
            hb = self.hb[b]
            A("dve", lambda e, xn=xn: e.tensor_tensor(out=xn[:], in0=xn[:], in1=self.mrep[:, 1024:2048], op=ALU.mult), r=[f"xn{xb}", "mrep"], w=[f"xn{xb}"])
            A("dve", lambda e, xn=xn, hb=hb: e.tensor_tensor(out=hb[:], in0=xn[:], in1=self.mrep[:, 0:1024], op=ALU.add), r=[f"xn{xb}", "mrep"], w=[f"hb{b}"])
            if i + 1 < n:
                norm(i + 1)
            pi = self.nxt("pt", 2)
            pt = self.pt[pi]
            for kt in range(8):
                A("pe", lambda e, kt=kt, pt=pt, hb=hb: e.transpose(pt[:, kt, :], hb[:, kt * 128:(kt + 1) * 128], self.identb[:]),
                  r=[f"hb{b}", "identb"], w=[f"pt{pi}"])
            A("act", lambda e, pt=pt, tt=tt: e.activation(out=self.hT[:, :, tt * 128:(tt + 1) * 128], in_=pt[:], func=AF.Copy), r=[f"pt{pi}"], w=["hT"])

    def fence(self, reads, writes):
        if not hasattr(self, "_dummy"):
            raise RuntimeError("alloc dummy first")
        self.A("dve", lambda e: e.memset(self._dummy[:], 0.0), r=list(reads), w=list(writes) + ["_dummy"])

    REG = {
        "AB": ["A", "gaA", "attA", "mixA"] + [f"Z{t}" for t in range(8)] + [ "ropeC", "ropeS", "qraw", "gz0", "gz1", "vh", "qT", "kT"] + [f"PT{i}" for i in range(4)] + [f"B{k}" for k in range(8)],
        "C": ["CX", "Ctab"] + [f"C{k}" for k in range(16)],
        "T": ["Wb0", "Wb1", "MQb0", "MQb1", "MQb0q", "MQb1q", "Sb0", "Sb1", "qT", "kT", "cs256", "ucs", "uT0", "uT1"],
    }

    def rfence(self, reg):
        ks = self.REG[reg]
        self.fence(ks, ks)

    def alloc2(self, es):
        self._dummy = es.enter_context(self.nc.sbuf_tensor("m_dummy", [128, 2], F32))

    def s5_path(self, grp):
        A, dr = self.A, self.dr
        isS = (grp == "S")
        ns = 1 if isS else 4
        NS = 128 // ns
        ua = self.AB[:, 0:8192].rearrange("p (g s m) -> p g s m", s=8, m=16)
        ga_t = self.AB[:, 0:8192].rearrange("p (j n) -> p j n", n=1024)
        ug = self.AB[:, 8192:16384].rearrange("p (g c) -> p g c", c=128)
        gaT = self.AB[:, 8192:16384].rearrange("p (k t) -> p k t", t=1024)
        Xp = self.C[:].bitcast(F32).rearrange("p (r g c) -> p r g c", r=2, g=32)
        Wb = [self.T[:, i * 1024:(i + 1) * 1024].rearrange("p (g r n) -> p g r n", g=4, r=2) for i in range(2)]
        MQb = [self.T[:, 2048 + i * 1536:2048 + (i + 1) * 1536] for i in range(2)]
        Sb = self.T[:, 5120:5120 + 8192].rearrange("p (r g c) -> p r g c", r=2, g=32)
        ALLB = [f"B{k}" for k in range(8)]
        pre = [self.wload(dr["win_e"][:, cb * 256:(cb + 1) * 256], 8, 256) for cb in range(4)]
        self.rfence("AB"); self.rfence("C"); self.rfence("T")
        for cb in range(4):
            wv, wk = pre[cb]
            for s in range(8):
                pi = self.nxt("pf01", 2)
                p = self.pf[pi]
                for kt in range(8):
                    A("pe", lambda e, kt=kt, s=s, p=p, wv=wv: e.matmul(p[:, 0:256], lhsT=self.hT[:, kt, s:1024:8], rhs=wv[:, kt, :],
                                                                      start=(kt == 0), stop=(kt == 7)), r=["hT", wk], w=[f"pf{pi}"])
                eng = "act" if s % 2 == 0 else "dve"
                if eng == "act":
                    A("act", lambda e, s=s, cb=cb, p=p: e.activation(out=ua[:, cb * 16:(cb + 1) * 16, s, :],
                                                                     in_=p[:, 0:256].rearrange("q (g m) -> q g m", m=16), func=AF.Copy),
                      r=[f"pf{pi}"], w=["A"])
                else:
                    A("dve", lambda e, s=s, cb=cb, p=p: e.tensor_copy(out=ua[:, cb * 16:(cb + 1) * 16, s, :],
                                                                      in_=p[:, 0:256].rearrange("q (g m) -> q g m", m=16)),
                      r=[f"pf{pi}"], w=["A"])
        for blk in range(8):
            pi = self.nxt("pt", 2)
            pt = self.pt[pi]
            for gl in range(8):
                g = 8 * blk + gl
                A("pe", lambda e, gl=gl, g=g, pt=pt: e.transpose(pt[:, gl, :], ua[:, g, :, :].rearrange("q s m -> q (s m)"), self.identb[:]),
                  r=["A", "identb"], w=[f"pt{pi}"])
            A("act", lambda e, blk=blk, pt=pt: e.activation(out=ug[:, 8 * blk:8 * blk + 8, :], in_=pt[:], func=AF.Copy),
              r=[f"pt{pi}"], w=[f"B{blk}"])
        self.fence(["A"], ["gaA"])
        for half in range(2):
            g0 = 32 * half
            self.fence([f"C{k}" for k in range(16)], ["CX"])
            for blk in range(8):
                bi = self.nxt("Wb", 2)
                A("sp", lambda e, bi=bi, blk=blk, g0=g0: e.dma_start(out=Wb[bi], in_=dr["s_opW"][:, g0 + 4 * blk:g0 + 4 * blk + 4]),
                  w=[f"Wb{bi}"], dk=f"Wb{bi}")
                for ri in range(2):
                    pi = 2 + self.nxt("pf23", 2)
                    p = self.pf[pi]
                    for gl in range(4):
                        g = g0 + 4 * blk + gl
                        A("pe", lambda e, p=p, gl=gl, g=g, ri=ri, bi=bi: e.matmul(p[:, gl * 128:(gl + 1) * 128], lhsT=Wb[bi][:, gl, ri, :],
                                                                                  rhs=ug[:, g, :], start=True, stop=True),
                          r=[f"Wb{bi}", f"B{g // 8}"], w=[f"pf{pi}"])
                    A("act", lambda e, p=p, ri=ri, blk=blk: e.activation(out=Xp[0:64, ri, 4 * blk:4 * blk + 4, :],
                                                                         in_=p[0:64, :].rearrange("q (g c) -> q g c", c=128), func=AF.Copy),
                      r=[f"pf{pi}"], w=["CX"])
                    A("dve", lambda e, p=p, ri=ri, blk=blk: e.tensor_copy(
                        out=Xp[64:128, ri, 4 * blk:4 * blk + 4, :].rearrange("q g (s c) -> q g s c", s=ns),
                        in_=p[64:128, :].rearrange("q (g s c) -> q g s c", g=4, s=ns)[:, :, :, ::-1]),
                      r=[f"pf{pi}"], w=["CX"])
            a8r_b = self.A8[:, 0, g0:g0 + 32].unsqueeze(1).unsqueeze(3).to_broadcast([128, 2, 32, ns])
            A("dve", lambda e, a8r_b=a8r_b: e.tensor_copy(out=self.a8[:, 0, :, :, 0:ns], in_=a8r_b), r=["A8"], w=["a8"])
            a8i_b = self.A8[:, 1, g0:g0 + 32].unsqueeze(2).to_broadcast([128, 32, ns])
            A("dve", lambda e, a8i_b=a8i_b: e.tensor_copy(out=self.a8[:, 1, 0, :, 0:ns], in_=a8i_b), r=["A8"], w=["a8"])
            A("dve", lambda e: e.tensor_scalar_mul(out=self.a8[:, 1, 1, :, 0:ns], in0=self.a8[:, 1, 0, :, 0:ns], scalar1=-1.0), r=["a8"], w=["a8"])
            if isS:
                for (ri, nm) in [(0, "st_re"), (1, "st_im")]:
                    for d in range(2):
                        A("sp", lambda e, ri=ri, nm=nm, d=d, g0=g0: e.dma_start(
                            out=self.st[64 * d:64 * d + 64, ri, :, 0:1],
                            in_=dr[nm][d, g0:g0 + 32, :].rearrange("g (p o) -> p g o", o=1), allow_slow_non_contiguous=True),
                          w=["st"], dk=f"st{ri}{d}")
            else:
                A("dve", lambda e: e.memset(self.st[:], 0.0), w=["st"])
            def cs(k):
                return slice(k, 128, NS) if ns > 1 else slice(k, k + 1)

            def csb(k):
                return cs(NS - 1 - k)
            stb = [self.st, self.st2]
            zz = self.zz[:, :, :, 0:ns]
            zsw = self.zz[:, ::-1, :, 0:ns]
            q12 = self.q12[:, :, :, :, 0:ns]
            q1 = self.q12[:, 0, :, :, 0:ns]
            q2s = self.q12[:, 1, ::-1, :, 0:ns]
            a8v = self.a8[:, :, :, :, 0:ns]
            zz2 = self.zz[:, :, :, 0:ns].unsqueeze(1).to_broadcast([128, 2, 2, 32, ns])

            def store(k, sbuf, skey):
                cf, cb_ = cs(k), csb(k)
                A("act", lambda e, cf=cf, sbuf=sbuf: e.activation(out=Sb[0:64, :, :, cf], in_=sbuf[0:64, :, :, 0:ns], func=AF.Copy), r=[skey], w=["Sb0"])
                A("pool", lambda e, cb_=cb_, sbuf=sbuf: e.tensor_copy(out=Sb[64:128, :, :, cb_], in_=sbuf[64:128, :, :, 0:ns]), r=[skey], w=["Sb1"])
            store(0, self.st, "st")
            A("dve", lambda e, c0=cs(0): e.tensor_tensor(out=zz, in0=self.st[:, :, :, 0:ns], in1=Xp[:, :, :, c0], op=ALU.add), r=["st", "CX"], w=["zz"])
            cur, curk = self.st, "st"
            for k in range(NS):
                nxt_, nxtk = (self.st2, "st2") if cur is self.st else (self.st, "st")
                if k == NS - 1:
                    nxt_, nxtk = self.st, "st"
                    if cur is self.st:
                        pass
                A("dve", lambda e: e.tensor_tensor(out=q12, in0=a8v, in1=zz2, op=ALU.mult), r=["a8", "zz"], w=["q12"])
                A("dve", lambda e, nxt_=nxt_: e.tensor_tensor(out=nxt_[:, :, :, 0:ns], in0=q1, in1=q2s, op=ALU.add), r=["q12"], w=[nxtk])
                cur, curk = nxt_, nxtk
                if k < NS - 1:
                    store(k + 1, cur, curk)
                    A("dve", lambda e, cn=cs(k + 1), cur=cur: e.tensor_tensor(out=zz, in0=cur[:, :, :, 0:ns], in1=Xp[:, :, :, cn], op=ALU.add), r=[curk, "CX"], w=["zz"])
            if not isS:
                A("act", lambda e, g0=g0: e.activation(out=self.fin[:, :, :, g0:g0 + 32].rearrange("q r s g -> q r g s"),
                                                     in_=self.st[:, :, :, 0:4], func=AF.Copy), r=["st"], w=["fin"])
            ga = ga_t
            for blk in range(8):
                bi = self.nxt("MQb", 2)
                Mv = MQb[bi][:, 0:512].rearrange("p (g n) -> p g n", n=128)
                Qv = MQb[bi][:, 512:1536].rearrange("p (g r n) -> p g r n", g=4, r=2)
                gq = g0 + 4 * blk
                A("sp", lambda e, Mv=Mv, gq=gq: e.dma_start(out=Mv, in_=dr["s_opM"][:, gq:gq + 4]), w=[f"MQb{bi}"], dk=f"Mb{bi}")
                A("sp", lambda e, Qv=Qv, gq=gq: e.dma_start(out=Qv, in_=dr["s_opQ"][:, gq:gq + 4]), w=[f"MQb{bi}q"], dk=f"Qb{bi}")
                pi = self.nxt("pf01", 2)
                p = self.pf[pi]
                pv = p[:].rearrange("q (g j n) -> q g j n", j=8, n=16)
                for gl in range(4):
                    g = gq + gl
                    gh = g - g0
                    A("pe", lambda e, p=p, gl=gl, g=g, Mv=Mv: e.matmul(p[:, gl * 128:(gl + 1) * 128], lhsT=ug[:, g, :],
                                                                        rhs=Mv[:, gl, :], start=True, stop=False),
                      r=[f"B{g // 8}", f"MQb{bi}"], w=[f"pf{pi}"])
                    for ri in range(2):
                        A("pe", lambda e, p=p, gl=gl, gh=gh, ri=ri, Qv=Qv: e.matmul(p[:, gl * 128:(gl + 1) * 128], lhsT=Sb[:, ri, gh, :],
                                                                                     rhs=Qv[:, gl, ri, :],
                                                                                     start=False, stop=(ri == 1)),
                          r=["Sb0", "Sb1", f"MQb{bi}q"], w=[f"pf{pi}"])
                t0, t1 = self.tmp[0], self.tmp[1]
                A("act", lambda e, p=p: e.activation(out=t0[:], in_=p[:], func=AF.Square, scale=math.sqrt(0.044715)), r=[f"pf{pi}"], w=["tmp0"])
                A("dve", lambda e, p=p: e.scalar_tensor_tensor(out=t0[:], in0=t0[:], scalar=1.0, in1=p[:], op0=ALU.add, op1=ALU.mult),
                  r=["tmp0", f"pf{pi}"], w=["tmp0"])
                A("act", lambda e: e.activation(out=t1[:], in_=t0[:], func=AF.Sigmoid, scale=GELU_C), r=["tmp0"], w=["tmp1"])
                gav = ga[:, :, gq * 16:gq * 16 + 64].rearrange("q j (g n) -> q g j n", n=16)
                A("dve", lambda e, pv=pv, gav=gav: e.tensor_tensor(out=gav, in0=t1[:].rearrange("q (g j n) -> q g j n", j=8, n=16), in1=pv, op=ALU.mult),
                  r=["tmp1", f"pf{pi}"], w=["gaA"])
                if blk % 2 == 1:
                    kt = gq // 8
                    pti = self.nxt("pt", 2)
                    pt = self.pt[pti]
                    for j in range(8):
                        A("pe", lambda e, pt=pt, j=j, kt=kt: e.transpose(pt[:, j, :], ga[:, j, kt * 128:(kt + 1) * 128], self.identb[:]),
                          r=["gaA", "identb"], w=[f"pt{pti}"])
                    A("act", lambda e, pt=pt, kt=kt: e.activation(out=gaT[:, kt, :].rearrange("q (c j) -> q j c", j=8), in_=pt[:], func=AF.Copy),
                      r=[f"pt{pti}"], w=[f"B{kt}"])
            self.fence(["CX"], [f"C{k}" for k in range(16)])
        if not isS:
            for (ri, nm) in [(0, "fre"), (1, "fim")]:
                for sp2 in range(2):
                    pj = 2 + self.nxt("pf23", 2)
                    p2 = self.pf[pj]
                    A("pe", lambda e, p2=p2, ri=ri, sp2=sp2: e.transpose(p2[:, 0:128], self.fin[:, ri, 2 * sp2:2 * sp2 + 2, :].rearrange("q s g -> q (s g)"),
                                                                       self.identf[:]), r=["fin", "identf"], w=[f"pf{pj}"])
                    oi = self.nxt("ost", 2)
                    ost = self.ost[oi]
                    A("dve", lambda e, p2=p2, ost=ost: e.tensor_copy(out=ost[:, 0:128], in_=p2[:, 0:128]), r=[f"pf{pj}"], w=[f"ost{oi}"])
                    for sl in range(2):
                        sq = 2 * sp2 + sl
                        tk = A("sp", lambda e, ost=ost, sl=sl, sq=sq, nm=nm: e.dma_start(
                            out=dr[nm][sq].rearrange("d g p -> g d p"), in_=ost[64 * sl:64 * sl + 64, 0:128].rearrange("q (d p) -> q d p", p=64)),
                            r=[f"ost{oi}"], w=[nm], dk=f"ostd{oi}")
                        self.out_tokens.append(tk)

    def glu_path(self):
        A, dr = self.A, self.dr
        gaT = self.AB[:, 8192:16384].rearrange("p (k t) -> p k t", t=1024)
        yT = self.C[:].rearrange("p (k t) -> p k t", t=1024)
        ALLB = [f"B{k}" for k in range(8)]
        for ot in range(8):
            wg, wgk = self.wload(dr["wglu"][:, ot * 128:(ot + 1) * 128], 8, 128)
            wz, wzk = self.wload(dr["win_e"][:, 1024 + ot * 128:1024 + (ot + 1) * 128], 8, 128)
            for tb in range(2):
                ts = slice(tb * 512, (tb + 1) * 512)
                pi = self.nxt("pf01", 2)
                p = self.pf[pi]
                for kt in range(8):
                    A("pe", lambda e, p=p, kt=kt, wg=wg, ts=ts: e.matmul(p[:], lhsT=wg[:, kt, :], rhs=gaT[:, kt, ts], start=(kt == 0), stop=(kt == 7)),
                      r=[wgk] + ALLB, w=[f"pf{pi}"])
                pj = 2 + self.nxt("pf23", 2)
                p2 = self.pf[pj]
                for kt in range(8):
                    A("pe", lambda e, p2=p2, kt=kt, wz=wz, ts=ts: e.matmul(p2[:], lhsT=wz[:, kt, :], rhs=self.hT[:, kt, ts], start=(kt == 0), stop=(kt == 7)),
                      r=[wzk, "hT"], w=[f"pf{pj}"])
                t0, t1 = self.tmp[0], self.tmp[1]
                A("act", lambda e, p=p, ot=ot: e.activation(out=t0[:], in_=p[:], func=AF.Sigmoid, bias=self.bglu[:, ot:ot + 1], scale=1.0),
                  r=[f"pf{pi}", "bglu"], w=["tmp0"])
                A("act", lambda e, p2=p2: e.activation(out=t1[:], in_=p2[:], func=AF.Sigmoid), r=[f"pf{pj}"], w=["tmp1"])
                A("dve", lambda e, p2=p2: e.tensor_tensor(out=t1[:], in0=t1[:], in1=p2[:], op=ALU.mult), r=["tmp1", f"pf{pj}"], w=["tmp1"])
                A("dve", lambda e, ot=ot, ts=ts: e.tensor_tensor(out=t0[:], in0=t0[:], in1=gaT[:, ot, ts], op=ALU.mult), r=["tmp0", f"B{ot}"], w=["tmp0"])
                A("dve", lambda e, ot=ot, ts=ts: e.tensor_tensor(out=yT[:, ot, ts], in0=t0[:], in1=t1[:], op=ALU.mult), r=["tmp0", "tmp1"], w=[f"C{ot}"])

    def attn_path(self, grp):
        A, dr = self.A, self.dr
        isS = (grp == "S")
        yT = self.C[:].rearrange("p (k t) -> p k t", t=1024)
        TB = 13056
        qTm = [self.AB[:, 8192:9216], self.AB[:, 9216:10240]]
        kT = self.AB[:, 10240:10240 + 1280]
        vh = self.AB[:, 6144:6144 + 1280].rearrange("p (t e) -> p t e", e=128)
        gzb = [self.AB[:, 0:1024], self.AB[:, 11520:12544]]
        self._blk = 0
        self._pend = None
        qraw = self.AB[:, 1024:2048]
        PT = [self.AB[:, 2048 + i * 512:2048 + (i + 1) * 512] for i in range(4)]
        ropeC = self.AB[:, 4096:5120]
        ropeS = self.AB[:, 5120:6144]
        cst = self.xt[0]
        oacc = [self.tmp[2], self.tmp[3]]
        self.rfence("AB")
        A("dve", lambda e: e.memset(qTm[0][64:128, :], 0.0), w=["qT"])
        A("dve", lambda e: e.memset(qTm[1][0:64, :], 0.0), w=["qT"])
        nkt = 10 if isS else 2
        koff = 256 if isS else 0
        if isS:
            A("pool", lambda e: e.dma_start(out=ropeC, in_=dr["ropeC"]), w=["ropeC"], dk="ropeC")
            A("pool", lambda e: e.dma_start(out=ropeS, in_=dr["ropeS"]), w=["ropeS"], dk="ropeS")
        seqs = [(0, 1024)] if isS else [(i * 256, 256) for i in range(4)]
        for h in range(8):
            gz, gzk = gzb[h % 2], f"gz{h % 2}"
            wq, wqk = self.wload(dr["win_e"][:, 2048 + h * 128:2048 + (h + 1) * 128], 8, 128)
            wk_, wkk = self.wload(dr["win_e"][:, 3072 + h * 128:3072 + (h + 1) * 128], 8, 128)
            wv_, wvk = self.wload(dr["win_e"][:, 4096 + h * 128:4096 + (h + 1) * 128], 8, 128)
            wzb, wzbk = self.wload(dr["win_e"][:, 5120 + h * 128:5120 + (h + 1) * 128], 8, 128)
            for (wv, wkey, dst, dkey, off, kind) in [(wq, wqk, None, "qT", 0, "q"), (wk_, wkk, kT, "kT", koff, "k"), (wzb, wzbk, gz, gzk, 0, "z")]:
                for tb in range(2):
                    ts = slice(tb * 512, (tb + 1) * 512)
                    od = slice(off + tb * 512, off + (tb + 1) * 512)
                    pi = self.nxt("pf01", 2)
                    p = self.pf[pi]
                    for kt in range(8):
                        A("pe", lambda e, p=p, kt=kt, wv=wv, ts=ts: e.matmul(p[:], lhsT=wv[:, kt, :], rhs=self.hT[:, kt, ts], start=(kt == 0), stop=(kt == 7)),
                          r=[wkey, "hT"], w=[f"pf{pi}"])
                    if kind == "z":
                        A("act", lambda e, p=p, dst=dst, od=od: e.activation(out=dst[:, od], in_=p[:], func=AF.Silu), r=[f"pf{pi}"], w=[dkey, "attA"])
                    elif not isS and kind == "q":
                        for mm in range(2):
                            A("act", lambda e, p=p, od=od, mm=mm: e.activation(out=qTm[mm][64 * mm:64 * mm + 64, od], in_=p[64 * mm:64 * mm + 64, :], func=AF.Copy),
                              r=[f"pf{pi}"], w=[dkey])
                    elif not isS:
                        A("act", lambda e, p=p, dst=dst, od=od: e.activation(out=dst[:, od], in_=p[:], func=AF.Copy), r=[f"pf{pi}"], w=[dkey])
                    else:
                        A("act", lambda e, p=p, ts=ts: e.activation(out=qraw[:, 0:512], in_=p[:], func=AF.Copy), r=[f"pf{pi}"], w=["qraw", "attA"])
                        pj = 2 + self.nxt("pf23", 2)
                        p2 = self.pf[pj]
                        A("pe", lambda e, p2=p2: e.matmul(p2[:], lhsT=self.pswb[:], rhs=qraw[:, 0:512], start=True, stop=True),
                          r=["pswb", "qraw"], w=[f"pf{pj}"])
                        t0, t1 = self.tmp[0], self.tmp[1]
                        A("dve", lambda e, ts=ts: e.tensor_tensor(out=t0[:], in0=qraw[:, 0:512], in1=ropeC[:, ts], op=ALU.mult), r=["qraw", "ropeC"], w=["tmp0"])
                        A("dve", lambda e, p2=p2, ts=ts: e.tensor_tensor(out=t1[:], in0=p2[:], in1=ropeS[:, ts], op=ALU.mult), r=[f"pf{pj}", "ropeS"], w=["tmp1"])
                        if kind == "q":
                            for mm in range(2):
                                A("dve", lambda e, od=od, mm=mm: e.tensor_tensor(out=qTm[mm][64 * mm:64 * mm + 64, od], in0=t0[64 * mm:64 * mm + 64, :],
                                                                                in1=t1[64 * mm:64 * mm + 64, :], op=ALU.add), r=["tmp0", "tmp1"], w=[dkey])
                        else:
                            A("dve", lambda e, dst=dst, od=od: e.tensor_tensor(out=dst[:, od], in0=t0[:], in1=t1[:], op=ALU.add), r=["tmp0", "tmp1"], w=[dkey])
            if getattr(self, "att_lvl", 9) < 1:
                continue
            for half in range(2):
                pi = self.nxt("pf01", 2)
                p = self.pf[pi]
                for t4 in range(4):
                    tt = half * 4 + t4
                    for kt in range(8):
                        A("pe", lambda e, p=p, t4=t4, tt=tt, kt=kt, wv_=wv_: e.matmul(p[:, t4 * 128:(t4 + 1) * 128], lhsT=self.hT[:, kt, tt * 128:(tt + 1) * 128],
                                                                                     rhs=wv_[:, kt, :], start=(kt == 0), stop=(kt == 7)),
                          r=[wvk, "hT"], w=[f"pf{pi}"])
                vo = (2 if isS else 0) + half * 4
                A("act", lambda e, p=p, vo=vo: e.activation(out=vh[:, vo:vo + 4, :], in_=p[:].rearrange("q (t e) -> q t e", e=128), func=AF.Copy),
                  r=[f"pf{pi}", "attA"], w=["vh"])
                if not isS:
                    oi = self.nxt("ost", 2)
                    ost = self.ost[oi]
                    A("dve", lambda e, p=p, ost=ost: e.tensor_copy(out=ost[:, 0:512], in_=p[:]), r=[f"pf{pi}", "vh"], w=[f"ost{oi}"])
                    tk = A("sp", lambda e, ost=ost, half=half, h=h: e.dma_start(
                        out=dr["nv"][half * 512:(half + 1) * 512, h * 128:(h + 1) * 128].rearrange("(t p) e -> p t e", p=128),
                        in_=ost[:, 0:512].rearrange("q (t e) -> q t e", e=128)), r=[f"ost{oi}"], w=["nv"], dk=f"ostd{oi}")
                    self.out_tokens.append(tk)
                    pj = 2 + self.nxt("pf23", 2)
                    p2 = self.pf[pj]
                    for t4 in range(4):
                        tt = half * 4 + t4
                        for kt in range(8):
                            A("pe", lambda e, p2=p2, t4=t4, tt=tt, kt=kt, wk_=wk_: e.matmul(p2[:, t4 * 128:(t4 + 1) * 128], lhsT=self.hT[:, kt, tt * 128:(tt + 1) * 128],
                                                                                           rhs=wk_[:, kt, :], start=(kt == 0), stop=(kt == 7)),
                              r=[wkk, "hT"], w=[f"pf{pj}"])
                    A("dve", lambda e, p2=p2, ost=ost: e.tensor_copy(out=ost[:, 512:1024], in_=p2[:]), r=[f"pf{pj}"], w=[f"ost{oi}"])
                    tk = A("sp", lambda e, ost=ost, half=half, h=h: e.dma_start(
                        out=dr["nk"][half * 512:(half + 1) * 512, h * 128:(h + 1) * 128].rearrange("(t p) e -> p t e", p=128),
                        in_=ost[:, 512:1024].rearrange("q (t e) -> q t e", e=128)), r=[f"ost{oi}"], w=["nk"], dk=f"ostk{oi}")
                    self.out_tokens.append(tk)
            if isS:
                A("sp", lambda e, h=h: e.dma_start(out=cst[:, 0:256].rearrange("q (t e) -> q t e", e=128),
                                                   in_=dr["cache_k"][:, h].rearrange("(t p) m d -> p t (m d)", p=128)), w=["xt0"], dk="xt0")
                A("sp", lambda e, h=h: e.dma_start(out=cst[:, 256:512].rearrange("q (t e) -> q t e", e=128),
                                                   in_=dr["cache_v"][:, h].rearrange("(t p) e -> p t e", p=128)), w=["xt0"], dk="xt0b")
                A("dve", lambda e: e.tensor_copy(out=self.hb[0][:, 0:256], in_=cst[:, 0:256]), r=["xt0"], w=["hb0"])
                A("dve", lambda e: e.tensor_copy(out=vh[:, 0:2, :], in_=cst[:, 256:512].rearrange("q (t e) -> q t e", e=128)), r=["xt0", "attA"], w=["vh"])
                pti = self.nxt("pt", 2)
                pt = self.pt[pti]
                for t in range(2):
                    A("pe", lambda e, pt=pt, t=t: e.transpose(pt[:, t, :], self.hb[0][:, t * 128:(t + 1) * 128], self.identb[:]),
                      r=["hb0", "identb"], w=[f"pt{pti}"])
                A("act", lambda e, pt=pt: e.activation(out=kT[:, 0:256], in_=pt[:, 0:2, :].rearrange("q t c -> q (t c)"), func=AF.Copy),
                  r=[f"pt{pti}"], w=["kT"])
            if getattr(self, "att_lvl", 9) < 2:
                continue
            blocks = []
            if isS:
                for qb in range(2):
                    blocks.append((qb * 512, [(0, 512, [(kb * 128, kb) for kb in range(10)])]))
            else:
                for bb in range(2):
                    subs = []
                    for sl in range(2):
                        s0 = (2 * bb + sl) * 256
                        subs.append((sl * 256, 256, [(s0 + kb * 128, s0 // 128 + kb) for kb in range(2)]))
                    blocks.append((bb * 512, subs))
            accs = [(self.pf[4], "pf4", self.pf[5], "pf5"),
                    (self.pt[0][:].bitcast(F32).rearrange("q a b -> q (a b)"), "pt0", self.pt[1][:].bitcast(F32).rearrange("q a b -> q (a b)"), "pt1")]
            osets = [((self.tmp[2], "tmp2"), (self.tmp[3], "tmp3")),
                     ((self.xt[1][:, 0:512], "xt1"), (self.xt[1][:, 512:1024], "xt1"))]
            nq = 512

            def stageA(bi, q0, subs):
                oset = osets[bi % 2]
                work = []
                for m in range(2):
                    for (qoff, nqs, keys) in subs:
                        for ki, (kbase, vt) in enumerate(keys):
                            work.append((m, qoff, nqs, kbase, vt, ki == 0, ki == len(keys) - 1))
                last_of_map = {m: max(i for i, w in enumerate(work) if w[0] == m) for m in range(2)}
                sc_bank = {}

                def issue_scores(wi):
                    m, qoff, nqs, kbase, vt, first, last = work[wi]
                    pj = self.nxt("pf03", 4)
                    p2 = self.pf[pj]
                    sc_bank[wi] = (p2, pj)
                    A("pe", lambda e, p2=p2, kbase=kbase, m=m, qoff=qoff, nqs=nqs: e.matmul(
                        p2[:, 0:nqs], lhsT=kT[:, kbase:kbase + 128], rhs=qTm[m][:, q0 + qoff:q0 + qoff + nqs], start=True, stop=True),
                        r=["kT", "qT"], w=[f"pf{pj}"])
                for wi in range(min(3, len(work))):
                    issue_scores(wi)
                for wi, (m, qoff, nqs, kbase, vt, first, last) in enumerate(work):
                    if wi + 3 < len(work):
                        issue_scores(wi + 3)
                    po, pok, pz, pzk = accs[m]
                    p2, pj = sc_bank[wi]
                    pti = self.nxt("PT", 4)
                    A("act", lambda e, p2=p2, pti=pti, nqs=nqs: e.activation(out=PT[pti][:, 0:nqs], in_=p2[:, 0:nqs], func=AF.Exp, scale=0.125),
                      r=[f"pf{pj}"], w=[f"PT{pti}"])
                    A("pe", lambda e, po=po, vt=vt, pti=pti, qoff=qoff, nqs=nqs, first=first, last=last: e.matmul(
                        po[:, qoff:qoff + nqs], lhsT=vh[:, vt, :], rhs=PT[pti][:, 0:nqs], start=first, stop=last), r=["vh", f"PT{pti}"], w=[pok])
                    A("pe", lambda e, pz=pz, pti=pti, qoff=qoff, nqs=nqs, first=first, last=last: e.matmul(
                        pz[:, qoff:qoff + nqs], lhsT=self.onesb[:], rhs=PT[pti][:, 0:nqs], start=first, stop=last), r=["onesb", f"PT{pti}"], w=[pzk])
                    if wi == last_of_map[m]:
                        te = self.xn[m][:, 0:512]
                        tek = f"xn{m}"
                        oa, oak = oset[m]
                        A("act", lambda e, pz=pz, te=te: e.activation(out=te[:, 0:nq], in_=pz[:, 0:nq], func=AF.Ln), r=[pzk], w=[tek])
                        A("act", lambda e, te=te: e.activation(out=te[:, 0:nq], in_=te[:, 0:nq], func=AF.Exp, scale=-1.0), r=[tek], w=[tek])
                        A("dve", lambda e, po=po, te=te, oa=oa: e.tensor_tensor(out=oa[:, 0:nq], in0=po[:, 0:nq], in1=te[:, 0:nq], op=ALU.mult),
                          r=[pok, tek], w=[oak])

            def stageB(bi, q0, h=h, gz=gz, gzk=gzk):
                qs = slice(q0, q0 + 512)
                (o0, k0), (o1, k1) = osets[bi % 2]
                t0, t1 = self.tmp[0], self.tmp[1]
                A("dve", lambda e: e.scalar_tensor_tensor(out=o0[:, 0:nq], in0=o1[:, 0:nq], scalar=self.nlam[:, 0:1], in1=o0[:, 0:nq],
                                                          op0=ALU.mult, op1=ALU.add), r=[k0, k1, "nlam"], w=[k0])
                A("dve", lambda e: e.tensor_tensor(out=t0[:, 0:nq], in0=o0[:, 0:nq], in1=o0[:, 0:nq], op=ALU.mult), r=[k0], w=["tmp0"])
                pj = self.nxt("pf03", 4)
                p2 = self.pf[pj]
                A("pe", lambda e, p2=p2: e.matmul(p2[:, 0:nq], lhsT=self.onesf[:], rhs=t0[:, 0:nq], start=True, stop=True),
                  r=["onesf", "tmp0"], w=[f"pf{pj}"])
                A("act", lambda e, p2=p2: e.activation(out=t1[:, 0:nq], in_=p2[:, 0:nq], func=AF.Ln, bias=self.epsc[:, 0:1], scale=1.0), r=[f"pf{pj}", "epsc"], w=["tmp1"])
                A("act", lambda e: e.activation(out=t1[:, 0:nq], in_=t1[:, 0:nq], func=AF.Exp, scale=-0.5), r=["tmp1"], w=["tmp1"])
                A("dve", lambda e: e.scalar_tensor_tensor(out=t1[:, 0:nq], in0=t1[:, 0:nq], scalar=self.subg[:, 0:1], in1=o0[:, 0:nq],
                                                          op0=ALU.mult, op1=ALU.mult), r=["tmp1", "subg", k0], w=["tmp1"])
                A("dve", lambda e: e.tensor_tensor(out=yT[:, 8 + h, qs], in0=t1[:, 0:nq], in1=gz[:, qs], op=ALU.mult),
                  r=["tmp1", gzk], w=[f"C{8 + h}"])
            for (q0, subs) in blocks:
                cnt = self._blk
                self._blk += 1
                stageA(cnt, q0, subs)
                if self._pend is not None:
                    self._pend()
                self._pend = (lambda cnt=cnt, q0=q0, sb_=stageB: sb_(cnt, q0))
        if self._pend is not None:
            self._pend()
            self._pend = None
        self.rfence("AB")

    def out_proj_postnorm(self, wkey, x_src, mid, sink, xkey=None):
        A, dr = self.A, self.dr
        yT = self.C[:].rearrange("p (k t) -> p k t", t=1024)
        zall = self.AB[:].bitcast(F32).rearrange("p (t n) -> p t n", n=1024)
        ALLC = [f"C{k}" for k in range(16)]
        pre = [self.wload(dr[wkey][:, cb * 128:(cb + 1) * 128], 16, 128) for cb in range(4)]
        self.rfence("AB")
        for cb in range(8):
            wv, wk = pre[cb] if cb < 4 else self.wload(dr[wkey][:, cb * 128:(cb + 1) * 128], 16, 128)
            for half in range(2):
                pi = self.nxt("pf01", 2)
                p = self.pf[pi]
                for t4 in range(4):
                    tt = half * 4 + t4
                    for kt in range(16):
                        A("pe", lambda e, p=p, t4=t4, tt=tt, kt=kt, wv=wv: e.matmul(p[:, t4 * 128:(t4 + 1) * 128], lhsT=yT[:, kt, tt * 128:(tt + 1) * 128],
                                                                                   rhs=wv[:, kt, :], start=(kt == 0), stop=(kt == 15)),
                          r=[wk] + ALLC, w=[f"pf{pi}"])
                gate = self.mrep[:, 2048 + cb * 128:2048 + (cb + 1) * 128].unsqueeze(1).to_broadcast([128, 4, 128])
                A("dve", lambda e, p=p, half=half, cb=cb, gate=gate: e.tensor_tensor(
                    out=zall[:, half * 4:half * 4 + 4, cb * 128:(cb + 1) * 128], in0=p[:].rearrange("q (t n) -> q t n", n=128), in1=gate, op=ALU.mult),
                  r=[f"pf{pi}", "mrep"], w=[f"Z{half * 4 + t}" for t in range(4)])
        if mid is not None:
            mid()
        sts = []
        for tt in range(8):
            b = self.nxt("xt", 2)
            xt = self.xt[b]
            A("sp", lambda e, xt=xt, tt=tt: e.dma_start(out=xt[:], in_=x_src[tt * 128:(tt + 1) * 128, :]), r=([xkey] if xkey else []), w=[f"xt{b}"], dk=f"xt{b}")
            A("dve", lambda e, xt=xt, tt=tt: e.scalar_tensor_tensor(out=zall[:, tt, :], in0=xt[:], scalar=ALPHA, in1=zall[:, tt, :], op0=ALU.mult, op1=ALU.add),
              r=[f"xt{b}", f"Z{tt}"], w=[f"Z{tt}"])
            sts.append(self.ln_stats_t(zall[:, tt, :], f"Z{tt}", tt, 0))
        for tt in range(8):
            rstd, nmr, lk = sts[tt]
            zt = zall[:, tt, :]
            A("act", lambda e, zt=zt, nmr=nmr, rstd=rstd: e.activation(out=zt, in_=zt, func=AF.Identity, bias=nmr, scale=rstd), r=[f"Z{tt}"] + lk, w=[f"Z{tt}"])
            A("dve", lambda e, zt=zt: e.tensor_tensor(out=zt, in0=zt, in1=self.lngr[:], op=ALU.mult), r=[f"Z{tt}", "lngr"], w=[f"Z{tt}"])
            A("dve", lambda e, zt=zt: e.tensor_tensor(out=zt, in0=zt, in1=self.lnbr[:], op=ALU.add), r=[f"Z{tt}", "lnbr"], w=[f"Z{tt}"])
        sink([(zall[:, tt, :], f"Z{tt}", tt) for tt in range(8)])

    def fourier_path(self, grp):
        A, dr = self.A, self.dr
        isS = (grp == "S")
        L = 1024 if isS else 256
        ntl = L // 128
        mixedT = self.AB[:].rearrange("p (k t) -> p k t", t=1024)
        uT = [self.T[:, i * 2048:(i + 1) * 2048].rearrange("p (k t) -> p k t", t=1024) for i in range(2)]
        ucs = self.T[:, 4096:8192].rearrange("p (t n) -> p t n", n=512)
        cs256 = self.T[:, 8192:9216].rearrange("p (k n) -> p k n", n=512)
        CL = self.C[:, 0:ntl * L].rearrange("p (t n) -> p t n", n=L)
        SL = self.C[:, 8192:8192 + ntl * L].rearrange("p (t n) -> p t n", n=L)
        pre = [self.wload(dr["win_o"][:, fg * 256:(fg + 1) * 256], 8, 256) for fg in range(4)]
        self.rfence("AB"); self.rfence("C"); self.rfence("T")
        A("pool", lambda e: e.dma_start(out=cs256, in_=dr["cs256"].rearrange("(k p) n -> p k n", p=128)), w=["cs256"], dk="cs256")
        cn, sn = ("cl1024", "sl1024") if isS else ("cl256", "sl256")
        for tl in range(ntl):
            A("pool", lambda e, tl=tl: e.dma_start(out=CL[:, tl, :], in_=dr[cn][tl * 128:(tl + 1) * 128, :]), w=["Ctab"], dk="ctabC")
            A("pool", lambda e, tl=tl: e.dma_start(out=SL[:, tl, :], in_=dr[sn][tl * 128:(tl + 1) * 128, :]), w=["Ctab"], dk="ctabS")
        seqs = [(0, 1024)] if isS else [(i * 256, 256) for i in range(4)]
        for fg in range(8):
            wv, wk = pre[fg] if fg < 4 else self.wload(dr["win_o"][:, fg * 256:(fg + 1) * 256], 8, 256)
            ub = self.nxt("uT", 2)
            u = uT[ub]
            for kt2 in range(2):
                for tb in range(2):
                    ts = slice(tb * 512, (tb + 1) * 512)
                    pi = self.nxt("pf01", 2)
                    p = self.pf[pi]
                    for kt in range(8):
                        A("pe", lambda e, p=p, kt=kt, kt2=kt2, ts=ts, wv=wv: e.matmul(p[:], lhsT=wv[:, kt, kt2 * 128:(kt2 + 1) * 128], rhs=self.hT[:, kt, ts],
                                                                                     start=(kt == 0), stop=(kt == 7)), r=[wk, "hT"], w=[f"pf{pi}"])
                    A("act", lambda e, p=p, u=u, kt2=kt2, ts=ts: e.activation(out=u[:, kt2, ts], in_=p[:], func=AF.Copy), r=[f"pf{pi}"], w=[f"uT{ub}"])
            for tt in range(8):
                pj = 2 + self.nxt("pf23", 2)
                p2 = self.pf[pj]
                for kt2 in range(2):
                    A("pe", lambda e, p2=p2, kt2=kt2, tt=tt, u=u: e.matmul(p2[:], lhsT=u[:, kt2, tt * 128:(tt + 1) * 128], rhs=cs256[:, kt2, :],
                                                                          start=(kt2 == 0), stop=(kt2 == 1)), r=[f"uT{ub}", "cs256"], w=[f"pf{pj}"])
                if tt % 2 == 0:
                    A("dve", lambda e, p2=p2, tt=tt: e.tensor_copy(out=ucs[:, tt, :], in_=p2[:]), r=[f"pf{pj}"], w=["ucs"])
                else:
                    A("act", lambda e, p2=p2, tt=tt: e.activation(out=ucs[:, tt, :], in_=p2[:], func=AF.Copy), r=[f"pf{pj}"], w=["ucs"])
            for (s0, Ls) in seqs:
                nk1 = min(512, Ls)
                for f2 in range(2):
                    for kb in range(Ls // nk1):
                        ks = slice(kb * nk1, (kb + 1) * nk1)
                        pi = self.nxt("pf01", 2)
                        p = self.pf[pi]
                        n = 0
                        for tl in range(ntl):
                            tt = s0 // 128 + tl
                            for (co, tab) in [(0, CL), (256, SL)]:
                                A("pe", lambda e, p=p, tt=tt, tl=tl, f2=f2, co=co, tab=tab, ks=ks, nk1=nk1, n=n: e.matmul(
                                    p[:, 0:nk1], lhsT=ucs[:, tt, co + f2 * 128:co + (f2 + 1) * 128], rhs=tab[:, tl, ks],
                                    start=(n == 0), stop=(n == 2 * ntl - 1)), r=["ucs", "Ctab"], w=[f"pf{pi}"])
                                n += 1
                        A("act", lambda e, p=p, fg=fg, f2=f2, s0=s0, ks=ks, nk1=nk1: e.activation(
                            out=mixedT[:, fg * 2 + f2, s0 + ks.start:s0 + ks.stop], in_=p[:, 0:nk1], func=AF.Copy), r=[f"pf{pi}"], w=["mixA"])

    def fno_path(self):
        A, dr = self.A, self.dr
        mixedT = self.AB[:].rearrange("p (k t) -> p k t", t=1024)
        yT = self.C[:].rearrange("p (k t) -> p k t", t=1024)
        pre = []
        for ot in range(2):
            pre.append((self.wload(dr["wfno"][:, ot * 128:(ot + 1) * 128], 16, 128), self.wload(dr["win_o"][:, 2048 + ot * 128:2048 + (ot + 1) * 128], 8, 128)))
        self.rfence("C")
        for ot in range(16):
            if ot < 2:
                (wf, wfk), (wz, wzk) = pre[ot]
            else:
                wf, wfk = self.wload(dr["wfno"][:, ot * 128:(ot + 1) * 128], 16, 128)
                wz, wzk = self.wload(dr["win_o"][:, 2048 + ot * 128:2048 + (ot + 1) * 128], 8, 128)
            for tb in range(2):
                ts = slice(tb * 512, (tb + 1) * 512)
                pi = self.nxt("pf01", 2)
                p = self.pf[pi]
                for kt in range(16):
                    A("pe", lambda e, p=p, kt=kt, wf=wf, ts=ts: e.matmul(p[:], lhsT=wf[:, kt, :], rhs=mixedT[:, kt, ts], start=(kt == 0), stop=(kt == 15)),
                      r=[wfk, "mixA"], w=[f"pf{pi}"])
                pj = 2 + self.nxt("pf23", 2)
                p2 = self.pf[pj]
                for kt in range(8):
                    A("pe", lambda e, p2=p2, kt=kt, wz=wz, ts=ts: e.matmul(p2[:], lhsT=wz[:, kt, :], rhs=self.hT[:, kt, ts], start=(kt == 0), stop=(kt == 7)),
                      r=[wzk, "hT"], w=[f"pf{pj}"])
                t1 = self.tmp[1]
                A("act", lambda e, p2=p2: e.activation(out=t1[:], in_=p2[:], func=AF.Silu), r=[f"pf{pj}"], w=["tmp1"])
                A("dve", lambda e, p=p, ot=ot, ts=ts: e.scalar_tensor_tensor(out=yT[:, ot, ts], in0=p[:], scalar=self.bfno[:, ot:ot + 1], in1=t1[:],
                                                                            op0=ALU.add, op1=ALU.mult), r=[f"pf{pi}", "bfno", "tmp1"], w=[f"C{ot}"])

    def load_ln(self, layer):
        A, dr = self.A, self.dr
        A("sp", lambda e: e.dma_start(out=self.lngr[:], in_=dr["lng"][layer].partition_broadcast(128)), w=["lngr"], dk="lngr")
        A("sp", lambda e: e.dma_start(out=self.lnbr[:], in_=dr["lnb"][layer].partition_broadcast(128)), w=["lnbr"], dk="lnbr")

    def run_group(self, grp):
        A, dr = self.A, self.dr
        isS = (grp == "S")
        xin = dr["xs"] if isS else dr["xp"]
        cond = dr["csmp"] if isS else dr["cctx"]
        x1s = dr["x1s"][1 if isS else 0]
        yout = dr["ys"] if isS else dr["yp"]
        self.rfence("AB")
        xall = self.AB[:].bitcast(F32).rearrange("p (t n) -> p t n", n=1024)
        tiles = []
        for tt in range(8):
            A("sp", lambda e, tt=tt: e.dma_start(out=xall[:, tt, :], in_=xin[tt * 128:(tt + 1) * 128, :]), w=[f"Z{tt}"], dk=f"xin{tt % 4}")
            tiles.append((xall[:, tt, :], f"Z{tt}", tt))
        st = [self.ln_stats_t(src, key, tt, 1) for (src, key, tt) in tiles]
        self.load_mod(0, 1 if isS else 0)
        self.load_ln(0)
        self.ln_mod_tiles(tiles, st)
        upto = getattr(self, "upto", 99)
        if upto < 1:
            return
        self.s5_path(grp)
        if upto < 2:
            return
        self.glu_path()
        if upto < 3:
            return
        self.attn_path(grp)
        if upto < 4:
            return

        def sink0(tiles):
            for (ap, key, tt) in tiles:
                A("sp", lambda e, ap=ap, tt=tt: e.dma_start(out=x1s[tt * 128:(tt + 1) * 128, :], in_=ap), r=[key], w=["x1s"], dk=f"x1w{tt % 2}")
            self.ln_mod_tiles(tiles)
        self.out_proj_postnorm("wout_e", xin, lambda: self.load_mod(1, 1 if isS else 0), sink0)
        if upto < 5:
            return
        self.load_ln(1)
        self.fourier_path(grp)
        if upto < 6:
            return
        self.fno_path()
        if upto < 7:
            return

        def sink1(tiles):
            for (ap, key, tt) in tiles:
                tk = A("sp", lambda e, ap=ap, tt=tt: e.dma_start(out=yout[tt * 128:(tt + 1) * 128, :], in_=ap), r=[key], w=["yout"], dk=f"yw{tt % 2}")
                self.out_tokens.append(tk)
        self.out_proj_postnorm("wout_o", x1s, None, sink1, xkey="x1s")

    def emit(self):
        nc, tr = self.nc, self.tr
        pre = self.pre
        outs = self.out_tokens
        with nc.Block() as block:
            @block.sync
            def _(e):
                tr.emit_engine("sp", e)
                tr.final_waits(e, tr.all_tokens())

            @block.scalar
            def _(e):
                tr.emit_engine("act", e)

            @block.vector
            def _(e):
                tr.emit_engine("dve", e)

            @block.gpsimd
            def _(e):
                tr.emit_engine("pool", e)

            @block.tensor
            def _(e):
                tr.emit_engine("pe", e)


IN_SPECS = [
    ("xp", [1024, 1024]), ("xs", [1024, 1024]), ("cctx", [1024]), ("csmp", [1024]),
    ("wmod", [2, 1024, 3072]), ("bmod", [2, 3072]), ("lng", [2, 1024]), ("lnb", [2, 1024]),
    ("win_e", [1024, 6144]), ("wglu", [1024, 1024]), ("bglu", [1024]), ("wout_e", [2048, 1024]), ("subg", [128]),
    ("win_o", [1024, 4096]), ("wfno", [2048, 2048]), ("bfno", [2048]), ("wout_o", [2048, 1024]),
    ("cache_k", [256, 8, 2, 64]), ("cache_v", [256, 8, 128]), ("st_re", [2, 64, 64]), ("st_im", [2, 64, 64]),
    ("lam_re", [2, 64, 64]), ("lam_im", [2, 64, 64]), ("log_dt", [2, 64]),
    ("b_re", [2, 64, 64, 16]), ("b_im", [2, 64, 64, 16]), ("c_re", [2, 64, 16, 64]), ("c_im", [2, 64, 16, 64]), ("ssm_d", [1024]),
    ("lq1", [64]), ("lk1", [64]), ("lq2", [64]), ("lk2", [64]),
    ("ident", [128, 128]), ("maskF", [128, 128]), ("maskB", [128, 128]), ("pswap", [128, 128]),
    ("ropeC", [128, 1024]), ("ropeS", [128, 1024]), ("cs256", [256, 512]),
    ("cl256", [256, 256]), ("sl256", [256, 256]), ("cl1024", [1024, 1024]), ("sl1024", [1024, 1024]),
]
OUT_SPECS = [("yp", [1024, 1024]), ("ys", [1024, 1024]), ("nk", [1024, 1024]), ("nv", [1024, 1024]),
             ("fre", [4, 2, 64, 64]), ("fim", [4, 2, 64, 64])]


def build_nc(groups=("P", "S"), upto=99, att_lvl=9):
    nc = bass.Bass("TRN2", target_bir_lowering=False)
    dr = {}
    for n, shp in IN_SPECS:
        dr[n] = nc.dram_tensor(n, shp, F32, kind="ExternalInput").ap()
    for n, shp in OUT_SPECS:
        dr[n] = nc.dram_tensor(n, shp, F32, kind="ExternalOutput").ap()
    dr["s_opW"] = nc.dram_tensor("s_opW", [128, 64, 2, 128], BF16).ap()
    dr["s_opQ"] = nc.dram_tensor("s_opQ", [128, 64, 2, 128], BF16).ap()
    dr["s_opM"] = nc.dram_tensor("s_opM", [128, 64, 128], BF16).ap()
    dr["s_A8"] = nc.dram_tensor("s_A8", [128, 2, 64], F32).ap()
    dr["s_nlam"] = nc.dram_tensor("s_nlam", [128, 1], F32).ap()
    dr["s_mod"] = nc.dram_tensor("s_mod", [2, 2, 3072], F32).ap()
    dr["x1s"] = nc.dram_tensor("x1s", [2, 1024, 1024], F32).ap()
    with contextlib.ExitStack() as es:
        sems = [es.enter_context(nc.semaphore(f"s{i}")) for i in range(100)]
        tr0 = phase0(nc, dr, sems[:38])
        with contextlib.ExitStack() as es2:
            m = Main(nc, dr, sems[38:], tr0.all_tokens())
            m.upto = upto
            m.att_lvl = att_lvl
            m.alloc(es2)
            m.alloc2(es2)
            m.load_consts()
            for g in groups:
                m.run_group(g)
            m.emit()
    return nc


def host_consts():
    i8 = np.arange(128) // 16
    c = {"ident": np.eye(128, dtype=np.float32),
         "maskF": (i8[:, None] <= i8[None, :]).astype(np.float32),
         "maskB": (i8[:, None] >= i8[None, :]).astype(np.float32)}
    psw = np.zeros((128, 128), np.float32)
    for m in range(128):
        blk, j = divmod(m, 64)
        psw[blk * 64 + (j + 32) % 64, m] = 1.0
    c["pswap"] = psw
    L = 1024
    row = np.repeat(np.arange(L // 64), 64).astype(np.float64)
    col = np.tile(np.arange(64), L // 64).astype(np.float64)
    freqs = 10000.0 ** (-np.arange(16, dtype=np.float64) / 16)
    ang = np.concatenate([row[:, None] * freqs, col[:, None] * freqs], axis=-1)
    cosT = np.cos(ang).T
    sinT = np.sin(ang).T
    c["ropeC"] = np.concatenate([cosT, cosT, cosT, cosT], axis=0).astype(np.float32)
    c["ropeS"] = np.concatenate([-sinT, sinT, -sinT, sinT], axis=0).astype(np.float32)
    k = np.arange(256, dtype=np.float64)
    a = 2 * np.pi * np.outer(k, k) / 256
    c["cs256"] = (np.concatenate([np.cos(a), np.sin(a)], axis=1) / 16.0).astype(np.float32)
    for Ls, cn, sn in [(256, "cl256", "sl256"), (1024, "cl1024", "sl1024")]:
        t = np.arange(Ls, dtype=np.float64)
        a = 2 * np.pi * (np.outer(t, t) % Ls) / Ls
        c[cn] = (np.cos(a) / np.sqrt(Ls)).astype(np.float32)
        c[sn] = (-np.sin(a) / np.sqrt(Ls)).astype(np.float32)
    return c


def make_in_maps(x_prompt, x_sample, cache_k, cache_v, state_ssm_re, state_ssm_im, c, c_ctx,
                 w_mod, b_mod, ln_g, ln_b, w_in_e, ssm_lam_re, ssm_lam_im, ssm_log_dt,
                 ssm_b_re, ssm_b_im, ssm_c_re, ssm_c_im, ssm_d, w_glu, b_glu,
                 lam_q1, lam_k1, lam_q2, lam_k2, subln_g, w_out_e, w_in_o, w_fno, b_fno, w_out_o):
    f = lambda a: np.ascontiguousarray(np.asarray(a, dtype=np.float32))
    shared = {
        "cctx": f(c_ctx), "wmod": f(w_mod), "bmod": f(b_mod), "lng": f(ln_g), "lnb": f(ln_b),
        "win_e": f(w_in_e[0]), "wglu": f(w_glu[0]), "bglu": f(b_glu[0]), "wout_e": f(w_out_e[0]), "subg": f(subln_g[0]),
        "win_o": f(w_in_o[0]), "wfno": f(w_fno[0]), "bfno": f(b_fno[0]), "wout_o": f(w_out_o[0]),
        "lam_re": f(ssm_lam_re[0]), "lam_im": f(ssm_lam_im[0]), "log_dt": f(ssm_log_dt[0]),
        "b_re": f(ssm_b_re[0]), "b_im": f(ssm_b_im[0]), "c_re": f(ssm_c_re[0]), "c_im": f(ssm_c_im[0]), "ssm_d": f(ssm_d[0]),
        "lq1": f(lam_q1[0]), "lk1": f(lam_k1[0]), "lq2": f(lam_q2[0]), "lk2": f(lam_k2[0]),
    }
    shared.update(host_consts())
    xp = np.asarray(x_prompt, np.float32)
    xs = np.asarray(x_sample, np.float32)
    maps = []
    for i in range(NCORES):
        b = i % 4
        m = dict(shared)
        m["xp"] = f(xp[4 * i:4 * i + 4].reshape(1024, 1024))
        m["xs"] = f(xs[b])
        m["csmp"] = f(np.asarray(c)[b])
        m["cache_k"] = f(np.asarray(cache_k)[b, 0])
        m["cache_v"] = f(np.asarray(cache_v)[b, 0])
        m["st_re"] = f(np.asarray(state_ssm_re)[b, 0])
        m["st_im"] = f(np.asarray(state_ssm_im)[b, 0])
        maps.append(m)
    return maps


def gather(results):
    g = lambda n, shp: np.concatenate([np.asarray(r[n], np.float32).reshape(shp) for r in results], axis=0)
    y_prompt = g("yp", (4, 256, 1024))
    new_k = g("nk", (4, 1, 256, 8, 2, 64))
    new_v = g("nv", (4, 1, 256, 8, 128))
    s_re = g("fre", (4, 1, 2, 64, 64))
    s_im = g("fim", (4, 1, 2, 64, 64))
    y_sample = np.stack([np.asarray(results[b]["ys"], np.float32) for b in range(4)], axis=0)
    return (y_prompt, y_sample, new_k, new_v, s_re, s_im)


def kernel(**inputs):
    nc = build_nc()
    maps = make_in_maps(**inputs)
    res = run_bass_kernel_spmd(nc, maps, core_ids=list(range(NCORES)))
    return gather(res.results)
```
